# Optimizing a Trainium2 kernel written in Bass

```python
import jax
import jax.numpy as jnp
from jax import lax
import numpy as np

D_MODEL = 4096
BATCH = 2
SEQ = 4096
DEPTH = 2

GRID_W = 64
CTX_LEN = 256
MIX_WIDTH = D_MODEL
POOL_WIDTH = MIX_WIDTH // 4
POOL_WINDOWS = (2, 4, 8, 16)
POOL_GROUPS = len(POOL_WINDOWS)
POOL_GROUP_DIM = POOL_WIDTH // POOL_GROUPS
CONV_WIDTH = MIX_WIDTH // 4
CONV_KSIZE = 31
ATTN_WIDTH = MIX_WIDTH - POOL_WIDTH - CONV_WIDTH
HEAD_DIM = 128
N_HEADS = ATTN_WIDTH // HEAD_DIM
WIN_H = 8
WIN_W = 16
D_FF = 4 * D_MODEL
N_MOD = 6
EPS = 1e-6

OFF_CONV = POOL_WIDTH
OFF_Q = OFF_CONV + 2 * CONV_WIDTH
OFF_K = OFF_Q + ATTN_WIDTH
OFF_V = OFF_K + ATTN_WIDTH
IN_WIDTH = OFF_V + ATTN_WIDTH

kernel_name = 'hybrid_dit_pool_conv_natten_block'


def rms_norm(x, g):
    xf = x.astype(jnp.float32)
    y = xf * lax.rsqrt(jnp.mean(xf * xf, axis=-1, keepdims=True) + EPS)
    return (y * g.astype(jnp.float32)).astype(x.dtype)


def modulate(x, g, shift, scale):
    return rms_norm(x, g) * (1 + scale) + shift


def adaln_params(cvec, w_ada, b_ada):
    m = jax.nn.silu(cvec) @ w_ada + b_ada
    return m.reshape(cvec.shape[0], N_MOD, 1, D_MODEL)


def split_heads(u):
    return u.reshape(*u.shape[:-1], N_HEADS, HEAD_DIM)


def pool_mixer(u, pool_w, pool_scale):
    b, l, _ = u.shape
    ug = u.reshape(b, l, POOL_GROUPS, POOL_GROUP_DIM)
    cs = jnp.cumsum(ug.astype(jnp.float32), axis=1)
    cs = jnp.pad(cs, ((0, 0), (1, 0), (0, 0), (0, 0)))
    t = jnp.arange(l)[:, None]
    win = jnp.array(POOL_WINDOWS, dtype=jnp.int32)[None, :]
    lo = jnp.clip(t - win // 2, 0, l - 1)
    hi = jnp.clip(t + (win - 1 - win // 2), 0, l - 1)
    g_idx = jnp.arange(POOL_GROUPS)[None, :]
    window_sum = cs[:, hi + 1, g_idx] - cs[:, lo, g_idx]
    count = (hi - lo + 1).astype(jnp.float32)[None, :, :, None]
    pooled = (window_sum / count - ug.astype(jnp.float32)).astype(u.dtype)
    mixed = jnp.einsum('blgc,gcd->blgd', pooled, pool_w)
    return mixed.reshape(b, l, POOL_WIDTH) * pool_scale


def conv_mixer(u, dw_w, dw_b, norm_g, pw_w):
    a, gate = jnp.split(u, 2, axis=-1)
    h = a * jax.nn.sigmoid(gate)
    h = lax.conv_general_dilated(
        h, dw_w[:, None, :], window_strides=(1,),
        padding=((CONV_KSIZE // 2, CONV_KSIZE // 2),),
        dimension_numbers=('NWC', 'WIO', 'NWC'),
        feature_group_count=CONV_WIDTH) + dw_b
    h = jax.nn.silu(rms_norm(h, norm_g))
    return h @ pw_w


def neighbourhood_attention(q, k, v, kc, vc, rpb):
    b, l, h, dh = q.shape
    rows = l // GRID_W
    kh = min(WIN_H, rows)
    scale = dh ** -0.5
    i = jnp.arange(rows)
    row_start = jnp.clip(i - kh // 2, 0, rows - kh)
    dr = row_start[:, None] + jnp.arange(kh)[None, :] - i[:, None] + (WIN_H - 1)
    j = jnp.arange(GRID_W)
    col_start = jnp.clip(j - WIN_W // 2, 0, GRID_W - WIN_W)
    col_valid = (j[None, :] >= col_start[:, None]) & (j[None, :] < col_start[:, None] + WIN_W)
    dc = jnp.clip(j[None, :] - j[:, None] + (WIN_W - 1), 0, 2 * WIN_W - 2)
    k_grid = k.reshape(b, rows, GRID_W, h, dh)
    v_grid = v.reshape(b, rows, GRID_W, h, dh)
    q_rows = jnp.moveaxis(q.reshape(b, rows, GRID_W, h, dh), 1, 0)
    kc32 = kc.astype(jnp.float32)

    def row_block(args):
        q_row, start, dr_row = args
        k_blk = lax.dynamic_slice_in_dim(k_grid, start, kh, axis=1)
        v_blk = lax.dynamic_slice_in_dim(v_grid, start, kh, axis=1)
        bias = rpb[:, dr_row[None, :, None], dc[:, None, :]].astype(jnp.float32)
        bias = jnp.where(col_valid[None, :, None, :], bias, -jnp.inf)
        q32 = q_row.astype(jnp.float32)
        s_nb = jnp.einsum('bjhd,brchd->bhjrc', q32, k_blk.astype(jnp.float32)) * scale + bias[None]
        s_ctx = jnp.einsum('bjhd,bnhd->bhjn', q32, kc32) * scale
        s = jnp.concatenate([s_nb.reshape(b, h, GRID_W, kh * GRID_W), s_ctx], axis=-1)
        p = jax.nn.softmax(s, axis=-1).astype(v.dtype)
        p_nb = p[..., :kh * GRID_W].reshape(b, h, GRID_W, kh, GRID_W)
        p_ctx = p[..., kh * GRID_W:]
        return (jnp.einsum('bhjrc,brchd->bjhd', p_nb, v_blk)
                + jnp.einsum('bhjn,bnhd->bjhd', p_ctx, vc))

    o = lax.map(row_block, (q_rows, row_start, dr))
    return jnp.moveaxis(o, 0, 1).reshape(b, l, h * dh)


def context_attention(qc, kc, vc):
    b, n, h, dh = qc.shape
    s = jnp.einsum('bqhd,bkhd->bhqk', qc.astype(jnp.float32), kc.astype(jnp.float32)) * dh ** -0.5
    p = jax.nn.softmax(s, axis=-1).astype(vc.dtype)
    return jnp.einsum('bhqk,bkhd->bqhd', p, vc).reshape(b, n, h * dh)


def sq_relu_mlp(h, w1, w2):
    return jnp.square(jax.nn.relu(h @ w1)) @ w2


def group_outputs(u, y_attn, pool_w, pool_scale, conv_dw_w, conv_dw_b, conv_norm_g, conv_pw_w):
    y_pool = pool_mixer(u[..., :OFF_CONV], pool_w, pool_scale)
    y_conv = conv_mixer(u[..., OFF_CONV:OFF_Q], conv_dw_w, conv_dw_b, conv_norm_g, conv_pw_w)
    return jnp.concatenate([y_pool, y_conv, y_attn], axis=-1)


def hybrid_layer(x, xc, mod, mod_c, norm1_g, norm2_g, w_in, pool_w, pool_scale,
                 conv_dw_w, conv_dw_b, conv_norm_g, conv_pw_w, q_norm_g, k_norm_g,
                 rpb, w_out, w_mlp1, w_mlp2, update_ctx):
    hc = modulate(xc, norm1_g, mod_c[:, 0], mod_c[:, 1])
    col0 = 0 if update_ctx else OFF_K
    uc = hc @ w_in[:, col0:]
    kc = rms_norm(split_heads(uc[..., OFF_K - col0:OFF_V - col0]), k_norm_g)
    vc = split_heads(uc[..., OFF_V - col0:])

    h = modulate(x, norm1_g, mod[:, 0], mod[:, 1])
    u = h @ w_in
    q = rms_norm(split_heads(u[..., OFF_Q:OFF_K]), q_norm_g)
    k = rms_norm(split_heads(u[..., OFF_K:OFF_V]), k_norm_g)
    v = split_heads(u[..., OFF_V:])
    y_attn = neighbourhood_attention(q, k, v, kc, vc, rpb)
    y = group_outputs(u, y_attn, pool_w, pool_scale, conv_dw_w, conv_dw_b, conv_norm_g, conv_pw_w)
    x = x + mod[:, 2] * (y @ w_out)
    x = x + mod[:, 5] * sq_relu_mlp(modulate(x, norm2_g, mod[:, 3], mod[:, 4]), w_mlp1, w_mlp2)

    if update_ctx:
        qc = rms_norm(split_heads(uc[..., OFF_Q:OFF_K]), q_norm_g)
        yc_attn = context_attention(qc, kc, vc)
        yc = group_outputs(uc, yc_attn, pool_w, pool_scale, conv_dw_w, conv_dw_b, conv_norm_g, conv_pw_w)
        xc = xc + mod_c[:, 2] * (yc @ w_out)
        xc = xc + mod_c[:, 5] * sq_relu_mlp(modulate(xc, norm2_g, mod_c[:, 3], mod_c[:, 4]), w_mlp1, w_mlp2)
    return x, xc


def setup_inputs(seed: int = 0) -> dict:
    key = jax.random.key(seed)
    ks = jax.random.split(key, 22)

    def nrm(k, shape, s):
        return jax.random.normal(k, shape, jnp.float32) * s

    return {
        'x': nrm(ks[0], (BATCH, SEQ, D_MODEL), 1.0),
        'c': nrm(ks[1], (BATCH, D_MODEL), 1.0),
        'ctx': nrm(ks[2], (BATCH, CTX_LEN, D_MODEL), 1.0),
        'c_ctx': nrm(ks[3], (D_MODEL,), 1.0),
        'w_ada': nrm(ks[4], (DEPTH, D_MODEL, N_MOD * D_MODEL), 0.5 * D_MODEL ** -0.5),
        'b_ada': nrm(ks[5], (DEPTH, N_MOD * D_MODEL), 0.02),
        'norm1_g': 1.0 + nrm(ks[6], (DEPTH, D_MODEL), 0.1),
        'norm2_g': 1.0 + nrm(ks[7], (DEPTH, D_MODEL), 0.1),
        'w_in': nrm(ks[8], (DEPTH, D_MODEL, IN_WIDTH), D_MODEL ** -0.5),
        'pool_w': nrm(ks[9], (DEPTH, POOL_GROUPS, POOL_GROUP_DIM, POOL_GROUP_DIM), POOL_GROUP_DIM ** -0.5),
        'pool_scale': 1.0 + nrm(ks[10], (DEPTH, POOL_WIDTH), 0.1),
        'conv_dw_w': nrm(ks[11], (DEPTH, CONV_KSIZE, CONV_WIDTH), CONV_KSIZE ** -0.5),
        'conv_dw_b': nrm(ks[12], (DEPTH, CONV_WIDTH), 0.02),
        'conv_norm_g': 1.0 + nrm(ks[13], (DEPTH, CONV_WIDTH), 0.1),
        'conv_pw_w': nrm(ks[14], (DEPTH, CONV_WIDTH, CONV_WIDTH), CONV_WIDTH ** -0.5),
        'q_norm_g': 1.0 + nrm(ks[15], (DEPTH, HEAD_DIM), 0.1),
        'k_norm_g': 1.0 + nrm(ks[16], (DEPTH, HEAD_DIM), 0.1),
        'rpb': nrm(ks[17], (DEPTH, N_HEADS, 2 * WIN_H - 1, 2 * WIN_W - 1), 0.5),
        'w_out': nrm(ks[18], (DEPTH, MIX_WIDTH, D_MODEL), MIX_WIDTH ** -0.5),
        'w_mlp1': nrm(ks[19], (DEPTH, D_MODEL, D_FF), D_MODEL ** -0.5),
        'w_mlp2': nrm(ks[20], (DEPTH, D_FF, D_MODEL), D_FF ** -0.5),
    }


def reference(x, c, ctx, c_ctx, w_ada, b_ada, norm1_g, norm2_g, w_in, pool_w, pool_scale,
              conv_dw_w, conv_dw_b, conv_norm_g, conv_pw_w, q_norm_g, k_norm_g, rpb,
              w_out, w_mlp1, w_mlp2):
    xc = ctx
    for layer in range(DEPTH):
        mod = adaln_params(c, w_ada[layer], b_ada[layer])
        mod_c = adaln_params(c_ctx[None, :], w_ada[layer], b_ada[layer])
        x, xc = hybrid_layer(
            x, xc, mod, mod_c, norm1_g[layer], norm2_g[layer], w_in[layer],
            pool_w[layer], pool_scale[layer], conv_dw_w[layer], conv_dw_b[layer],
            conv_norm_g[layer], conv_pw_w[layer], q_norm_g[layer], k_norm_g[layer],
            rpb[layer], w_out[layer], w_mlp1[layer], w_mlp2[layer],
            update_ctx=layer < DEPTH - 1)
    return x
```

```python
import numpy as np
from contextlib import ExitStack
import concourse.bass as bass
import concourse.mybir as mybir
from concourse.bass_utils import run_bass_kernel_spmd

F32 = mybir.dt.float32
BF16 = mybir.dt.bfloat16
AF = mybir.ActivationFunctionType
ALU = mybir.AluOpType

D = 4096
KC = 32
NE = 1536
NCX = 256
NA = NE + NCX
OWN0 = 256
NOWN = 1024
NE0 = 2048
NOWN0 = 1536
INW = 9216
DFF = 16384
NEG = -30000.0
EPS = 1e-6
PAD = 16
EO = PAD
CO = PAD + NE + 2 * PAD
NPADDED = CO + NCX + PAD
NVEC = 32 + 32 + 8 + 8 + 8 + 1 + 1 + 8 * 31
V_N1, V_N2, V_PS, V_CB, V_CG, V_QG, V_KG, V_CW = 0, 32, 64, 72, 80, 88, 89, 90
AGRP = [(0, 512), (512, 512), (1024, 512), (1536, 256)]


class _Rec:
    def __init__(self):
        self.call = None

    def __getattr__(self, name):
        def f(*a, **k):
            self.call = (name, a, k)
        return f


class Prog:
    def __init__(self, nc):
        self.nc = nc
        self.engs = ["pe", "act", "dve", "pool", "sp"]
        self.rec = {e: [] for e in self.engs}
        self.cnt = {e: 0 for e in self.engs}
        self.seen = {e: {} for e in self.engs}
        self.lastw = {}
        self.readers = {}
        self.dsems = {}
        self.semobj = {}
        for e in self.engs:
            self.semobj[f"S_{e}"] = nc.alloc_semaphore(name=f"S_{e}")

    def dma_sem(self, name):
        if name not in self.dsems:
            h = self.nc.alloc_semaphore(name=name)
            self.dsems[name] = [h, 0]
            self.semobj[name] = h
        return name

    def _need(self, eng, tok, waits):
        if tok is None:
            return
        sname, val = tok
        if self.seen[eng].get(sname, 0) >= val:
            return
        waits[sname] = max(waits.get(sname, 0), val)

    def op(self, eng, fn, reads=(), writes=(), dsem=None, track=True, dinc=16):
        waits = {}
        for k in reads:
            self._need(eng, self.lastw.get(k), waits)
        for k in writes:
            self._need(eng, self.lastw.get(k), waits)
            for t in self.readers.get(k, ()):
                self._need(eng, t, waits)
        if eng == "pe":
            waits.pop("S_pe", None)
        for s, v in waits.items():
            self.seen[eng][s] = max(self.seen[eng].get(s, 0), v)
        if dsem is not None:
            self.dma_sem(dsem)
            d = self.dsems[dsem]
            d[1] += dinc
            tok = (dsem, d[1])
            inc = (dsem, dinc)
        elif track:
            self.cnt[eng] += 1
            tok = (f"S_{eng}", self.cnt[eng])
            inc = (f"S_{eng}", 1)
        else:
            tok = None
            inc = None
        r = _Rec()
        fn(r)
        self.rec[eng].append((list(waits.items()), r.call, inc))
        if tok is not None:
            for k in writes:
                self.lastw[k] = tok
                self.readers[k] = []
            for k in reads:
                self.readers.setdefault(k, []).append(tok)
        return tok

    def barrier(self):
        latest = {}
        for e in self.engs:
            if self.cnt[e] > 0:
                latest[f"S_{e}"] = self.cnt[e]
        for name, (h, v) in self.dsems.items():
            if v > 0:
                latest[name] = v
        for e in self.engs:
            waits = {}
            for sname, v in latest.items():
                self._need(e, (sname, v), waits)
            for sname, v in waits.items():
                self.seen[e][sname] = v
            self.rec[e].append((list(waits.items()), None, None))

    def final_wait(self, eng, toks):
        waits = {}
        for t in toks:
            self._need(eng, t, waits)
        self.rec[eng].append((list(waits.items()), None, None))

    def emit(self, block):
        m = {"pe": block.tensor, "act": block.scalar, "dve": block.vector,
             "pool": block.gpsimd, "sp": block.sync}
        semobj = self.semobj
        for e in self.engs:
            rec = self.rec[e]

            def body(engine, rec=rec):
                for waits, fn, inc in rec:
                    for s, v in waits:
                        engine.wait_ge(semobj[s], v)
                    if fn is not None:
                        ins = getattr(engine, fn[0])(*fn[1], **fn[2])
                        if inc is not None:
                            ins.then_inc(semobj[inc[0]], inc[1])
            m[e](body)


def emit_layer(nc, P, ps, cmn, L, outs):
    tag = L["tag"]; with_ctx_out = L["ctx_out"]
    NE = L["NE"]; NOWN = L["NOWN"]; OWN0 = 256
    NA = NE + NCX
    NB = NOWN + (NCX if with_ctx_out else 0)
    NQ = NOWN // 64
    NTT = NA // 128
    E0 = OWN0 - 16; EN = NOWN + 32
    EO = PAD; CO = EO + EN + 2 * PAD; NPADDED = CO + NCX + PAD
    WB = L["WB"]; WC = WB // 128

    def groups(n):
        g = []
        o = 0
        while o < n:
            g.append((o, min(512, n - o)))
            o += 512
        return g
    AGRP = groups(NA)
    BGRP = groups(NB)
    NAG = len(AGRP)

    def din(name, shape, dt=F32):
        return nc.dram_tensor(name + tag, shape, dt, kind="ExternalInput").ap()

    def dscr(name, shape, dt):
        return nc.dram_tensor(name + tag, shape, dt).ap()

    cv = cmn["cv"]
    aux = din("aux", [5, NE]); auxc = cmn["auxc"]
    biasd = din("bias", [16, 128, NQ * 320])
    vecs_d = din("vecs", [128, NVEC]); bada = din("bada", [2, 6 * D])
    w_ada = din("w_ada", [D, 6 * D]); w_in = din("w_in", [D, INW]); w_out = din("w_out", [D, D])
    w1 = din("w1", [D, DFF]); w2 = din("w2", [DFF, D])
    pool_w = din("pool_w", [1024, 256]); pw_w = din("pw_w", [1024, 1024])
    Xw = L["Xw"]
    Upc = dscr("Upc", [24, 128, NA], F32)
    Qd = dscr("Qd", [16, 128, NA], BF16); Kd = dscr("Kd", [16, 128, NA], BF16)
    Vd = dscr("Vd", [NTT, 128, 2048], BF16)
    Yd = dscr("Yd", [KC, 128, NB], BF16)
    H2d = dscr("H2d", [KC, 128, NB], BF16)
    ident = cmn["ident"]; ones = cmn["ones"]; epsb = cmn["epsb"]
    dbg = False

    class Rot:
        def __init__(self, banks):
            self.b = list(banks); self.i = 0

        def next(self):
            v = self.b[self.i % len(self.b)]; self.i += 1
            return v

    with ExitStack() as top:
        sb = lambda name, shape, dt, es=top: es.enter_context(nc.sbuf_tensor("sb_" + tag + name, shape, dt))
        vecs = sb("vecs", [128, NVEC], F32)
        mod = sb("mod", [128, 192, 2], F32)
        A1 = sb("A1", [128, KC, 2], F32); A2 = sb("A2", [128, KC, 2], F32)
        P.op("sp", lambda e: e.dma_start(out=vecs[:], in_=vecs_d), writes=["vecs"], dsem="d_misc")

        rr = {"evac": 0}

        def evac_eng():
            rr["evac"] += 1
            return "dve" if rr["evac"] % 2 == 0 else "act"

        def copy_op(eng, out, in_, reads, writes):
            if eng == "act":
                return P.op("act", lambda e: e.activation(out=out, in_=in_, func=AF.Copy), reads=reads, writes=writes)
            return P.op(eng, lambda e: e.tensor_copy(out=out, in_=in_), reads=reads, writes=writes)

        with ExitStack() as es:
            NWA = 3
            wb = [sb(f"wa{i}", [128, KC, 512], BF16, es) for i in range(NWA)]
            cv32 = sb("cv32", [128, KC, 2], F32, es); sil = sb("sil", [128, KC, 2], BF16, es)
            mrow = [sb(f"mrow{i}", [2, 512], F32, es) for i in range(2)]
            brow = sb("brow", [2, 6 * D], F32, es)
            P.op("sp", lambda e: e.dma_start(out=cv32[:], in_=cv), writes=["cv32"], dsem="d_cv")
            P.op("sp", lambda e: e.dma_start(out=brow[:], in_=bada), writes=["brow"], dsem="d_brow")
            P.op("act", lambda e: e.activation(out=sil[:], in_=cv32[:], func=AF.Silu), reads=["cv32"], writes=["sil"])
            wsrc = w_ada.rearrange("(k p) n -> p k n", p=128)
            for b in range(48):
                s = b % NWA
                P.op("pool", lambda e, b=b, s=s: e.dma_start(out=wb[s][:], in_=wsrc[:, :, b * 512:(b + 1) * 512]),
                     writes=[("wa", s)], dsem=f"d_wa{s}")
                pi = b % 2
                for k in range(KC):
                    P.op("pe", lambda e, k=k, s=s, pi=pi: e.matmul(ps[pi][0:2, :], lhsT=sil[:, k, :], rhs=wb[s][:, k, :],
                                                                   start=(k == 0), stop=(k == KC - 1)),
                         reads=[("wa", s), "sil"], writes=[("ps", pi)], track=(k == KC - 1))
                P.op("dve", lambda e, b=b, pi=pi: e.tensor_tensor(out=mrow[pi][:], in0=ps[pi][0:2, :],
                                                               in1=brow[:, b * 512:(b + 1) * 512], op=ALU.add),
                     reads=[("ps", pi), "brow"], writes=[("mrow", pi)])
                tb = 2 + (b // 24)
                for jj in range(4):
                    j = b * 4 + jj
                    P.op("pe", lambda e, j=j, jj=jj, pi=pi, tb=tb: e.transpose(out=ps[tb][:, (j % 96) * 2:(j % 96) * 2 + 2],
                                                                         in_=mrow[pi][0:2, jj * 128:(jj + 1) * 128], identity=ident[0:2, 0:2]),
                         reads=[("mrow", pi), "ident"], writes=[("ps", tb)], track=(jj == 3))
            for hh in range(2):
                P.op("dve", lambda e, hh=hh: e.tensor_copy(out=mod[:, hh * 96:(hh + 1) * 96, :].rearrange("p a b -> p (a b)"),
                                                          in_=ps[2 + hh][:, 0:192]),
                     reads=[("ps", 2 + hh)], writes=["mod"])
            for which in range(2):
                P.op("dve", lambda e, which=which: e.scalar_tensor_tensor(out=A1[:, :, which], in0=mod[:, 32:64, which], scalar=1.0,
                                                                         in1=vecs[:, V_N1:V_N1 + 32], op0=ALU.add, op1=ALU.mult),
                     reads=["mod", "vecs"], writes=["A1"])
                P.op("dve", lambda e, which=which: e.scalar_tensor_tensor(out=A2[:, :, which], in0=mod[:, 128:160, which], scalar=1.0,
                                                                         in1=vecs[:, V_N2:V_N2 + 32], op0=ALU.add, op1=ALU.mult),
                     reads=["mod", "vecs"], writes=["A2"])

        P.barrier()
        def modv(m, fc, which):
            return mod[:, m * 32 + fc, which:which + 1]

        with ExitStack() as es:
            hT = sb("hT", [128, KC, NA], BF16, es)
            with ExitStack() as es2:
                xc = [sb(f"xc{i}", [128, NA], F32, es2) for i in range(2)]
                sq = [sb(f"sq{i}", [128, NA], BF16, es2) for i in range(2)]
                tmp = [sb(f"tmp{i}", [128, NA], F32, es2) for i in range(1)] * 2
                rstd = sb("rstd", [128, NA], F32, es2)
                xsrc = L["xsrc"]; xcsrc = L["xcsrc"]; sel = L.get("sel")
                if sel is not None:
                    alt = [sb(f"alt{i}", [128, 448], F32, es2) for i in range(2)]
                    msel = sb("msel", [128, 4], F32, es2)
                    P.op("sp", lambda e: e.dma_start(out=msel[:], in_=sel["msel"]), writes=["msel"], dsem="d_msel")

                def load_x(fc):
                    s = fc % 2
                    P.op("sp", lambda e: e.dma_start(out=xc[s][:, 0:NE], in_=xsrc[fc]), writes=[("xc", s, 0)], dsem=f"d_xa{s}")
                    P.op("sp", lambda e: e.dma_start(out=xc[s][:, NE:NA], in_=xcsrc[fc]), writes=[("xc", s, 1)], dsem=f"d_xb{s}")
                    if sel is not None:
                        P.op("sp", lambda e: e.dma_start(out=alt[s][:, 0:256], in_=xsrc[fc, :, 512:768]), writes=[("alt", s, 0)], dsem=f"d_alta{s}")
                        P.op("sp", lambda e: e.dma_start(out=alt[s][:, 256:448], in_=xsrc[fc, :, 768:960]), writes=[("alt", s, 1)], dsem=f"d_altb{s}")
                        for (x0, xn, a0, m0) in ((0, 256, 0, 0), (1280, 192, 256, 2)):
                            P.op("dve", lambda e, x0=x0, xn=xn, m0=m0: e.tensor_scalar(out=xc[s][:, x0:x0 + xn], in0=xc[s][:, x0:x0 + xn],
                                                                                 scalar1=msel[:, m0:m0 + 1], scalar2=None, op0=ALU.mult),
                                 reads=[("xc", s, 0), "msel"], writes=[("xc", s, 0)])
                            P.op("dve", lambda e, x0=x0, xn=xn, a0=a0, m0=m0: e.scalar_tensor_tensor(
                                out=xc[s][:, x0:x0 + xn], in0=alt[s][:, a0:a0 + xn], scalar=msel[:, m0 + 1:m0 + 2], in1=xc[s][:, x0:x0 + xn],
                                op0=ALU.mult, op1=ALU.add),
                                reads=[("xc", s, 0), ("alt", s, 0), ("alt", s, 1), "msel"], writes=[("xc", s, 0)])
                    return s
                for fc in range(KC):
                    s = load_x(fc)
                    P.op("act", lambda e, s=s: e.activation(out=sq[s][:], in_=xc[s][:], func=AF.Square),
                         reads=[("xc", s, 0), ("xc", s, 1)], writes=[("sq", s)])
                    for gi, (g0, gn) in enumerate(AGRP):
                        P.op("pe", lambda e, s=s, gi=gi, g0=g0, gn=gn, fc=fc: e.matmul(ps[gi][:, 0:gn], lhsT=ones[:], rhs=sq[s][:, g0:g0 + gn],
                                                                                  start=(fc == 0), stop=(fc == KC - 1)),
                             reads=[("sq", s), "ones"], writes=[("ps", gi)], track=(gi == NAG - 1))
                for gi, (g0, gn) in enumerate(AGRP):
                    P.op("act", lambda e, gi=gi, g0=g0, gn=gn: e.activation(out=rstd[:, g0:g0 + gn], in_=ps[gi][:, 0:gn], func=AF.Sqrt,
                                                                         bias=epsb[:], scale=1.0 / D),
                         reads=[("ps", gi), "epsb"], writes=[("rstd", gi)])
                    P.op("dve", lambda e, g0=g0, gn=gn: e.reciprocal(out=rstd[:, g0:g0 + gn], in_=rstd[:, g0:g0 + gn]),
                         reads=[("rstd", gi)], writes=[("rstd", gi)])
                for fc in range(KC):
                    s = load_x(fc)
                    P.op("dve", lambda e, s=s: e.tensor_tensor(out=tmp[s][:], in0=xc[s][:], in1=rstd[:], op=ALU.mult),
                         reads=[("xc", s, 0), ("xc", s, 1)] + [("rstd", g) for g in range(NAG)], writes=[("tmp", 0)])
                    P.op("act", lambda e, s=s, fc=fc: e.activation(out=hT[:, fc, 0:NE], in_=tmp[s][:, 0:NE], func=AF.Identity,
                                                                 bias=modv(0, fc, 0), scale=A1[:, fc, 0:1]),
                         reads=[("tmp", 0), "mod", "A1"], writes=[("hT", fc, 0)])
                    P.op("act", lambda e, s=s, fc=fc: e.activation(out=hT[:, fc, NE:NA], in_=tmp[s][:, NE:NA], func=AF.Identity,
                                                                 bias=modv(0, fc, 1), scale=A1[:, fc, 1:2]),
                         reads=[("tmp", 0), "mod", "A1"], writes=[("hT", fc, 1)])
            P.barrier()
            hT_keys = [("hT", fc, w) for fc in range(KC) for w in range(2)]
            with ExitStack() as es2:
                wb = [sb(f"wi{i}", [128, KC, WB], BF16, es2) for i in range(2)]
                stg = [sb(f"stg{i}", [128, 512], F32, es2) for i in range(3)]
                stb = [sb(f"stb{i}", [128, 512], BF16, es2) for i in range(3)]
                sqb = [sb(f"sqb{i}", [128, 512], BF16, es2) for i in range(2)]
                rs = [sb(f"rs{i}", [128, 512], F32, es2) for i in range(2)]
                vst = [sb(f"vst{i}", [128, WB], BF16, es2) for i in range(2)]
                wsrc = w_in.rearrange("(k p) n -> p k n", p=128)
                cnt = {"stg": 0, "stb": 0, "g": 0, "v": 0}
                rot = Rot(range(6)); rots = Rot([6, 7])
                for cb in range(INW // WB):
                    s = cb % 2
                    P.op("pool", lambda e, cb=cb, s=s: e.dma_start(out=wb[s][:], in_=wsrc[:, :, cb * WB:(cb + 1) * WB]),
                         writes=[("wi", s)], dsem=f"d_wi{s}")
                    if cb * WB < 7168:
                        for c4 in range(WC):
                            chunk = cb * WC + c4
                            for gi, (g0, gn) in enumerate(AGRP):
                                if chunk < 24:
                                    si = cnt["stg"] % 3; cnt["stg"] += 1
                                else:
                                    si = cnt["stb"] % 3; cnt["stb"] += 1
                                pb = rot.next()
                                for k in range(KC):
                                    P.op("pe", lambda e, k=k, s=s, c4=c4, pb=pb, g0=g0, gn=gn: e.matmul(
                                        ps[pb][:, 0:gn], lhsT=wb[s][:, k, c4 * 128:(c4 + 1) * 128], rhs=hT[:, k, g0:g0 + gn],
                                        start=(k == 0), stop=(k == KC - 1)),
                                        reads=[("wi", s)] + (hT_keys if k in (0, KC - 1) else []), writes=[("ps", pb)], track=(k == KC - 1))
                                if chunk < 24:
                                    copy_op(evac_eng(), stg[si][:, 0:gn], ps[pb][:, 0:gn], [("ps", pb)], [("stg", si)])
                                    P.op("sp", lambda e, si=si, chunk=chunk, g0=g0, gn=gn: e.dma_start(out=Upc[chunk, :, g0:g0 + gn], in_=stg[si][:, 0:gn]),
                                         reads=[("stg", si)], writes=[("Upc", chunk, gi)], dsem=f"d_stg{si}")
                                else:
                                    gcol = V_QG if chunk < 40 else V_KG
                                    qi = cnt["g"] % 2; cnt["g"] += 1
                                    P.op("act", lambda e, pb=pb, gn=gn, qi=qi: e.activation(out=sqb[qi][:, 0:gn], in_=ps[pb][:, 0:gn], func=AF.Square),
                                         reads=[("ps", pb)], writes=[("sqb", qi)])
                                    pj = rots.next()
                                    P.op("pe", lambda e, gn=gn, qi=qi, pj=pj: e.matmul(ps[pj][:, 0:gn], lhsT=ones[:], rhs=sqb[qi][:, 0:gn], start=True, stop=True),
                                         reads=[("sqb", qi), "ones"], writes=[("ps", pj)])
                                    P.op("act", lambda e, gn=gn, qi=qi, pj=pj: e.activation(out=rs[qi][:, 0:gn], in_=ps[pj][:, 0:gn], func=AF.Sqrt,
                                                                                          bias=epsb[:], scale=1.0 / 128),
                                         reads=[("ps", pj), "epsb"], writes=[("rs", qi)])
                                    P.op("dve", lambda e, gn=gn, qi=qi: e.reciprocal(out=rs[qi][:, 0:gn], in_=rs[qi][:, 0:gn]),
                                         reads=[("rs", qi)], writes=[("rs", qi)])
                                    P.op("dve", lambda e, pb=pb, g0=g0, gn=gn, qi=qi, si=si, gcol=gcol: e.scalar_tensor_tensor(
                                        out=stb[si][:, 0:gn], in0=ps[pb][:, 0:gn], scalar=vecs[:, gcol:gcol + 1], in1=rs[qi][:, 0:gn],
                                        op0=ALU.mult, op1=ALU.mult),
                                        reads=[("ps", pb), ("rs", qi), "vecs"], writes=[("stb", si)])
                                    dst = Qd[chunk - 24, :, g0:g0 + gn] if chunk < 40 else Kd[chunk - 40, :, g0:g0 + gn]
                                    key = ("Qd", chunk - 24, gi) if chunk < 40 else ("Kd", chunk - 40, gi)
                                    P.op("sp", lambda e, si=si, dst=dst, gn=gn: e.dma_start(out=dst, in_=stb[si][:, 0:gn]),
                                         reads=[("stb", si)], writes=[key], dsem=f"d_stb{si}")
                    else:
                        vb = cb - 7168 // WB
                        for tt in range(NTT):
                            pi = rot.next()
                            for k in range(KC):
                                P.op("pe", lambda e, k=k, s=s, tt=tt, pi=pi: e.matmul(ps[pi][:, 0:WB], lhsT=hT[:, k, tt * 128:(tt + 1) * 128], rhs=wb[s][:, k, :],
                                                                                    start=(k == 0), stop=(k == KC - 1)),
                                     reads=[("wi", s)] + (hT_keys if k in (0, KC - 1) else []), writes=[("ps", pi)], track=(k == KC - 1))
                            vi = cnt["v"] % 2; cnt["v"] += 1
                            copy_op(evac_eng(), vst[vi][:], ps[pi][:, 0:WB], [("ps", pi)], [("vst", vi)])
                            P.op("sp", lambda e, vi=vi, tt=tt, vb=vb: e.dma_start(out=Vd[tt, :, vb * WB:(vb + 1) * WB], in_=vst[vi][:]),
                                 reads=[("vst", vi)], writes=[("Vd", tt, vb * WB // 512)], dsem=f"d_vst{vi}")

        P.barrier()
        with ExitStack() as es:
            WINS = (2, 4, 8, 16)
            with ExitStack() as es2:
                up = [sb(f"up{i}", [128, NPADDED], F32, es2) for i in range(2)]
                sa = sb("sa", [128, NPADDED], F32, es2); sbb = sb("sbb", [128, NPADDED], F32, es2)
                valid = sb("valid", [128, EN], F32, es2)
                icnt = sb("icnt", [128, 4, NB], F32, es2)
                pooled = sb("pooled", [128, 2, NB], BF16, es2)
                pw = sb("pw", [128, 8, 256], BF16, es2)
                yst = [sb(f"yst{i}", [128, NB], BF16, es2) for i in range(2)]
                P.op("sp", lambda e: e.dma_start(out=valid[:], in_=aux[0, E0:E0 + EN].partition_broadcast(128)), writes=["valid"], dsem="d_valid")
                for wi in range(4):
                    P.op("sp", lambda e, wi=wi: e.dma_start(out=icnt[:, wi, 0:NOWN], in_=aux[1 + wi, OWN0:OWN0 + NOWN].partition_broadcast(128)),
                         writes=[("icnt", wi, 0)], dsem=f"d_icnt{wi}")
                    if with_ctx_out:
                        P.op("sp", lambda e, wi=wi: e.dma_start(out=icnt[:, wi, NOWN:NB], in_=auxc[wi].partition_broadcast(128)),
                             writes=[("icnt", wi, 1)], dsem=f"d_icnt{wi}")
                P.op("pool", lambda e: e.dma_start(out=pw[:], in_=pool_w.rearrange("(k p) n -> p k n", p=128)), writes=["pw"], dsem="d_pw")
                for i in range(2):
                    P.op("dve", lambda e, i=i: e.memset(up[i][:], 0.0), writes=[("up", i)])
                P.op("dve", lambda e: e.memset(sa[:], 0.0), writes=["sa"])
                P.op("dve", lambda e: e.memset(sbb[:], 0.0), writes=["sbb"])
                yc = 0
                for g in range(4):
                    for c2 in range(2):
                        chunk = 2 * g + c2
                        ui = chunk % 2
                        P.op("sp", lambda e, ui=ui, chunk=chunk: e.dma_start(out=up[ui][:, EO:EO + EN], in_=Upc[chunk, :, E0:E0 + EN]),
                             reads=[("Upc", chunk, g_) for g_ in range(NAG)], writes=[("up", ui)], dsem=f"d_up{ui}")
                        P.op("sp", lambda e, ui=ui, chunk=chunk: e.dma_start(out=up[ui][:, CO:CO + NCX], in_=Upc[chunk, :, NE:NA]),
                             reads=[("Upc", chunk, g_) for g_ in range(NAG)], writes=[("upc", ui)], dsem=f"d_upc{ui}")
                        P.op("dve", lambda e, ui=ui: e.tensor_tensor(out=up[ui][:, EO:EO + EN], in0=up[ui][:, EO:EO + EN], in1=valid[:], op=ALU.mult),
                             reads=[("up", ui), "valid"], writes=[("up", ui)])
                        lo, hi = EO - PAD + 8, NPADDED - 8
                        src = up[ui]
                        bufs = [sa, sbb]
                        P.op("dve", lambda e, src=src: e.tensor_tensor(out=sa[:, 8:NPADDED - 8], in0=src[:, 7:NPADDED - 9], in1=src[:, 8:NPADDED - 8], op=ALU.add),
                             reads=[("up", ui), ("upc", ui)], writes=["sa"])
                        cur, other = sa, sbb
                        curk, otherk = "sa", "sbb"
                        for lvl in range(g):
                            sh = 1 << lvl
                            P.op("dve", lambda e, cur=cur, other=other, sh=sh: e.tensor_tensor(
                                out=other[:, 8:NPADDED - 8], in0=cur[:, 8 - sh:NPADDED - 8 - sh], in1=cur[:, 8 + sh:NPADDED - 8 + sh], op=ALU.add),
                                reads=[curk], writes=[otherk])
                            cur, other = other, cur
                            curk, otherk = otherk, curk
                        segs = [(EO + 16, 0, NOWN)] + ([(CO, NOWN, NCX)] if with_ctx_out else [])
                        for (so, bo, n) in segs:
                            P.op("dve", lambda e, cur=cur, so=so, bo=bo, n=n, g=g: e.tensor_tensor(
                                out=cur[:, so:so + n], in0=cur[:, so:so + n], in1=icnt[:, g, bo:bo + n], op=ALU.mult),
                                reads=[curk] + [("icnt", g, 0), ("icnt", g, 1)], writes=[curk])
                            P.op("dve", lambda e, cur=cur, so=so, bo=bo, n=n, c2=c2, src=src: e.tensor_tensor(
                                out=pooled[:, c2, bo:bo + n], in0=cur[:, so:so + n], in1=src[:, so:so + n], op=ALU.subtract),
                                reads=[curk, ("up", ui), ("upc", ui)], writes=[("pooled", c2)])
                    for oc in range(2):
                        yi = yc % 2; yc += 1
                        for gi, (g0, gn) in enumerate(BGRP):
                            for k in range(2):
                                P.op("pe", lambda e, k=k, g=g, oc=oc, gi=gi, g0=g0, gn=gn: e.matmul(
                                    ps[gi][:, 0:gn], lhsT=pw[:, 2 * g + k, oc * 128:(oc + 1) * 128], rhs=pooled[:, k, g0:g0 + gn],
                                    start=(k == 0), stop=(k == 1)),
                                    reads=["pw", ("pooled", 0), ("pooled", 1)], writes=[("ps", gi)], track=(k == 1))
                            P.op("act", lambda e, gi=gi, g0=g0, gn=gn, yi=yi, g=g, oc=oc: e.activation(
                                out=yst[yi][:, g0:g0 + gn], in_=ps[gi][:, 0:gn], func=AF.Copy, scale=vecs[:, V_PS + 2 * g + oc:V_PS + 2 * g + oc + 1]),
                                reads=[("ps", gi), "vecs"], writes=[("yst", yi)])
                        P.op("sp", lambda e, yi=yi, g=g, oc=oc: e.dma_start(out=Yd[2 * g + oc], in_=yst[yi][:]),
                             reads=[("yst", yi)], writes=[("Yd", 2 * g + oc)], dsem=f"d_yst{yi}")

            P.barrier()
            with ExitStack() as es2:
                ua = [sb(f"ua{i}", [128, EN + NCX], F32, es2) for i in range(2)]
                ug = [sb(f"ug{i}", [128, EN + NCX], F32, es2) for i in range(2)]
                hp = [sb(f"hp{i}", [128, NPADDED], F32, es2) for i in range(2)]
                valid = sb("valid2", [128, EN], F32, es2)
                cvo = sb("cvo", [128, 8, NB], F32, es2)
                sqc = [sb(f"sqc{i}", [128, NB], BF16, es2) for i in range(2)]
                rsc = sb("rsc", [128, NB], F32, es2)
                zt = sb("zt", [128, 8, NB], BF16, es2)
                ztmp = [sb(f"ztmp{i}", [128, NB], F32, es2) for i in range(2)]
                pwc = sb("pwc", [128, 8, 1024], BF16, es2)
                yst = [sb(f"ystc{i}", [128, NB], BF16, es2) for i in range(2)]
                P.op("sp", lambda e: e.dma_start(out=valid[:], in_=aux[0, E0:E0 + EN].partition_broadcast(128)), writes=["valid2"], dsem="d_valid")
                P.op("pool", lambda e: e.dma_start(out=pwc[:], in_=pw_w.rearrange("(k p) n -> p k n", p=128)), writes=["pwc"], dsem="d_pwc")
                for i in range(2):
                    P.op("dve", lambda e, i=i: e.memset(hp[i][:], 0.0), writes=[("hp", i)])
                segs = [(EO + 16, 0, NOWN)] + ([(CO, NOWN, NCX)] if with_ctx_out else [])
                for i in range(8):
                    s = i % 2
                    P.op("sp", lambda e, s=s, i=i: e.dma_start(out=ua[s][:, 0:EN], in_=Upc[8 + i, :, E0:E0 + EN]), writes=[("ua", s)], dsem=f"d_ua{s}")
                    P.op("sp", lambda e, s=s, i=i: e.dma_start(out=ua[s][:, EN:EN + NCX], in_=Upc[8 + i, :, NE:NA]), writes=[("uac", s)], dsem=f"d_uac{s}")
                    P.op("sp", lambda e, s=s, i=i: e.dma_start(out=ug[s][:, 0:EN], in_=Upc[16 + i, :, E0:E0 + EN]), writes=[("ug", s)], dsem=f"d_ug{s}")
                    P.op("sp", lambda e, s=s, i=i: e.dma_start(out=ug[s][:, EN:EN + NCX], in_=Upc[16 + i, :, NE:NA]), writes=[("ugc", s)], dsem=f"d_ugc{s}")
                    P.op("act", lambda e, s=s: e.activation(out=ug[s][:], in_=ug[s][:], func=AF.Sigmoid), reads=[("ug", s), ("ugc", s)], writes=[("ug", s), ("ugc", s)])
                    P.op("dve", lambda e, s=s: e.tensor_tensor(out=hp[s][:, EO:EO + EN], in0=ua[s][:, 0:EN], in1=ug[s][:, 0:EN], op=ALU.mult),
                         reads=[("ua", s), ("ug", s)], writes=[("hp", s)])
                    P.op("dve", lambda e, s=s: e.tensor_tensor(out=hp[s][:, CO:CO + NCX], in0=ua[s][:, EN:EN + NCX], in1=ug[s][:, EN:EN + NCX], op=ALU.mult),
                         reads=[("uac", s), ("ugc", s), ("ug", s)], writes=[("hp", s)])
                    P.op("dve", lambda e, s=s: e.tensor_tensor(out=hp[s][:, EO:EO + EN], in0=hp[s][:, EO:EO + EN], in1=valid[:], op=ALU.mult),
                         reads=[("hp", s), "valid2"], writes=[("hp", s)])
                    for (so, bo, n) in segs:
                        for k in range(31):
                            wcol = V_CW + i * 31 + k
                            off = so + k - 15
                            if k == 0:
                                P.op("dve", lambda e, s=s, i=i, off=off, bo=bo, n=n, wcol=wcol: e.tensor_scalar(
                                    out=cvo[:, i, bo:bo + n], in0=hp[s][:, off:off + n], scalar1=vecs[:, wcol:wcol + 1],
                                    scalar2=vecs[:, V_CB + i:V_CB + i + 1], op0=ALU.mult, op1=ALU.add),
                                    reads=[("hp", s), "vecs"], writes=[("cvo", i, bo)])
                            else:
                                P.op("dve", lambda e, s=s, i=i, off=off, bo=bo, n=n, wcol=wcol: e.scalar_tensor_tensor(
                                    out=cvo[:, i, bo:bo + n], in0=hp[s][:, off:off + n], scalar=vecs[:, wcol:wcol + 1],
                                    in1=cvo[:, i, bo:bo + n], op0=ALU.mult, op1=ALU.add),
                                    reads=[("hp", s), "vecs"], writes=[("cvo", i, bo)])
                    qi = i % 2
                    P.op("act", lambda e, i=i, qi=qi: e.activation(out=sqc[qi][:], in_=cvo[:, i, :], func=AF.Square),
                         reads=[("cvo", i, b_) for (_, b_, _) in segs], writes=[("sqc", qi)])
                    for gi, (g0, gn) in enumerate(BGRP):
                        P.op("pe", lambda e, qi=qi, gi=gi, g0=g0, gn=gn, i=i: e.matmul(ps[4 + gi][:, 0:gn], lhsT=ones[:], rhs=sqc[qi][:, g0:g0 + gn],
                                                                                 start=(i == 0), stop=(i == 7)),
                             reads=[("sqc", qi), "ones"], writes=[("ps", 4 + gi)], track=(gi == len(BGRP) - 1))
                for gi, (g0, gn) in enumerate(BGRP):
                    P.op("act", lambda e, gi=gi, g0=g0, gn=gn: e.activation(out=rsc[:, g0:g0 + gn], in_=ps[4 + gi][:, 0:gn], func=AF.Sqrt,
                                                                         bias=epsb[:], scale=1.0 / 1024),
                         reads=[("ps", 4 + gi), "epsb"], writes=[("rsc", gi)])
                    P.op("dve", lambda e, g0=g0, gn=gn: e.reciprocal(out=rsc[:, g0:g0 + gn], in_=rsc[:, g0:g0 + gn]),
                         reads=[("rsc", gi)], writes=[("rsc", gi)])
                for i in range(8):
                    qi = i % 2
                    P.op("dve", lambda e, i=i, qi=qi: e.scalar_tensor_tensor(out=ztmp[qi][:], in0=cvo[:, i, :], scalar=vecs[:, V_CG + i:V_CG + i + 1],
                                                                         in1=rsc[:], op0=ALU.mult, op1=ALU.mult),
                         reads=[("cvo", i, b_) for (_, b_, _) in segs] + [("rsc", g) for g in range(len(BGRP))] + ["vecs"], writes=[("ztmp", qi)])
                    P.op("act", lambda e, i=i, qi=qi: e.activation(out=zt[:, i, :], in_=ztmp[qi][:], func=AF.Silu),
                         reads=[("ztmp", qi)], writes=[("zt", i)])
                yc = 0
                for oc in range(8):
                    yi = yc % 2; yc += 1
                    for gi, (g0, gn) in enumerate(BGRP):
                        for k in range(8):
                            P.op("pe", lambda e, k=k, oc=oc, gi=gi, g0=g0, gn=gn: e.matmul(
                                ps[gi][:, 0:gn], lhsT=pwc[:, k, oc * 128:(oc + 1) * 128], rhs=zt[:, k, g0:g0 + gn], start=(k == 0), stop=(k == 7)),
                                reads=["pwc"] + [("zt", kk) for kk in range(8)], writes=[("ps", gi)], track=(k == 7))
                        copy_op(evac_eng(), yst[yi][:, g0:g0 + gn], ps[gi][:, 0:gn], [("ps", gi)], [("ystc", yi)])
                    P.op("sp", lambda e, yi=yi, oc=oc: e.dma_start(out=Yd[8 + oc], in_=yst[yi][:]),
                         reads=[("ystc", yi)], writes=[("Yd", 8 + oc)], dsem=f"d_ystc{yi}")

            P.barrier()
            with ExitStack() as es2:
                qT = [sb(f"qT{i}", [128, NA], BF16, es2) for i in range(2)]
                kT = [sb(f"kT{i}", [128, NA], BF16, es2) for i in range(2)]
                Vh = [sb(f"Vh{i}", [128, NTT, 128], BF16, es2) for i in range(2)]
                bt = [sb(f"bt{i}", [128, NQ * 320], F32, es2) for i in range(2)]
                NSB = 3
                sbuf_s = [sb(f"ss{i}", [128, 320], F32, es2) for i in range(NSB)]
                pT = [sb(f"pT{i}", [128, 320], BF16, es2) for i in range(NSB)]
                pc = [sb(f"pc{i}", [128, 2, 512], BF16, es2) for i in range(2)]
                rden = [sb(f"rden{i}", [128, 512], F32, es2) for i in range(2)]
                yst = [sb(f"ysta{i}", [128, NB], BF16, es2) for i in range(2)]
                hook = L.get("attn_hook")
                if hook is not None:
                    hook_state = L["attn_hook_alloc"](sb, es2)
                SCALE = 128.0 ** -0.5
                rotS = Rot([0, 1, 2])
                st = {"cq": 0, "u": 0}
                SK = 2
                for h in range(16):
                    s = h % 2
                    P.op("sp", lambda e, s=s, h=h: e.dma_start(out=qT[s][:], in_=Qd[h]), reads=[("Qd", h, g_) for g_ in range(NAG)], writes=[("qT", s)], dsem=f"d_qT{s}")
                    P.op("sp", lambda e, s=s, h=h: e.dma_start(out=kT[s][:], in_=Kd[h]), reads=[("Kd", h, g_) for g_ in range(NAG)], writes=[("kT", s)], dsem=f"d_kT{s}")
                    P.op("sp", lambda e, s=s, h=h: e.dma_start(out=Vh[s][:], in_=Vd[:, :, h * 128:(h + 1) * 128].rearrange("t p d -> p t d")),
                         reads=[("Vd", tt, h // 4) for tt in range(NTT)], writes=[("Vh", s)], dsem=f"d_Vh{s}")
                    P.op("sp", lambda e, s=s, h=h: e.dma_start(out=bt[s][:], in_=biasd[h]), writes=[("bt", s)], dsem=f"d_bt{s}")
                    grp_ci = {}

                    def stageA(j, s=s):
                        u = st["u"] % NSB; st["u"] += 1
                        nt = 4 if j % 2 == 0 else 5
                        t0 = j // 2
                        sbk = rotS.next()
                        psS = ps[sbk]
                        qs = OWN0 + j * 64
                        for m in range(nt):
                            P.op("pe", lambda e, m=m: e.matmul(
                                psS[:, m * 64:(m + 1) * 64], lhsT=kT[s][:, (t0 + m) * 128:(t0 + m + 1) * 128], rhs=qT[s][:, qs:qs + 64],
                                start=True, stop=True),
                                reads=[("kT", s), ("qT", s)], writes=[("ps", sbk)], track=(m == nt - 1))
                        P.op("dve", lambda e: e.scalar_tensor_tensor(
                            out=sbuf_s[u][:, 0:nt * 64], in0=psS[:, 0:nt * 64], scalar=SCALE, in1=bt[s][:, j * 320:j * 320 + nt * 64],
                            op0=ALU.mult, op1=ALU.add),
                            reads=[("ps", sbk), ("bt", s)], writes=[("ss", u)])
                        P.op("act", lambda e: e.activation(out=pT[u][:, 0:nt * 64], in_=sbuf_s[u][:, 0:nt * 64], func=AF.Exp),
                             reads=[("ss", u)], writes=[("pT", u)])
                        return u

                    def stageC(grp, s=s):
                        ci = st["cq"] % 2; st["cq"] += 1
                        grp_ci[grp] = ci
                        pO, pD = 4 + 2 * ci, 5 + 2 * ci
                        q0 = OWN0 + grp * 512
                        for ct in range(2):
                            sbk = rotS.next()
                            P.op("pe", lambda e, ct=ct, sbk=sbk: e.matmul(ps[sbk][:, :], lhsT=kT[s][:, NE + ct * 128:NE + (ct + 1) * 128],
                                                                        rhs=qT[s][:, q0:q0 + 512], start=True, stop=True),
                                 reads=[("kT", s), ("qT", s)], writes=[("ps", sbk)])
                            P.op("act", lambda e, ct=ct, sbk=sbk: e.activation(out=pc[ci][:, ct, :], in_=ps[sbk][:, :], func=AF.Exp, scale=SCALE),
                                 reads=[("ps", sbk)], writes=[("pc", ci, ct)])
                        for ct in range(2):
                            P.op("pe", lambda e, ct=ct: e.matmul(ps[pO][:, :], lhsT=Vh[s][:, NE // 128 + ct, :], rhs=pc[ci][:, ct, :],
                                                               start=(ct == 0), stop=False, skip_group_check=True),
                                 reads=[("Vh", s), ("pc", ci, ct)], writes=[("ps", pO)])
                            P.op("pe", lambda e, ct=ct: e.matmul(ps[pD][:, :], lhsT=ones[:], rhs=pc[ci][:, ct, :],
                                                               start=(ct == 0), stop=False, skip_group_check=True),
                                 reads=["ones", ("pc", ci, ct)], writes=[("ps", pD)])

                    def stageB(j, u, s=s):
                        grp, jr = j // 8, j % 8
                        if jr == 0:
                            stageC(grp)
                        ci = grp_ci[grp]
                        pO, pD = 4 + 2 * ci, 5 + 2 * ci
                        nt = 4 if j % 2 == 0 else 5
                        t0 = j // 2
                        for m in range(nt):
                            last = (jr == 7 and m == nt - 1)
                            P.op("pe", lambda e, m=m, last=last: e.matmul(
                                ps[pO][:, jr * 64:(jr + 1) * 64], lhsT=Vh[s][:, t0 + m, :], rhs=pT[u][:, m * 64:(m + 1) * 64],
                                start=False, stop=last, skip_group_check=True),
                                reads=[("Vh", s), ("pT", u)], writes=[("ps", pO)], track=(m == nt - 1))
                            P.op("pe", lambda e, m=m, last=last: e.matmul(
                                ps[pD][:, jr * 64:(jr + 1) * 64], lhsT=ones[:], rhs=pT[u][:, m * 64:(m + 1) * 64],
                                start=False, stop=last, skip_group_check=True),
                                reads=["ones", ("pT", u)], writes=[("ps", pD)], track=(m == nt - 1))
                        if jr == 7:
                            P.op("dve", lambda e: e.reciprocal(out=rden[ci][:], in_=ps[pD][:, :]),
                                 reads=[("ps", pD)], writes=[("rden", ci)])
                            P.op("dve", lambda e: e.tensor_tensor(out=yst[s][:, grp * 512:(grp + 1) * 512], in0=ps[pO][:, :],
                                                                  in1=rden[ci][:], op=ALU.mult),
                                 reads=[("ps", pO), ("rden", ci)], writes=[("ysta", s, grp)])

                    pend = []
                    for j in range(NQ):
                        pend.append((j, stageA(j)))
                        if len(pend) > SK:
                            stageB(*pend.pop(0))
                        if hook is not None and j % 8 == 7:
                            hook(hook_state)
                    while pend:
                        stageB(*pend.pop(0))
                    if with_ctx_out:
                        ci = st["cq"] % 2; st["cq"] += 1
                        pO, pD = 4 + 2 * ci, 5 + 2 * ci
                        for ct in range(2):
                            sbk = rotS.next()
                            P.op("pe", lambda e, s=s, ct=ct, sbk=sbk: e.matmul(ps[sbk][:, 0:256], lhsT=kT[s][:, NE + ct * 128:NE + (ct + 1) * 128],
                                                                             rhs=qT[s][:, NE:NA], start=True, stop=True),
                                 reads=[("kT", s), ("qT", s)], writes=[("ps", sbk)])
                            P.op("act", lambda e, ct=ct, ci=ci, sbk=sbk: e.activation(out=pc[ci][:, ct, 0:256], in_=ps[sbk][:, 0:256], func=AF.Exp, scale=SCALE),
                                 reads=[("ps", sbk)], writes=[("pc", ci, ct)])
                        for ct in range(2):
                            P.op("pe", lambda e, s=s, ct=ct, ci=ci, pO=pO: e.matmul(ps[pO][:, 0:256], lhsT=Vh[s][:, NE // 128 + ct, :], rhs=pc[ci][:, ct, 0:256],
                                                                               start=(ct == 0), stop=(ct == 1)),
                                 reads=[("Vh", s), ("pc", ci, ct)], writes=[("ps", pO)])
                            P.op("pe", lambda e, ct=ct, ci=ci, pD=pD: e.matmul(ps[pD][:, 0:256], lhsT=ones[:], rhs=pc[ci][:, ct, 0:256],
                                                                          start=(ct == 0), stop=(ct == 1)),
                                 reads=["ones", ("pc", ci, ct)], writes=[("ps", pD)])
                        P.op("dve", lambda e, ci=ci, pD=pD: e.reciprocal(out=rden[ci][:, 0:256], in_=ps[pD][:, 0:256]),
                             reads=[("ps", pD)], writes=[("rden", ci)])
                        P.op("dve", lambda e, ci=ci, pO=pO, s=s: e.tensor_tensor(out=yst[s][:, NOWN:NB], in0=ps[pO][:, 0:256],
                                                                           in1=rden[ci][:, 0:256], op=ALU.mult),
                             reads=[("ps", pO), ("rden", ci)], writes=[("ysta", s, "c")])
                    P.op("sp", lambda e, s=s, h=h: e.dma_start(out=Yd[16 + h], in_=yst[s][:]),
                         reads=[("ysta", s, g) for g in range(NOWN // 512)] + [("ysta", s, "c")], writes=[("Yd", 16 + h)], dsem=f"d_ysta{s}")
                if hook is not None:
                    L["attn_hook_flush"](hook_state)

        P.barrier()
        with ExitStack() as es:
            rstd2 = sb("rstd2", [128, NB], F32, es)
            xsrc = L["xsrc"]; xcsrc = L["xcsrc"]
            NG = len(BGRP)
            with ExitStack() as es2:
                yT = sb("yT", [128, KC, NB], BF16, es2)
                wb = [sb(f"wo{i}", [128, KC, WB], BF16, es2) for i in range(2)]
                xo = [sb(f"xo{i}", [128, NB], F32, es2) for i in range(2)]
                xm = [sb(f"xm{i}", [128, NB], F32, es2) for i in range(2)]
                sq2 = [sb(f"sq2{i}", [128, NB], BF16, es2) for i in range(2)]
                for k in range(KC):
                    P.op("sp", lambda e, k=k: e.dma_start(out=yT[:, k, :], in_=Yd[k]), reads=[("Yd", k)], writes=[("yT", k)], dsem=f"d_yT{k % 4}")
                yT_keys = [("yT", k) for k in range(KC)]
                wsrc = w_out.rearrange("(k p) n -> p k n", p=128)
                rot = Rot(range(4))
                for cb in range(D // WB):
                    s = cb % 2
                    P.op("pool", lambda e, cb=cb, s=s: e.dma_start(out=wb[s][:], in_=wsrc[:, :, cb * WB:(cb + 1) * WB]),
                         writes=[("wo", s)], dsem=f"d_wo{s}")
                    for c4 in range(WC):
                        oc = cb * WC + c4
                        xi = oc % 2
                        P.op("sp", lambda e, xi=xi, oc=oc: e.dma_start(out=xo[xi][:, 0:NOWN], in_=xsrc[oc, :, OWN0:OWN0 + NOWN]),
                             writes=[("xo", xi, 0)], dsem=f"d_xo{xi}")
                        if with_ctx_out:
                            P.op("sp", lambda e, xi=xi, oc=oc: e.dma_start(out=xo[xi][:, NOWN:NB], in_=xcsrc[oc]),
                                 writes=[("xo", xi, 1)], dsem=f"d_xoc{xi}")
                        for gi, (g0, gn) in enumerate(BGRP):
                            pb = rot.next()
                            for k in range(KC):
                                P.op("pe", lambda e, k=k, s=s, c4=c4, pb=pb, g0=g0, gn=gn: e.matmul(
                                    ps[pb][:, 0:gn], lhsT=wb[s][:, k, c4 * 128:(c4 + 1) * 128], rhs=yT[:, k, g0:g0 + gn],
                                    start=(k == 0), stop=(k == KC - 1)),
                                    reads=[("wo", s)] + (yT_keys if k in (0, KC - 1) else []), writes=[("ps", pb)], track=(k == KC - 1))
                            which = 0 if g0 < NOWN else 1
                            P.op("dve", lambda e, pb=pb, g0=g0, gn=gn, xi=xi, oc=oc, which=which: e.scalar_tensor_tensor(
                                out=xm[xi][:, g0:g0 + gn], in0=ps[pb][:, 0:gn], scalar=modv(2, oc, which), in1=xo[xi][:, g0:g0 + gn],
                                op0=ALU.mult, op1=ALU.add),
                                reads=[("ps", pb), ("xo", xi, 0), ("xo", xi, 1), "mod"], writes=[("xm", xi, gi)])
                        P.op("sp", lambda e, xi=xi, oc=oc: e.dma_start(out=Xw[oc], in_=xm[xi][:]),
                             reads=[("xm", xi, g) for g in range(NG)], writes=[("Xw", oc)], dsem=f"d_xm{xi}")
                        P.op("act", lambda e, xi=xi: e.activation(out=sq2[xi][:], in_=xm[xi][:], func=AF.Square),
                             reads=[("xm", xi, g) for g in range(NG)], writes=[("sq2", xi)])
                        for gi, (g0, gn) in enumerate(BGRP):
                            P.op("pe", lambda e, xi=xi, gi=gi, g0=g0, gn=gn, oc=oc: e.matmul(ps[4 + gi][:, 0:gn], lhsT=ones[:], rhs=sq2[xi][:, g0:g0 + gn],
                                                                                      start=(oc == 0), stop=(oc == KC - 1), skip_group_check=True),
                                 reads=[("sq2", xi), "ones"], writes=[("ps", 4 + gi)], track=(gi == NG - 1))
                for gi, (g0, gn) in enumerate(BGRP):
                    P.op("act", lambda e, gi=gi, g0=g0, gn=gn: e.activation(out=rstd2[:, g0:g0 + gn], in_=ps[4 + gi][:, 0:gn], func=AF.Sqrt,
                                                                         bias=epsb[:], scale=1.0 / D),
                         reads=[("ps", 4 + gi), "epsb"], writes=[("rstd2", gi)])
                    P.op("dve", lambda e, g0=g0, gn=gn: e.reciprocal(out=rstd2[:, g0:g0 + gn], in_=rstd2[:, g0:g0 + gn]),
                         reads=[("rstd2", gi)], writes=[("rstd2", gi)])
            P.barrier()
            with ExitStack() as es2:
                xo = [sb(f"xo_{i}", [128, NB], F32, es2) for i in range(2)]
                xm = [sb(f"xm_{i}", [128, NB], F32, es2) for i in range(2)]
                hst = [sb(f"hst{i}", [128, NB], BF16, es2) for i in range(2)]
                for fc in range(KC):
                    xi = fc % 2
                    P.op("sp", lambda e, xi=xi, fc=fc: e.dma_start(out=xo[xi][:], in_=Xw[fc]), reads=[("Xw", fc)],
                         writes=[("xo", xi, 0), ("xo", xi, 1)], dsem=f"d_xo{xi}")
                    P.op("dve", lambda e, xi=xi: e.tensor_tensor(out=xm[xi][:], in0=xo[xi][:], in1=rstd2[:], op=ALU.mult),
                         reads=[("xo", xi, 0), ("xo", xi, 1)] + [("rstd2", g) for g in range(NG)], writes=[("xm", xi, g) for g in range(NG)])
                    P.op("act", lambda e, xi=xi, fc=fc: e.activation(out=hst[xi][:, 0:NOWN], in_=xm[xi][:, 0:NOWN], func=AF.Identity,
                                                                  bias=modv(3, fc, 0), scale=A2[:, fc, 0:1]),
                         reads=[("xm", xi, g) for g in range(NG)] + ["mod", "A2"], writes=[("hst", xi, 0)])
                    if with_ctx_out:
                        P.op("act", lambda e, xi=xi, fc=fc: e.activation(out=hst[xi][:, NOWN:NB], in_=xm[xi][:, NOWN:NB], func=AF.Identity,
                                                                      bias=modv(3, fc, 1), scale=A2[:, fc, 1:2]),
                             reads=[("xm", xi, g) for g in range(NG)] + ["mod", "A2"], writes=[("hst", xi, 1)])
                    P.op("sp", lambda e, xi=xi, fc=fc: e.dma_start(out=H2d[fc], in_=hst[xi][:]), reads=[("hst", xi, 0), ("hst", xi, 1)],
                         writes=[("H2d", fc)], dsem=f"d_hst{xi}")
            P.barrier()
            NSL = 8; HCS = DFF // NSL // 128
            parts = L["parts"]
            TPM = max(n for _, n in parts)
            with ExitStack() as es2:
                h2T = sb("h2T", [128, KC, TPM], BF16, es2)
                hid = sb("hid", [128, HCS, TPM], BF16, es2)
                w1b = [sb(f"w1b{i}", [128, KC, 256], BF16, es2) for i in range(2)]
                w2b = [sb(f"w2b{i}", [128, HCS, 512], BF16, es2) for i in range(2)]
                rl = [sb(f"rl{i}", [128, 512], F32, es2) for i in range(2)]
                xa = [sb(f"xa{i}", [128, TPM], F32, es2) for i in range(2)]
                w1src = w1.rearrange("(k p) n -> p k n", p=128)
                w2src = w2.rearrange("(k p) n -> p k n", p=128)
                c1 = 0; c2 = 0; rc = 0; xc_ = 0
                rot1 = Rot(range(4)); rot2 = Rot(range(4, 8))
                for pi_, (t0, tn) in enumerate(parts):
                    TG = groups(tn)
                    for k in range(KC):
                        P.op("sp", lambda e, k=k: e.dma_start(out=h2T[:, k, 0:tn], in_=H2d[k, :, t0:t0 + tn]), reads=[("H2d", k)],
                             writes=[("h2T", k)], dsem=f"d_h2T{k % 4}")
                    h2_keys = [("h2T", k) for k in range(KC)]
                    for q8 in range(NSL):
                        for b4 in range(HCS // 2):
                            s = c1 % 2; c1 += 1
                            col0 = q8 * HCS * 128 + b4 * 256
                            P.op("pool", lambda e, s=s, col0=col0: e.dma_start(out=w1b[s][:], in_=w1src[:, :, col0:col0 + 256]),
                                 writes=[("w1b", s)], dsem=f"d_w1b{s}")
                            for c4 in range(2):
                                hc = b4 * 2 + c4
                                for gi, (g0, gn) in enumerate(TG):
                                    pb = rot1.next()
                                    for k in range(KC):
                                        P.op("pe", lambda e, k=k, s=s, c4=c4, pb=pb, g0=g0, gn=gn: e.matmul(
                                            ps[pb][:, 0:gn], lhsT=w1b[s][:, k, c4 * 128:(c4 + 1) * 128], rhs=h2T[:, k, g0:g0 + gn],
                                            start=(k == 0), stop=(k == KC - 1)),
                                            reads=[("w1b", s)] + (h2_keys if k in (0, KC - 1) else []), writes=[("ps", pb)], track=(k == KC - 1))
                                    ri = rc % 2; rc += 1
                                    P.op("act", lambda e, pb=pb, gn=gn, ri=ri: e.activation(out=rl[ri][:, 0:gn], in_=ps[pb][:, 0:gn], func=AF.Relu),
                                         reads=[("ps", pb)], writes=[("rl", ri)])
                                    P.op("dve", lambda e, g0=g0, gn=gn, ri=ri, hc=hc: e.tensor_tensor(out=hid[:, hc, g0:g0 + gn], in0=rl[ri][:, 0:gn],
                                                                                               in1=rl[ri][:, 0:gn], op=ALU.mult),
                                         reads=[("rl", ri)], writes=[("hid", hc, gi)])
                        hid_keys = [("hid", hc, gi) for hc in range(HCS) for gi in range(len(TG))]
                        for b8 in range(8):
                            s = c2 % 2; c2 += 1
                            P.op("pool", lambda e, s=s, q8=q8, b8=b8: e.dma_start(out=w2b[s][:], in_=w2src[:, q8 * HCS:(q8 + 1) * HCS, b8 * 512:(b8 + 1) * 512]),
                                 writes=[("w2b", s)], dsem=f"d_w2b{s}")
                            for c4 in range(4):
                                oc = b8 * 4 + c4
                                xi = xc_ % 2; xc_ += 1
                                P.op("sp", lambda e, xi=xi, oc=oc: e.dma_start(out=xa[xi][:, 0:tn], in_=Xw[oc, :, t0:t0 + tn]), reads=[("Xw", oc)],
                                     writes=[("xa", xi)], dsem=f"d_xa_{xi}")
                                for gi, (g0, gn) in enumerate(TG):
                                    pb = rot2.next()
                                    for k in range(HCS):
                                        P.op("pe", lambda e, k=k, s=s, c4=c4, pb=pb, g0=g0, gn=gn: e.matmul(
                                            ps[pb][:, 0:gn], lhsT=w2b[s][:, k, c4 * 128:(c4 + 1) * 128], rhs=hid[:, k, g0:g0 + gn],
                                            start=(k == 0), stop=(k == HCS - 1)),
                                            reads=[("w2b", s)] + (hid_keys if k in (0, HCS - 1) else []), writes=[("ps", pb)], track=(k == HCS - 1))
                                    which = 0 if (t0 + g0) < NOWN else 1
                                    P.op("dve", lambda e, pb=pb, g0=g0, gn=gn, xi=xi, oc=oc, which=which: e.scalar_tensor_tensor(
                                        out=xa[xi][:, g0:g0 + gn], in0=ps[pb][:, 0:gn], scalar=modv(5, oc, which), in1=xa[xi][:, g0:g0 + gn],
                                        op0=ALU.mult, op1=ALU.add),
                                        reads=[("ps", pb), ("xa", xi), "mod"], writes=[("xa", xi)])
                                t = P.op("sp", lambda e, xi=xi, oc=oc: e.dma_start(out=Xw[oc, :, t0:t0 + tn], in_=xa[xi][:, 0:tn]),
                                         reads=[("xa", xi)], writes=[("Xw", oc)], dsem=f"d_xs_{xi}")
                                if q8 == NSL - 1 and L.get("final"):
                                    outs.append(t)
        P.barrier()


def build_fused():
    nc = bass.Bass("TRN2", target_bir_lowering=False)
    P = Prog(nc)
    outs = []
    xT = nc.dram_tensor("xT", [D, NE0], F32, kind="ExternalInput").ap()
    xcT = nc.dram_tensor("xcT", [D, NCX], F32, kind="ExternalInput").ap()
    cv = nc.dram_tensor("cv", [128, KC, 2], F32, kind="ExternalInput").ap()
    auxc = nc.dram_tensor("auxc", [4, NCX], F32, kind="ExternalInput").ap()
    ident_d = nc.dram_tensor("ident", [128, 128], F32, kind="ExternalInput").ap()
    msel_d = nc.dram_tensor("msel", [128, 4], F32, kind="ExternalInput").ap()
    NB0 = NOWN0 + NCX
    Xw0 = nc.dram_tensor("Xw0", [KC, 128, NB0], F32).ap()
    XwF = nc.dram_tensor("Xw", [KC, 128, NOWN], F32, kind="ExternalOutput").ap()
    with ExitStack() as top:
        ps = [top.enter_context(nc.psum_tensor(f"ps{i}", [128, 512], F32)) for i in range(8)]
        ident = top.enter_context(nc.sbuf_tensor("sb_ident", [128, 128], F32))
        ones = top.enter_context(nc.sbuf_tensor("sb_ones", [128, 128], BF16))
        epsb = top.enter_context(nc.sbuf_tensor("sb_epsb", [128, 1], F32))
        P.op("sp", lambda e: e.dma_start(out=ident[:], in_=ident_d), writes=["ident"], dsem="d_misc2")
        P.op("dve", lambda e: e.memset(ones[:], 1.0), writes=["ones"])
        P.op("dve", lambda e: e.memset(epsb[:], EPS), writes=["epsb"])
        cmn = {"cv": cv, "auxc": auxc, "ident": ident, "ones": ones, "epsb": epsb}
        L0 = {"tag": "_0", "NE": NE0, "NOWN": NOWN0, "ctx_out": True, "WB": 256,
              "xsrc": xT.rearrange("(k p) n -> k p n", p=128), "xcsrc": xcT.rearrange("(k p) n -> k p n", p=128),
              "Xw": Xw0, "parts": [(0, 1024), (1024, 768)]}
        emit_layer(nc, P, ps, cmn, L0, outs)
        L1 = {"tag": "_1", "NE": NE, "NOWN": NOWN, "ctx_out": False, "WB": 512,
              "xsrc": Xw0[:, :, 0:NOWN0], "xcsrc": Xw0[:, :, NOWN0:NB0], "sel": {"msel": msel_d},
              "Xw": XwF, "parts": [(0, 1024)], "final": True}
        emit_layer(nc, P, ps, cmn, L1, outs)
        P.final_wait("sp", outs)
        with nc.Block() as block:
            P.emit(block)
    return nc


def _chunked(v):
    return np.ascontiguousarray(v.reshape(-1, 128).T)


def _rowmap(r, nrows, rel0):
    R0 = 16 * r
    rows = []
    for b in range(nrows):
        rel = rel0 + b
        g = R0 + rel
        if 0 <= g <= 63 and rel <= 22:
            rows.append((g, True, True))
        elif r == 0 and -4 <= rel <= -1:
            rows.append((rel + 8, False, True))
        elif r == 3 and 16 <= rel <= 18:
            rows.append((56 + rel - 16, False, True))
        else:
            rows.append((0, False, False))
    return rows


def _bias_tables(rpb_l, rows, nq):
    jq = np.arange(64)[None, :]; jk = np.arange(64)[:, None]
    cs = np.clip(jq - 8, 0, 48)
    cvalid = (jk >= cs) & (jk < cs + 16)
    dc = np.clip(jk - jq + 15, 0, 30)
    out = np.full((16, 128, nq * 320), NEG, np.float32)
    for j in range(nq):
        qi, qnat, _ = rows[j + 4]
        nt = 4 if j % 2 == 0 else 5
        t0 = j // 2
        want = set(range(int(np.clip(qi - 4, 0, 56)), int(np.clip(qi - 4, 0, 56)) + 8)) if qnat else None
        got = set()
        for m in range(nt):
            for half in range(2):
                b = 2 * (t0 + m) + half
                if b < j or b > j + 7 or b >= len(rows):
                    continue
                kr, knat, kused = rows[b]
                if qnat:
                    if (not kused) or kr not in want or kr in got:
                        continue
                    got.add(kr)
                    dr = kr - qi + 7
                else:
                    dr = b - j + 3
                blk = np.where(cvalid[None], rpb_l[:, dr][:, dc], NEG)
                out[:, half * 64:(half + 1) * 64, j * 320 + m * 64:j * 320 + (m + 1) * 64] = blk
        if qnat:
            assert got == want, (j, qi, got, want)
    return out


def _aux(rows, r, rel0):
    R0 = 16 * r
    L = 4096
    n = len(rows)
    aux = np.ones((5, n * 64), np.float32)
    for b in range(n):
        nat = rows[b][1]
        aux[0, b * 64:(b + 1) * 64] = 1.0 if nat else 0.0
        if nat:
            g = (R0 + rel0 + b) * 64 + np.arange(64)
            for wi, w in enumerate((2, 4, 8, 16)):
                lo = np.clip(g - w // 2, 0, L - 1); hi = np.clip(g + (w - 1 - w // 2), 0, L - 1)
                aux[1 + wi, b * 64:(b + 1) * 64] = 1.0 / (hi - lo + 1)
    return aux


def _auxc():
    L = NCX
    g = np.arange(L)
    a = np.zeros((4, L), np.float32)
    for wi, w in enumerate((2, 4, 8, 16)):
        lo = np.clip(g - w // 2, 0, L - 1); hi = np.clip(g + (w - 1 - w // 2), 0, L - 1)
        a[wi] = 1.0 / (hi - lo + 1)
    return a


def _ext_xT(xb, rows):
    out = np.zeros((len(rows) * 64, D), np.float32)
    for b, (g, nat, used) in enumerate(rows):
        if used:
            out[b * 64:(b + 1) * 64] = xb[g * 64:(g + 1) * 64]
    return np.ascontiguousarray(out.T)


def _layer_shared(l, inp, tag):
    vec = np.zeros((128, NVEC), np.float32)
    vec[:, V_N1:V_N1 + 32] = _chunked(inp["norm1_g"][l]); vec[:, V_N2:V_N2 + 32] = _chunked(inp["norm2_g"][l])
    vec[:, V_PS:V_PS + 8] = _chunked(inp["pool_scale"][l]); vec[:, V_CB:V_CB + 8] = _chunked(inp["conv_dw_b"][l])
    vec[:, V_CG:V_CG + 8] = _chunked(inp["conv_norm_g"][l])
    vec[:, V_QG] = inp["q_norm_g"][l]; vec[:, V_KG] = inp["k_norm_g"][l]
    cw = inp["conv_dw_w"][l]
    vec[:, V_CW:V_CW + 248] = cw.T.reshape(8, 128, 31).transpose(1, 0, 2).reshape(128, 248)
    return {
        "vecs" + tag: vec, "bada" + tag: np.ascontiguousarray(np.broadcast_to(inp["b_ada"][l][None], (2, 6 * D))),
        "w_ada" + tag: inp["w_ada"][l], "w_in" + tag: inp["w_in"][l], "w_out" + tag: inp["w_out"][l],
        "w1" + tag: inp["w_mlp1"][l], "w2" + tag: inp["w_mlp2"][l],
        "pool_w" + tag: np.ascontiguousarray(inp["pool_w"][l].reshape(1024, 256)), "pw_w" + tag: inp["conv_pw_w"][l],
    }


def _fused_inputs(inp):
    x = inp["x"]; xc = inp["ctx"]
    shared = {"ident": np.eye(128, dtype=np.float32), "auxc": _auxc()}
    shared.update(_layer_shared(0, inp, "_0"))
    shared.update(_layer_shared(1, inp, "_1"))
    per_r = []
    for r in range(4):
        rows0 = _rowmap(r, 32, -8)
        rows1 = _rowmap(r, 24, -4)
        ms = np.zeros((128, 4), np.float32)
        ms[:, 0] = 0.0 if r == 0 else 1.0; ms[:, 1] = 1.0 if r == 0 else 0.0
        ms[:, 2] = 0.0 if r == 3 else 1.0; ms[:, 3] = 1.0 if r == 3 else 0.0
        per_r.append({
            "rows0": rows0,
            "aux_0": _aux(rows0, r, -8), "aux_1": _aux(rows1, r, -4),
            "bias_0": _bias_tables(inp["rpb"][0], rows0, NOWN0 // 64), "bias_1": _bias_tables(inp["rpb"][1], rows1, NOWN // 64),
            "msel": ms,
        })
    maps = []
    for core in range(8):
        b, r = core // 4, core % 4
        cvv = np.stack([inp["c"][b], inp["c_ctx"]], axis=-1)
        m = dict(shared)
        pr = per_r[r]
        m["xT"] = _ext_xT(x[b], pr["rows0"])
        m["xcT"] = np.ascontiguousarray(xc[b].T)
        m["cv"] = np.ascontiguousarray(cvv.reshape(KC, 128, 2).transpose(1, 0, 2))
        for k in ("aux_0", "aux_1", "bias_0", "bias_1", "msel"):
            m[k] = pr[k]
        maps.append(m)
    return maps


_NC_CACHE = {}


def kernel(**inp):
    inp = {k: np.asarray(v) for k, v in inp.items()}
    inp["x"] = inp["x"].astype(np.float32, copy=False)
    inp["ctx"] = inp["ctx"].astype(np.float32, copy=False)
    if "nc" not in _NC_CACHE:
        _NC_CACHE["nc"] = build_fused()
    nc = _NC_CACHE["nc"]
    maps = _fused_inputs(inp)
    res = run_bass_kernel_spmd(nc, maps, core_ids=list(range(8)))
    out = np.empty_like(inp["x"])
    for core in range(8):
        b, r = core // 4, core % 4
        out[b, r * 1024:(r + 1) * 1024] = res.results[core]["Xw"].reshape(D, NOWN).T
    return out
```

```python
import numpy as np
from contextlib import ExitStack
import concourse.bass as bass
import concourse.mybir as mybir
from concourse.bass_utils import run_bass_kernel_spmd

F32 = mybir.dt.float32
BF16 = mybir.dt.bfloat16
AF = mybir.ActivationFunctionType
ALU = mybir.AluOpType

D = 4096
KC = 32
NE = 1536
NCX = 256
NA = NE + NCX
OWN0 = 256
NOWN = 1024
NE0 = 2048
NOWN0 = 1536
INW = 9216
DFF = 16384
NEG = -30000.0
EPS = 1e-6
PAD = 16
EO = PAD
CO = PAD + NE + 2 * PAD
NPADDED = CO + NCX + PAD
NVEC = 32 + 32 + 8 + 8 + 8 + 1 + 1 + 8 * 31
V_N1, V_N2, V_PS, V_CB, V_CG, V_QG, V_KG, V_CW = 0, 32, 64, 72, 80, 88, 89, 90
AGRP = [(0, 512), (512, 512), (1024, 512), (1536, 256)]


class _Rec:
    def __init__(self):
        self.call = None

    def __getattr__(self, name):
        def f(*a, **k):
            self.call = (name, a, k)
        return f


class Prog:
    def __init__(self, nc):
        self.nc = nc
        self.engs = ["pe", "act", "dve", "pool", "sp"]
        self.rec = {e: [] for e in self.engs}
        self.cnt = {e: 0 for e in self.engs}
        self.seen = {e: {} for e in self.engs}
        self.lastw = {}
        self.readers = {}
        self.dsems = {}
        self.semobj = {}
        for e in self.engs:
            self.semobj[f"S_{e}"] = nc.alloc_semaphore(name=f"S_{e}")

    def dma_sem(self, name):
        if name not in self.dsems:
            h = self.nc.alloc_semaphore(name=name)
            self.dsems[name] = [h, 0]
            self.semobj[name] = h
        return name

    def _need(self, eng, tok, waits):
        if tok is None:
            return
        sname, val = tok
        if self.seen[eng].get(sname, 0) >= val:
            return
        waits[sname] = max(waits.get(sname, 0), val)

    def op(self, eng, fn, reads=(), writes=(), dsem=None, track=True, dinc=16):
        waits = {}
        for k in reads:
            self._need(eng, self.lastw.get(k), waits)
        for k in writes:
            self._need(eng, self.lastw.get(k), waits)
            for t in self.readers.get(k, ()):
                self._need(eng, t, waits)
        if eng == "pe":
            waits.pop("S_pe", None)
        for s, v in waits.items():
            self.seen[eng][s] = max(self.seen[eng].get(s, 0), v)
        if dsem is not None:
            self.dma_sem(dsem)
            d = self.dsems[dsem]
            d[1] += dinc
            tok = (dsem, d[1])
            inc = (dsem, dinc)
        elif track:
            self.cnt[eng] += 1
            tok = (f"S_{eng}", self.cnt[eng])
            inc = (f"S_{eng}", 1)
        else:
            tok = None
            inc = None
        r = _Rec()
        fn(r)
        self.rec[eng].append((list(waits.items()), r.call, inc))
        if tok is not None:
            for k in writes:
                self.lastw[k] = tok
                self.readers[k] = []
            for k in reads:
                self.readers.setdefault(k, []).append(tok)
        return tok

    def barrier(self):
        latest = {}
        for e in self.engs:
            if self.cnt[e] > 0:
                latest[f"S_{e}"] = self.cnt[e]
        for name, (h, v) in self.dsems.items():
            if v > 0:
                latest[name] = v
        for e in self.engs:
            waits = {}
            for sname, v in latest.items():
                self._need(e, (sname, v), waits)
            for sname, v in waits.items():
                self.seen[e][sname] = v
            self.rec[e].append((list(waits.items()), None, None))

    def final_wait(self, eng, toks):
        waits = {}
        for t in toks:
            self._need(eng, t, waits)
        self.rec[eng].append((list(waits.items()), None, None))

    def emit(self, block):
        m = {"pe": block.tensor, "act": block.scalar, "dve": block.vector,
             "pool": block.gpsimd, "sp": block.sync}
        semobj = self.semobj
        for e in self.engs:
            rec = self.rec[e]

            def body(engine, rec=rec):
                for waits, fn, inc in rec:
                    for s, v in waits:
                        engine.wait_ge(semobj[s], v)
                    if fn is not None:
                        ins = getattr(engine, fn[0])(*fn[1], **fn[2])
                        if inc is not None:
                            ins.then_inc(semobj[inc[0]], inc[1])
            m[e](body)


def emit_layer(nc, P, ps, cmn, L, outs):
    tag = L["tag"]; with_ctx_out = L["ctx_out"]
    NE = L["NE"]; NOWN = L["NOWN"]; OWN0 = 256
    NA = NE + NCX
    NB = NOWN + (NCX if with_ctx_out else 0)
    NQ = NOWN // 64
    NTT = NA // 128
    E0 = OWN0 - 16; EN = NOWN + 32
    EO = PAD; CO = EO + EN + 2 * PAD; NPADDED = CO + NCX + PAD
    WB = L["WB"]; WC = WB // 128

    def groups(n):
        g = []
        o = 0
        while o < n:
            g.append((o, min(512, n - o)))
            o += 512
        return g
    AGRP = groups(NA)
    BGRP = groups(NB)
    NAG = len(AGRP)

    def din(name, shape, dt=F32):
        return nc.dram_tensor(name + tag, shape, dt, kind="ExternalInput").ap()

    def dscr(name, shape, dt):
        return nc.dram_tensor(name + tag, shape, dt).ap()

    cv = cmn["cv"]
    aux = din("aux", [5, NE]); auxc = cmn["auxc"]
    biasd = din("bias", [16, 128, NQ * 320])
    vecs_d = din("vecs", [128, NVEC]); bada = din("bada", [2, 6 * D])
    w_ada = din("w_ada", [D, 6 * D]); w_in = din("w_in", [D, INW]); w_out = din("w_out", [D, D])
    w1 = din("w1", [D, DFF]); w2 = din("w2", [DFF, D])
    pool_w = din("pool_w", [1024, 256]); pw_w = din("pw_w", [1024, 1024])
    Xw = L["Xw"]
    Upc = dscr("Upc", [24, 128, NA], F32)
    Qd = dscr("Qd", [16, 128, NA], BF16); Kd = dscr("Kd", [16, 128, NA], BF16)
    Vd = dscr("Vd", [NTT, 128, 2048], BF16)
    Yd = dscr("Yd", [KC, 128, NB], BF16)
    H2d = dscr("H2d", [KC, 128, NB], BF16)
    ident = cmn["ident"]; ones = cmn["ones"]; epsb = cmn["epsb"]
    dbg = False

    class Rot:
        def __init__(self, banks):
            self.b = list(banks); self.i = 0

        def next(self):
            v = self.b[self.i % len(self.b)]; self.i += 1
            return v

    with ExitStack() as top:
        sb = lambda name, shape, dt, es=top: es.enter_context(nc.sbuf_tensor("sb_" + tag + name, shape, dt))
        vecs = sb("vecs", [128, NVEC], F32)
        mod = sb("mod", [128, 192, 2], F32)
        A1 = sb("A1", [128, KC, 2], F32); A2 = sb("A2", [128, KC, 2], F32)
        P.op("sp", lambda e: e.dma_start(out=vecs[:], in_=vecs_d), writes=["vecs"], dsem="d_misc")

        rr = {"evac": 0}

        def evac_eng():
            rr["evac"] += 1
            return "dve" if rr["evac"] % 2 == 0 else "act"

        def copy_op(eng, out, in_, reads, writes):
            if eng == "act":
                return P.op("act", lambda e: e.activation(out=out, in_=in_, func=AF.Copy), reads=reads, writes=writes)
            return P.op(eng, lambda e: e.tensor_copy(out=out, in_=in_), reads=reads, writes=writes)

        with ExitStack() as es:
            NWA = 5
            wb = [sb(f"wa{i}", [128, KC, 512], BF16, es) for i in range(NWA)]
            cv32 = sb("cv32", [128, KC, 2], F32, es); sil = sb("sil", [128, KC, 2], BF16, es)
            mrow = [sb(f"mrow{i}", [2, 512], F32, es) for i in range(2)]
            brow8 = [sb(f"brow{i}", [2, 4096], F32, es) for i in range(2)]
            P.op("sp", lambda e: e.dma_start(out=cv32[:], in_=cv), writes=["cv32"], dsem="d_cv")
            P.op("act", lambda e: e.activation(out=sil[:], in_=cv32[:], func=AF.Silu), reads=["cv32"], writes=["sil"])
            wsrc = w_ada.rearrange("(k p) n -> p k n", p=128)
            for b in range(48):
                s = b % NWA
                P.op("pool", lambda e, b=b, s=s: e.dma_start(out=wb[s][:], in_=wsrc[:, :, b * 512:(b + 1) * 512]),
                     writes=[("wa", s)], dsem=f"d_wa{s}")
                pi = b % 2
                bi = (b // 8) % 2
                if b % 8 == 0:
                    P.op("sp", lambda e, b=b, bi=bi: e.dma_start(out=brow8[bi][:], in_=bada[:, b * 512:b * 512 + 4096]),
                         writes=[("brow", bi)], dsem=f"d_brow{bi}")
                for k in range(KC):
                    P.op("pe", lambda e, k=k, s=s, pi=pi: e.matmul(ps[pi][0:2, :], lhsT=sil[:, k, :], rhs=wb[s][:, k, :],
                                                                   start=(k == 0), stop=(k == KC - 1)),
                         reads=[("wa", s), "sil"], writes=[("ps", pi)], track=(k == KC - 1))
                P.op("dve", lambda e, b=b, pi=pi, bi=bi: e.tensor_tensor(out=mrow[pi][:], in0=ps[pi][0:2, :],
                                                                      in1=brow8[bi][:, (b % 8) * 512:(b % 8 + 1) * 512], op=ALU.add),
                     reads=[("ps", pi), ("brow", bi)], writes=[("mrow", pi)])
                tb = 2 + (b // 24)
                for jj in range(4):
                    j = b * 4 + jj
                    P.op("pe", lambda e, j=j, jj=jj, pi=pi, tb=tb: e.transpose(out=ps[tb][:, (j % 96) * 2:(j % 96) * 2 + 2],
                                                                         in_=mrow[pi][0:2, jj * 128:(jj + 1) * 128], identity=ident[0:2, 0:2]),
                         reads=[("mrow", pi), "ident"], writes=[("ps", tb)], track=(jj == 3))
            for hh in range(2):
                P.op("dve", lambda e, hh=hh: e.tensor_copy(out=mod[:, hh * 96:(hh + 1) * 96, :].rearrange("p a b -> p (a b)"),
                                                          in_=ps[2 + hh][:, 0:192]),
                     reads=[("ps", 2 + hh)], writes=["mod"])
            for which in range(2):
                P.op("dve", lambda e, which=which: e.scalar_tensor_tensor(out=A1[:, :, which], in0=mod[:, 32:64, which], scalar=1.0,
                                                                         in1=vecs[:, V_N1:V_N1 + 32], op0=ALU.add, op1=ALU.mult),
                     reads=["mod", "vecs"], writes=["A1"])
                P.op("dve", lambda e, which=which: e.scalar_tensor_tensor(out=A2[:, :, which], in0=mod[:, 128:160, which], scalar=1.0,
                                                                         in1=vecs[:, V_N2:V_N2 + 32], op0=ALU.add, op1=ALU.mult),
                     reads=["mod", "vecs"], writes=["A2"])

        P.barrier()
        def modv(m, fc, which):
            return mod[:, m * 32 + fc, which:which + 1]

        with ExitStack() as es:
            hT = sb("hT", [128, KC, NA], BF16, es)
            with ExitStack() as es2:
                xc = [sb(f"xc{i}", [128, NA], F32, es2) for i in range(2)]
                sq = [sb(f"sq{i}", [128, NA], BF16, es2) for i in range(2)]
                tmp = [sb(f"tmp{i}", [128, NA], F32, es2) for i in range(1)] * 2
                rstd = sb("rstd", [128, NA], F32, es2)
                xsrc = L["xsrc"]; xcsrc = L["xcsrc"]; sel = L.get("sel")
                if sel is not None:
                    alt = [sb(f"alt{i}", [128, 448], F32, es2) for i in range(2)]
                    msel = sb("msel", [128, 4], F32, es2)
                    P.op("sp", lambda e: e.dma_start(out=msel[:], in_=sel["msel"]), writes=["msel"], dsem="d_msel")

                def load_x(fc):
                    s = fc % 2
                    P.op("sp", lambda e: e.dma_start(out=xc[s][:, 0:NE], in_=xsrc[fc]), writes=[("xc", s, 0)], dsem=f"d_xa{s}")
                    P.op("sp", lambda e: e.dma_start(out=xc[s][:, NE:NA], in_=xcsrc[fc]), writes=[("xc", s, 1)], dsem=f"d_xb{s}")
                    if sel is not None:
                        P.op("sp", lambda e: e.dma_start(out=alt[s][:, 0:256], in_=xsrc[fc, :, 512:768]), writes=[("alt", s, 0)], dsem=f"d_alta{s}")
                        P.op("sp", lambda e: e.dma_start(out=alt[s][:, 256:448], in_=xsrc[fc, :, 768:960]), writes=[("alt", s, 1)], dsem=f"d_altb{s}")
                        for (x0, xn, a0, m0) in ((0, 256, 0, 0), (1280, 192, 256, 2)):
                            P.op("dve", lambda e, x0=x0, xn=xn, m0=m0: e.tensor_scalar(out=xc[s][:, x0:x0 + xn], in0=xc[s][:, x0:x0 + xn],
                                                                                 scalar1=msel[:, m0:m0 + 1], scalar2=None, op0=ALU.mult),
                                 reads=[("xc", s, 0), "msel"], writes=[("xc", s, 0)])
                            P.op("dve", lambda e, x0=x0, xn=xn, a0=a0, m0=m0: e.scalar_tensor_tensor(
                                out=xc[s][:, x0:x0 + xn], in0=alt[s][:, a0:a0 + xn], scalar=msel[:, m0 + 1:m0 + 2], in1=xc[s][:, x0:x0 + xn],
                                op0=ALU.mult, op1=ALU.add),
                                reads=[("xc", s, 0), ("alt", s, 0), ("alt", s, 1), "msel"], writes=[("xc", s, 0)])
                    return s
                for fc in range(KC):
                    s = load_x(fc)
                    P.op("act", lambda e, s=s: e.activation(out=sq[s][:], in_=xc[s][:], func=AF.Square),
                         reads=[("xc", s, 0), ("xc", s, 1)], writes=[("sq", s)])
                    for gi, (g0, gn) in enumerate(AGRP):
                        P.op("pe", lambda e, s=s, gi=gi, g0=g0, gn=gn, fc=fc: e.matmul(ps[gi][:, 0:gn], lhsT=ones[:], rhs=sq[s][:, g0:g0 + gn],
                                                                                  start=(fc == 0), stop=(fc == KC - 1)),
                             reads=[("sq", s), "ones"], writes=[("ps", gi)], track=(gi == NAG - 1))
                for gi, (g0, gn) in enumerate(AGRP):
                    P.op("act", lambda e, gi=gi, g0=g0, gn=gn: e.activation(out=rstd[:, g0:g0 + gn], in_=ps[gi][:, 0:gn], func=AF.Sqrt,
                                                                         bias=epsb[:], scale=1.0 / D),
                         reads=[("ps", gi), "epsb"], writes=[("rstd", gi)])
                    P.op("dve", lambda e, g0=g0, gn=gn: e.reciprocal(out=rstd[:, g0:g0 + gn], in_=rstd[:, g0:g0 + gn]),
                         reads=[("rstd", gi)], writes=[("rstd", gi)])
                for fc in range(KC):
                    s = load_x(fc)
                    P.op("dve", lambda e, s=s: e.tensor_tensor(out=tmp[s][:], in0=xc[s][:], in1=rstd[:], op=ALU.mult),
                         reads=[("xc", s, 0), ("xc", s, 1)] + [("rstd", g) for g in range(NAG)], writes=[("tmp", 0)])
                    P.op("act", lambda e, s=s, fc=fc: e.activation(out=hT[:, fc, 0:NE], in_=tmp[s][:, 0:NE], func=AF.Identity,
                                                                 bias=modv(0, fc, 0), scale=A1[:, fc, 0:1]),
                         reads=[("tmp", 0), "mod", "A1"], writes=[("hT", fc, 0)])
                    P.op("act", lambda e, s=s, fc=fc: e.activation(out=hT[:, fc, NE:NA], in_=tmp[s][:, NE:NA], func=AF.Identity,
                                                                 bias=modv(0, fc, 1), scale=A1[:, fc, 1:2]),
                         reads=[("tmp", 0), "mod", "A1"], writes=[("hT", fc, 1)])
            P.barrier()
            hT_keys = [("hT", fc, w) for fc in range(KC) for w in range(2)]
            with ExitStack() as es2:
                wb = [sb(f"wi{i}", [128, KC, WB], BF16, es2) for i in range(2)]
                stg = [sb(f"stg{i}", [128, 512], F32, es2) for i in range(3)]
                stb = [sb(f"stb{i}", [128, 512], BF16, es2) for i in range(3)]
                sqb = [sb(f"sqb{i}", [128, 512], BF16, es2) for i in range(2)]
                rs = [sb(f"rs{i}", [128, 512], F32, es2) for i in range(2)]
                vst = [sb(f"vst{i}", [128, WB], BF16, es2) for i in range(2)]
                wsrc = w_in.rearrange("(k p) n -> p k n", p=128)
                cnt = {"stg": 0, "stb": 0, "g": 0, "v": 0}
                rot = Rot(range(6)); rots = Rot([6, 7])
                for cb in range(INW // WB):
                    s = cb % 2
                    P.op("pool", lambda e, cb=cb, s=s: e.dma_start(out=wb[s][:], in_=wsrc[:, :, cb * WB:(cb + 1) * WB]),
                         writes=[("wi", s)], dsem=f"d_wi{s}")
                    if cb * WB < 7168:
                        for c4 in range(WC):
                            chunk = cb * WC + c4
                            for gi, (g0, gn) in enumerate(AGRP):
                                if chunk < 24:
                                    si = cnt["stg"] % 3; cnt["stg"] += 1
                                else:
                                    si = cnt["stb"] % 3; cnt["stb"] += 1
                                pb = rot.next()
                                for k in range(KC):
                                    P.op("pe", lambda e, k=k, s=s, c4=c4, pb=pb, g0=g0, gn=gn: e.matmul(
                                        ps[pb][:, 0:gn], lhsT=wb[s][:, k, c4 * 128:(c4 + 1) * 128], rhs=hT[:, k, g0:g0 + gn],
                                        start=(k == 0), stop=(k == KC - 1)),
                                        reads=[("wi", s)] + (hT_keys if k in (0, KC - 1) else []), writes=[("ps", pb)], track=(k == KC - 1))
                                if chunk < 24:
                                    copy_op(evac_eng(), stg[si][:, 0:gn], ps[pb][:, 0:gn], [("ps", pb)], [("stg", si)])
                                    P.op("sp", lambda e, si=si, chunk=chunk, g0=g0, gn=gn: e.dma_start(out=Upc[chunk, :, g0:g0 + gn], in_=stg[si][:, 0:gn]),
                                         reads=[("stg", si)], writes=[("Upc", chunk, gi)], dsem=f"d_stg{si}")
                                else:
                                    gcol = V_QG if chunk < 40 else V_KG
                                    qi = cnt["g"] % 2; cnt["g"] += 1
                                    P.op("act", lambda e, pb=pb, gn=gn, qi=qi: e.activation(out=sqb[qi][:, 0:gn], in_=ps[pb][:, 0:gn], func=AF.Square),
                                         reads=[("ps", pb)], writes=[("sqb", qi)])
                                    pj = rots.next()
                                    P.op("pe", lambda e, gn=gn, qi=qi, pj=pj: e.matmul(ps[pj][:, 0:gn], lhsT=ones[:], rhs=sqb[qi][:, 0:gn], start=True, stop=True),
                                         reads=[("sqb", qi), "ones"], writes=[("ps", pj)])
                                    P.op("act", lambda e, gn=gn, qi=qi, pj=pj: e.activation(out=rs[qi][:, 0:gn], in_=ps[pj][:, 0:gn], func=AF.Sqrt,
                                                                                          bias=epsb[:], scale=1.0 / 128),
                                         reads=[("ps", pj), "epsb"], writes=[("rs", qi)])
                                    P.op("dve", lambda e, gn=gn, qi=qi: e.reciprocal(out=rs[qi][:, 0:gn], in_=rs[qi][:, 0:gn]),
                                         reads=[("rs", qi)], writes=[("rs", qi)])
                                    P.op("dve", lambda e, pb=pb, g0=g0, gn=gn, qi=qi, si=si, gcol=gcol: e.scalar_tensor_tensor(
                                        out=stb[si][:, 0:gn], in0=ps[pb][:, 0:gn], scalar=vecs[:, gcol:gcol + 1], in1=rs[qi][:, 0:gn],
                                        op0=ALU.mult, op1=ALU.mult),
                                        reads=[("ps", pb), ("rs", qi), "vecs"], writes=[("stb", si)])
                                    dst = Qd[chunk - 24, :, g0:g0 + gn] if chunk < 40 else Kd[chunk - 40, :, g0:g0 + gn]
                                    key = ("Qd", chunk - 24, gi) if chunk < 40 else ("Kd", chunk - 40, gi)
                                    P.op("sp", lambda e, si=si, dst=dst, gn=gn: e.dma_start(out=dst, in_=stb[si][:, 0:gn]),
                                         reads=[("stb", si)], writes=[key], dsem=f"d_stb{si}")
                    else:
                        vb = cb - 7168 // WB
                        for tt in range(NTT):
                            pi = rot.next()
                            for k in range(KC):
                                P.op("pe", lambda e, k=k, s=s, tt=tt, pi=pi: e.matmul(ps[pi][:, 0:WB], lhsT=hT[:, k, tt * 128:(tt + 1) * 128], rhs=wb[s][:, k, :],
                                                                                    start=(k == 0), stop=(k == KC - 1)),
                                     reads=[("wi", s)] + (hT_keys if k in (0, KC - 1) else []), writes=[("ps", pi)], track=(k == KC - 1))
                            vi = cnt["v"] % 2; cnt["v"] += 1
                            copy_op(evac_eng(), vst[vi][:], ps[pi][:, 0:WB], [("ps", pi)], [("vst", vi)])
                            P.op("sp", lambda e, vi=vi, tt=tt, vb=vb: e.dma_start(out=Vd[tt, :, vb * WB:(vb + 1) * WB], in_=vst[vi][:]),
                                 reads=[("vst", vi)], writes=[("Vd", tt, vb * WB // 512)], dsem=f"d_vst{vi}")

        P.barrier()
        with ExitStack() as es:
            WINS = (2, 4, 8, 16)
            with ExitStack() as es2:
                up = [sb(f"up{i}", [128, NPADDED], F32, es2) for i in range(2)]
                sa = sb("sa", [128, NPADDED], F32, es2); sbb = sb("sbb", [128, NPADDED], F32, es2)
                valid = sb("valid", [128, EN], F32, es2)
                icnt = sb("icnt", [128, 4, NB], F32, es2)
                pooled = sb("pooled", [128, 2, NB], BF16, es2)
                pw = sb("pw", [128, 8, 256], BF16, es2)
                yst = [sb(f"yst{i}", [128, NB], BF16, es2) for i in range(2)]
                P.op("sp", lambda e: e.dma_start(out=valid[:], in_=aux[0, E0:E0 + EN].partition_broadcast(128)), writes=["valid"], dsem="d_valid")
                for wi in range(4):
                    P.op("sp", lambda e, wi=wi: e.dma_start(out=icnt[:, wi, 0:NOWN], in_=aux[1 + wi, OWN0:OWN0 + NOWN].partition_broadcast(128)),
                         writes=[("icnt", wi, 0)], dsem=f"d_icnt{wi}")
                    if with_ctx_out:
                        P.op("sp", lambda e, wi=wi: e.dma_start(out=icnt[:, wi, NOWN:NB], in_=auxc[wi].partition_broadcast(128)),
                             writes=[("icnt", wi, 1)], dsem=f"d_icnt{wi}")
                P.op("pool", lambda e: e.dma_start(out=pw[:], in_=pool_w.rearrange("(k p) n -> p k n", p=128)), writes=["pw"], dsem="d_pw")
                for i in range(2):
                    P.op("dve", lambda e, i=i: e.memset(up[i][:], 0.0), writes=[("up", i)])
                P.op("dve", lambda e: e.memset(sa[:], 0.0), writes=["sa"])
                P.op("dve", lambda e: e.memset(sbb[:], 0.0), writes=["sbb"])
                yc = 0
                for g in range(4):
                    for c2 in range(2):
                        chunk = 2 * g + c2
                        ui = chunk % 2
                        P.op("sp", lambda e, ui=ui, chunk=chunk: e.dma_start(out=up[ui][:, EO:EO + EN], in_=Upc[chunk, :, E0:E0 + EN]),
                             reads=[("Upc", chunk, g_) for g_ in range(NAG)], writes=[("up", ui)], dsem=f"d_up{ui}")
                        P.op("sp", lambda e, ui=ui, chunk=chunk: e.dma_start(out=up[ui][:, CO:CO + NCX], in_=Upc[chunk, :, NE:NA]),
                             reads=[("Upc", chunk, g_) for g_ in range(NAG)], writes=[("upc", ui)], dsem=f"d_upc{ui}")
                        P.op("dve", lambda e, ui=ui: e.tensor_tensor(out=up[ui][:, EO:EO + EN], in0=up[ui][:, EO:EO + EN], in1=valid[:], op=ALU.mult),
                             reads=[("up", ui), "valid"], writes=[("up", ui)])
                        lo, hi = EO - PAD + 8, NPADDED - 8
                        src = up[ui]
                        bufs = [sa, sbb]
                        P.op("dve", lambda e, src=src: e.tensor_tensor(out=sa[:, 8:NPADDED - 8], in0=src[:, 7:NPADDED - 9], in1=src[:, 8:NPADDED - 8], op=ALU.add),
                             reads=[("up", ui), ("upc", ui)], writes=["sa"])
                        cur, other = sa, sbb
                        curk, otherk = "sa", "sbb"
                        for lvl in range(g):
                            sh = 1 << lvl
                            P.op("dve", lambda e, cur=cur, other=other, sh=sh: e.tensor_tensor(
                                out=other[:, 8:NPADDED - 8], in0=cur[:, 8 - sh:NPADDED - 8 - sh], in1=cur[:, 8 + sh:NPADDED - 8 + sh], op=ALU.add),
                                reads=[curk], writes=[otherk])
                            cur, other = other, cur
                            curk, otherk = otherk, curk
                        segs = [(EO + 16, 0, NOWN)] + ([(CO, NOWN, NCX)] if with_ctx_out else [])
                        for (so, bo, n) in segs:
                            P.op("dve", lambda e, cur=cur, so=so, bo=bo, n=n, g=g: e.tensor_tensor(
                                out=cur[:, so:so + n], in0=cur[:, so:so + n], in1=icnt[:, g, bo:bo + n], op=ALU.mult),
                                reads=[curk] + [("icnt", g, 0), ("icnt", g, 1)], writes=[curk])
                            P.op("dve", lambda e, cur=cur, so=so, bo=bo, n=n, c2=c2, src=src: e.tensor_tensor(
                                out=pooled[:, c2, bo:bo + n], in0=cur[:, so:so + n], in1=src[:, so:so + n], op=ALU.subtract),
                                reads=[curk, ("up", ui), ("upc", ui)], writes=[("pooled", c2)])
                    for oc in range(2):
                        yi = yc % 2; yc += 1
                        for gi, (g0, gn) in enumerate(BGRP):
                            for k in range(2):
                                P.op("pe", lambda e, k=k, g=g, oc=oc, gi=gi, g0=g0, gn=gn: e.matmul(
                                    ps[gi][:, 0:gn], lhsT=pw[:, 2 * g + k, oc * 128:(oc + 1) * 128], rhs=pooled[:, k, g0:g0 + gn],
                                    start=(k == 0), stop=(k == 1)),
                                    reads=["pw", ("pooled", 0), ("pooled", 1)], writes=[("ps", gi)], track=(k == 1))
                            P.op("act", lambda e, gi=gi, g0=g0, gn=gn, yi=yi, g=g, oc=oc: e.activation(
                                out=yst[yi][:, g0:g0 + gn], in_=ps[gi][:, 0:gn], func=AF.Copy, scale=vecs[:, V_PS + 2 * g + oc:V_PS + 2 * g + oc + 1]),
                                reads=[("ps", gi), "vecs"], writes=[("yst", yi)])
                        P.op("sp", lambda e, yi=yi, g=g, oc=oc: e.dma_start(out=Yd[2 * g + oc], in_=yst[yi][:]),
                             reads=[("yst", yi)], writes=[("Yd", 2 * g + oc)], dsem=f"d_yst{yi}")

            P.barrier()
            with ExitStack() as es2:
                ua = [sb(f"ua{i}", [128, EN + NCX], F32, es2) for i in range(2)]
                ug = [sb(f"ug{i}", [128, EN + NCX], F32, es2) for i in range(2)]
                hp = [sb(f"hp{i}", [128, NPADDED], F32, es2) for i in range(2)]
                valid = sb("valid2", [128, EN], F32, es2)
                cvo = sb("cvo", [128, 8, NB], F32, es2)
                sqc = [sb(f"sqc{i}", [128, NB], BF16, es2) for i in range(2)]
                rsc = sb("rsc", [128, NB], F32, es2)
                zt = sb("zt", [128, 8, NB], BF16, es2)
                ztmp = [sb(f"ztmp{i}", [128, NB], F32, es2) for i in range(2)]
                pwc = sb("pwc", [128, 8, 1024], BF16, es2)
                yst = [sb(f"ystc{i}", [128, NB], BF16, es2) for i in range(2)]
                P.op("sp", lambda e: e.dma_start(out=valid[:], in_=aux[0, E0:E0 + EN].partition_broadcast(128)), writes=["valid2"], dsem="d_valid")
                P.op("pool", lambda e: e.dma_start(out=pwc[:], in_=pw_w.rearrange("(k p) n -> p k n", p=128)), writes=["pwc"], dsem="d_pwc")
                for i in range(2):
                    P.op("dve", lambda e, i=i: e.memset(hp[i][:], 0.0), writes=[("hp", i)])
                segs = [(EO + 16, 0, NOWN)] + ([(CO, NOWN, NCX)] if with_ctx_out else [])
                for i in range(8):
                    s = i % 2
                    P.op("sp", lambda e, s=s, i=i: e.dma_start(out=ua[s][:, 0:EN], in_=Upc[8 + i, :, E0:E0 + EN]), writes=[("ua", s)], dsem=f"d_ua{s}")
                    P.op("sp", lambda e, s=s, i=i: e.dma_start(out=ua[s][:, EN:EN + NCX], in_=Upc[8 + i, :, NE:NA]), writes=[("uac", s)], dsem=f"d_uac{s}")
                    P.op("sp", lambda e, s=s, i=i: e.dma_start(out=ug[s][:, 0:EN], in_=Upc[16 + i, :, E0:E0 + EN]), writes=[("ug", s)], dsem=f"d_ug{s}")
                    P.op("sp", lambda e, s=s, i=i: e.dma_start(out=ug[s][:, EN:EN + NCX], in_=Upc[16 + i, :, NE:NA]), writes=[("ugc", s)], dsem=f"d_ugc{s}")
                    P.op("act", lambda e, s=s: e.activation(out=ug[s][:], in_=ug[s][:], func=AF.Sigmoid), reads=[("ug", s), ("ugc", s)], writes=[("ug", s), ("ugc", s)])
                    P.op("dve", lambda e, s=s: e.tensor_tensor(out=hp[s][:, EO:EO + EN], in0=ua[s][:, 0:EN], in1=ug[s][:, 0:EN], op=ALU.mult),
                         reads=[("ua", s), ("ug", s)], writes=[("hp", s)])
                    P.op("dve", lambda e, s=s: e.tensor_tensor(out=hp[s][:, CO:CO + NCX], in0=ua[s][:, EN:EN + NCX], in1=ug[s][:, EN:EN + NCX], op=ALU.mult),
                         reads=[("uac", s), ("ugc", s), ("ug", s)], writes=[("hp", s)])
                    P.op("dve", lambda e, s=s: e.tensor_tensor(out=hp[s][:, EO:EO + EN], in0=hp[s][:, EO:EO + EN], in1=valid[:], op=ALU.mult),
                         reads=[("hp", s), "valid2"], writes=[("hp", s)])
                    for (so, bo, n) in segs:
                        for k in range(31):
                            wcol = V_CW + i * 31 + k
                            off = so + k - 15
                            if k == 0:
                                P.op("dve", lambda e, s=s, i=i, off=off, bo=bo, n=n, wcol=wcol: e.tensor_scalar(
                                    out=cvo[:, i, bo:bo + n], in0=hp[s][:, off:off + n], scalar1=vecs[:, wcol:wcol + 1],
                                    scalar2=vecs[:, V_CB + i:V_CB + i + 1], op0=ALU.mult, op1=ALU.add),
                                    reads=[("hp", s), "vecs"], writes=[("cvo", i, bo)])
                            else:
                                P.op("dve", lambda e, s=s, i=i, off=off, bo=bo, n=n, wcol=wcol: e.scalar_tensor_tensor(
                                    out=cvo[:, i, bo:bo + n], in0=hp[s][:, off:off + n], scalar=vecs[:, wcol:wcol + 1],
                                    in1=cvo[:, i, bo:bo + n], op0=ALU.mult, op1=ALU.add),
                                    reads=[("hp", s), "vecs"], writes=[("cvo", i, bo)])
                    qi = i % 2
                    P.op("act", lambda e, i=i, qi=qi: e.activation(out=sqc[qi][:], in_=cvo[:, i, :], func=AF.Square),
                         reads=[("cvo", i, b_) for (_, b_, _) in segs], writes=[("sqc", qi)])
                    for gi, (g0, gn) in enumerate(BGRP):
                        P.op("pe", lambda e, qi=qi, gi=gi, g0=g0, gn=gn, i=i: e.matmul(ps[4 + gi][:, 0:gn], lhsT=ones[:], rhs=sqc[qi][:, g0:g0 + gn],
                                                                                 start=(i == 0), stop=(i == 7)),
                             reads=[("sqc", qi), "ones"], writes=[("ps", 4 + gi)], track=(gi == len(BGRP) - 1))
                for gi, (g0, gn) in enumerate(BGRP):
                    P.op("act", lambda e, gi=gi, g0=g0, gn=gn: e.activation(out=rsc[:, g0:g0 + gn], in_=ps[4 + gi][:, 0:gn], func=AF.Sqrt,
                                                                         bias=epsb[:], scale=1.0 / 1024),
                         reads=[("ps", 4 + gi), "epsb"], writes=[("rsc", gi)])
                    P.op("dve", lambda e, g0=g0, gn=gn: e.reciprocal(out=rsc[:, g0:g0 + gn], in_=rsc[:, g0:g0 + gn]),
                         reads=[("rsc", gi)], writes=[("rsc", gi)])
                for i in range(8):
                    qi = i % 2
                    P.op("dve", lambda e, i=i, qi=qi: e.scalar_tensor_tensor(out=ztmp[qi][:], in0=cvo[:, i, :], scalar=vecs[:, V_CG + i:V_CG + i + 1],
                                                                         in1=rsc[:], op0=ALU.mult, op1=ALU.mult),
                         reads=[("cvo", i, b_) for (_, b_, _) in segs] + [("rsc", g) for g in range(len(BGRP))] + ["vecs"], writes=[("ztmp", qi)])
                    P.op("act", lambda e, i=i, qi=qi: e.activation(out=zt[:, i, :], in_=ztmp[qi][:], func=AF.Silu),
                         reads=[("ztmp", qi)], writes=[("zt", i)])
                yc = 0
                for oc in range(8):
                    yi = yc % 2; yc += 1
                    for gi, (g0, gn) in enumerate(BGRP):
                        for k in range(8):
                            P.op("pe", lambda e, k=k, oc=oc, gi=gi, g0=g0, gn=gn: e.matmul(
                                ps[gi][:, 0:gn], lhsT=pwc[:, k, oc * 128:(oc + 1) * 128], rhs=zt[:, k, g0:g0 + gn], start=(k == 0), stop=(k == 7)),
                                reads=["pwc"] + [("zt", kk) for kk in range(8)], writes=[("ps", gi)], track=(k == 7))
                        copy_op(evac_eng(), yst[yi][:, g0:g0 + gn], ps[gi][:, 0:gn], [("ps", gi)], [("ystc", yi)])
                    P.op("sp", lambda e, yi=yi, oc=oc: e.dma_start(out=Yd[8 + oc], in_=yst[yi][:]),
                         reads=[("ystc", yi)], writes=[("Yd", 8 + oc)], dsem=f"d_ystc{yi}")

            P.barrier()
            with ExitStack() as es2:
                qT = [sb(f"qT{i}", [128, NA], BF16, es2) for i in range(2)]
                kT = [sb(f"kT{i}", [128, NA], BF16, es2) for i in range(2)]
                Vh = [sb(f"Vh{i}", [128, NTT, 128], BF16, es2) for i in range(2)]
                bt = [sb(f"bt{i}", [128, NQ * 320], F32, es2) for i in range(2)]
                NSB = 3
                sbuf_s = [sb(f"ss{i}", [128, 320], F32, es2) for i in range(NSB)]
                pT = [sb(f"pT{i}", [128, 320], BF16, es2) for i in range(NSB)]
                pc = [sb(f"pc{i}", [128, 2, 512], BF16, es2) for i in range(2)]
                rden = [sb(f"rden{i}", [128, 512], F32, es2) for i in range(2)]
                yst = [sb(f"ysta{i}", [128, NB], BF16, es2) for i in range(2)]
                hook = L.get("attn_hook")
                if hook is not None:
                    hook_state = L["attn_hook_alloc"](sb, es2)
                SCALE = 128.0 ** -0.5
                rotS = Rot([0, 1, 2])
                st = {"cq": 0, "u": 0}
                SK = 2
                for h in range(16):
                    s = h % 2
                    P.op("sp", lambda e, s=s, h=h: e.dma_start(out=qT[s][:], in_=Qd[h]), reads=[("Qd", h, g_) for g_ in range(NAG)], writes=[("qT", s)], dsem=f"d_qT{s}")
                    P.op("sp", lambda e, s=s, h=h: e.dma_start(out=kT[s][:], in_=Kd[h]), reads=[("Kd", h, g_) for g_ in range(NAG)], writes=[("kT", s)], dsem=f"d_kT{s}")
                    P.op("sp", lambda e, s=s, h=h: e.dma_start(out=Vh[s][:], in_=Vd[:, :, h * 128:(h + 1) * 128].rearrange("t p d -> p t d")),
                         reads=[("Vd", tt, h // 4) for tt in range(NTT)], writes=[("Vh", s)], dsem=f"d_Vh{s}")
                    P.op("sp", lambda e, s=s, h=h: e.dma_start(out=bt[s][:], in_=biasd[h]), writes=[("bt", s)], dsem=f"d_bt{s}")
                    grp_ci = {}

                    def stageA(j, s=s):
                        u = st["u"] % NSB; st["u"] += 1
                        nt = 4 if j % 2 == 0 else 5
                        t0 = j // 2
                        sbk = rotS.next()
                        psS = ps[sbk]
                        qs = OWN0 + j * 64
                        for m in range(nt):
                            P.op("pe", lambda e, m=m: e.matmul(
                                psS[:, m * 64:(m + 1) * 64], lhsT=kT[s][:, (t0 + m) * 128:(t0 + m + 1) * 128], rhs=qT[s][:, qs:qs + 64],
                                start=True, stop=True),
                                reads=[("kT", s), ("qT", s)], writes=[("ps", sbk)], track=(m == nt - 1))
                        P.op("dve", lambda e: e.scalar_tensor_tensor(
                            out=sbuf_s[u][:, 0:nt * 64], in0=psS[:, 0:nt * 64], scalar=SCALE, in1=bt[s][:, j * 320:j * 320 + nt * 64],
                            op0=ALU.mult, op1=ALU.add),
                            reads=[("ps", sbk), ("bt", s)], writes=[("ss", u)])
                        P.op("act", lambda e: e.activation(out=pT[u][:, 0:nt * 64], in_=sbuf_s[u][:, 0:nt * 64], func=AF.Exp),
                             reads=[("ss", u)], writes=[("pT", u)])
                        return u

                    def stageC(grp, s=s):
                        ci = st["cq"] % 2; st["cq"] += 1
                        grp_ci[grp] = ci
                        pO, pD = 4 + 2 * ci, 5 + 2 * ci
                        q0 = OWN0 + grp * 512
                        for ct in range(2):
                            sbk = rotS.next()
                            P.op("pe", lambda e, ct=ct, sbk=sbk: e.matmul(ps[sbk][:, :], lhsT=kT[s][:, NE + ct * 128:NE + (ct + 1) * 128],
                                                                        rhs=qT[s][:, q0:q0 + 512], start=True, stop=True),
                                 reads=[("kT", s), ("qT", s)], writes=[("ps", sbk)])
                            P.op("act", lambda e, ct=ct, sbk=sbk: e.activation(out=pc[ci][:, ct, :], in_=ps[sbk][:, :], func=AF.Exp, scale=SCALE),
                                 reads=[("ps", sbk)], writes=[("pc", ci, ct)])
                        for ct in range(2):
                            P.op("pe", lambda e, ct=ct: e.matmul(ps[pO][:, :], lhsT=Vh[s][:, NE // 128 + ct, :], rhs=pc[ci][:, ct, :],
                                                               start=(ct == 0), stop=False, skip_group_check=True),
                                 reads=[("Vh", s), ("pc", ci, ct)], writes=[("ps", pO)])
                            P.op("pe", lambda e, ct=ct: e.matmul(ps[pD][:, :], lhsT=ones[:], rhs=pc[ci][:, ct, :],
                                                               start=(ct == 0), stop=False, skip_group_check=True),
                                 reads=["ones", ("pc", ci, ct)], writes=[("ps", pD)])

                    def stageB(j, u, s=s):
                        grp, jr = j // 8, j % 8
                        if jr == 0:
                            stageC(grp)
                        ci = grp_ci[grp]
                        pO, pD = 4 + 2 * ci, 5 + 2 * ci
                        nt = 4 if j % 2 == 0 else 5
                        t0 = j // 2
                        for m in range(nt):
                            last = (jr == 7 and m == nt - 1)
                            P.op("pe", lambda e, m=m, last=last: e.matmul(
                                ps[pO][:, jr * 64:(jr + 1) * 64], lhsT=Vh[s][:, t0 + m, :], rhs=pT[u][:, m * 64:(m + 1) * 64],
                                start=False, stop=last, skip_group_check=True),
                                reads=[("Vh", s), ("pT", u)], writes=[("ps", pO)], track=(m == nt - 1))
                            P.op("pe", lambda e, m=m, last=last: e.matmul(
                                ps[pD][:, jr * 64:(jr + 1) * 64], lhsT=ones[:], rhs=pT[u][:, m * 64:(m + 1) * 64],
                                start=False, stop=last, skip_group_check=True),
                                reads=["ones", ("pT", u)], writes=[("ps", pD)], track=(m == nt - 1))
                        if jr == 7:
                            P.op("dve", lambda e: e.reciprocal(out=rden[ci][:], in_=ps[pD][:, :]),
                                 reads=[("ps", pD)], writes=[("rden", ci)])
                            P.op("dve", lambda e: e.tensor_tensor(out=yst[s][:, grp * 512:(grp + 1) * 512], in0=ps[pO][:, :],
                                                                  in1=rden[ci][:], op=ALU.mult),
                                 reads=[("ps", pO), ("rden", ci)], writes=[("ysta", s, grp)])

                    pend = []
                    for j in range(NQ):
                        pend.append((j, stageA(j)))
                        if len(pend) > SK:
                            stageB(*pend.pop(0))
                        if hook is not None and j % 8 == 7:
                            hook(hook_state)
                    while pend:
                        stageB(*pend.pop(0))
                    if with_ctx_out:
                        ci = st["cq"] % 2; st["cq"] += 1
                        pO, pD = 4 + 2 * ci, 5 + 2 * ci
                        for ct in range(2):
                            sbk = rotS.next()
                            P.op("pe", lambda e, s=s, ct=ct, sbk=sbk: e.matmul(ps[sbk][:, 0:256], lhsT=kT[s][:, NE + ct * 128:NE + (ct + 1) * 128],
                                                                             rhs=qT[s][:, NE:NA], start=True, stop=True),
                                 reads=[("kT", s), ("qT", s)], writes=[("ps", sbk)])
                            P.op("act", lambda e, ct=ct, ci=ci, sbk=sbk: e.activation(out=pc[ci][:, ct, 0:256], in_=ps[sbk][:, 0:256], func=AF.Exp, scale=SCALE),
                                 reads=[("ps", sbk)], writes=[("pc", ci, ct)])
                        for ct in range(2):
                            P.op("pe", lambda e, s=s, ct=ct, ci=ci, pO=pO: e.matmul(ps[pO][:, 0:256], lhsT=Vh[s][:, NE // 128 + ct, :], rhs=pc[ci][:, ct, 0:256],
                                                                               start=(ct == 0), stop=(ct == 1)),
                                 reads=[("Vh", s), ("pc", ci, ct)], writes=[("ps", pO)])
                            P.op("pe", lambda e, ct=ct, ci=ci, pD=pD: e.matmul(ps[pD][:, 0:256], lhsT=ones[:], rhs=pc[ci][:, ct, 0:256],
                                                                          start=(ct == 0), stop=(ct == 1)),
                                 reads=["ones", ("pc", ci, ct)], writes=[("ps", pD)])
                        P.op("dve", lambda e, ci=ci, pD=pD: e.reciprocal(out=rden[ci][:, 0:256], in_=ps[pD][:, 0:256]),
                             reads=[("ps", pD)], writes=[("rden", ci)])
                        P.op("dve", lambda e, ci=ci, pO=pO, s=s: e.tensor_tensor(out=yst[s][:, NOWN:NB], in0=ps[pO][:, 0:256],
                                                                           in1=rden[ci][:, 0:256], op=ALU.mult),
                             reads=[("ps", pO), ("rden", ci)], writes=[("ysta", s, "c")])
                    P.op("sp", lambda e, s=s, h=h: e.dma_start(out=Yd[16 + h], in_=yst[s][:]),
                         reads=[("ysta", s, g) for g in range(NOWN // 512)] + [("ysta", s, "c")], writes=[("Yd", 16 + h)], dsem=f"d_ysta{s}")
                if hook is not None:
                    L["attn_hook_flush"](hook_state)

        P.barrier()
        with ExitStack() as es:
            rstd2 = sb("rstd2", [128, NB], F32, es)
            xsrc = L["xsrc"]; xcsrc = L["xcsrc"]
            NG = len(BGRP)
            with ExitStack() as es2:
                yT = sb("yT", [128, KC, NB], BF16, es2)
                wb = [sb(f"wo{i}", [128, KC, WB], BF16, es2) for i in range(2)]
                xo = [sb(f"xo{i}", [128, NB], F32, es2) for i in range(3)]
                xm = [sb(f"xm{i}", [128, NB], F32, es2) for i in range(2)]
                sq2 = [sb(f"sq2{i}", [128, NB], BF16, es2) for i in range(2)]
                for k in range(KC):
                    P.op("sp", lambda e, k=k: e.dma_start(out=yT[:, k, :], in_=Yd[k]), reads=[("Yd", k)], writes=[("yT", k)], dsem=f"d_yT{k % 4}")
                yT_keys = [("yT", k) for k in range(KC)]
                wsrc = w_out.rearrange("(k p) n -> p k n", p=128)
                rot = Rot(range(4))
                def load_xo(oc):
                    xi = oc % 3
                    P.op("sp", lambda e: e.dma_start(out=xo[xi][:, 0:NOWN], in_=xsrc[oc, :, OWN0:OWN0 + NOWN]),
                         writes=[("xo", xi, 0)], dsem=f"d_xo{xi}")
                    if with_ctx_out:
                        P.op("sp", lambda e: e.dma_start(out=xo[xi][:, NOWN:NB], in_=xcsrc[oc]),
                             writes=[("xo", xi, 1)], dsem=f"d_xoc{xi}")
                load_xo(0); load_xo(1)
                for cb in range(D // WB):
                    s = cb % 2
                    P.op("pool", lambda e, cb=cb, s=s: e.dma_start(out=wb[s][:], in_=wsrc[:, :, cb * WB:(cb + 1) * WB]),
                         writes=[("wo", s)], dsem=f"d_wo{s}")
                    for c4 in range(WC):
                        oc = cb * WC + c4
                        xi = oc % 3
                        if oc + 2 < KC:
                            load_xo(oc + 2)
                        xmi = oc % 2
                        for gi, (g0, gn) in enumerate(BGRP):
                            pb = rot.next()
                            for k in range(KC):
                                P.op("pe", lambda e, k=k, s=s, c4=c4, pb=pb, g0=g0, gn=gn: e.matmul(
                                    ps[pb][:, 0:gn], lhsT=wb[s][:, k, c4 * 128:(c4 + 1) * 128], rhs=yT[:, k, g0:g0 + gn],
                                    start=(k == 0), stop=(k == KC - 1)),
                                    reads=[("wo", s)] + (yT_keys if k in (0, KC - 1) else []), writes=[("ps", pb)], track=(k == KC - 1))
                            which = 0 if g0 < NOWN else 1
                            P.op("dve", lambda e, pb=pb, g0=g0, gn=gn, xi=xi, xmi=xmi, oc=oc, which=which: e.scalar_tensor_tensor(
                                out=xm[xmi][:, g0:g0 + gn], in0=ps[pb][:, 0:gn], scalar=modv(2, oc, which), in1=xo[xi][:, g0:g0 + gn],
                                op0=ALU.mult, op1=ALU.add),
                                reads=[("ps", pb), ("xo", xi, 0), ("xo", xi, 1), "mod"], writes=[("xm", xmi, gi)])
                        P.op("sp", lambda e, xmi=xmi, oc=oc: e.dma_start(out=Xw[oc], in_=xm[xmi][:]),
                             reads=[("xm", xmi, g) for g in range(NG)], writes=[("Xw", oc)], dsem=f"d_xm{xmi}")
                        P.op("act", lambda e, xmi=xmi: e.activation(out=sq2[xmi][:], in_=xm[xmi][:], func=AF.Square),
                             reads=[("xm", xmi, g) for g in range(NG)], writes=[("sq2", xmi)])
                        for gi, (g0, gn) in enumerate(BGRP):
                            P.op("pe", lambda e, xmi=xmi, gi=gi, g0=g0, gn=gn, oc=oc: e.matmul(ps[4 + gi][:, 0:gn], lhsT=ones[:], rhs=sq2[xmi][:, g0:g0 + gn],
                                                                                      start=(oc == 0), stop=(oc == KC - 1), skip_group_check=True),
                                 reads=[("sq2", xmi), "ones"], writes=[("ps", 4 + gi)], track=(gi == NG - 1))
                for gi, (g0, gn) in enumerate(BGRP):
                    P.op("act", lambda e, gi=gi, g0=g0, gn=gn: e.activation(out=rstd2[:, g0:g0 + gn], in_=ps[4 + gi][:, 0:gn], func=AF.Sqrt,
                                                                         bias=epsb[:], scale=1.0 / D),
                         reads=[("ps", 4 + gi), "epsb"], writes=[("rstd2", gi)])
                    P.op("dve", lambda e, g0=g0, gn=gn: e.reciprocal(out=rstd2[:, g0:g0 + gn], in_=rstd2[:, g0:g0 + gn]),
                         reads=[("rstd2", gi)], writes=[("rstd2", gi)])
            P.barrier()
            with ExitStack() as es2:
                xo = [sb(f"xo_{i}", [128, NB], F32, es2) for i in range(2)]
                xm = [sb(f"xm_{i}", [128, NB], F32, es2) for i in range(2)]
                hst = [sb(f"hst{i}", [128, NB], BF16, es2) for i in range(2)]
                for fc in range(KC):
                    xi = fc % 2
                    P.op("sp", lambda e, xi=xi, fc=fc: e.dma_start(out=xo[xi][:], in_=Xw[fc]), reads=[("Xw", fc)],
                         writes=[("xo", xi, 0), ("xo", xi, 1)], dsem=f"d_xo{xi}")
                    P.op("dve", lambda e, xi=xi: e.tensor_tensor(out=xm[xi][:], in0=xo[xi][:], in1=rstd2[:], op=ALU.mult),
                         reads=[("xo", xi, 0), ("xo", xi, 1)] + [("rstd2", g) for g in range(NG)], writes=[("xm", xi, g) for g in range(NG)])
                    P.op("act", lambda e, xi=xi, fc=fc: e.activation(out=hst[xi][:, 0:NOWN], in_=xm[xi][:, 0:NOWN], func=AF.Identity,
                                                                  bias=modv(3, fc, 0), scale=A2[:, fc, 0:1]),
                         reads=[("xm", xi, g) for g in range(NG)] + ["mod", "A2"], writes=[("hst", xi, 0)])
                    if with_ctx_out:
                        P.op("act", lambda e, xi=xi, fc=fc: e.activation(out=hst[xi][:, NOWN:NB], in_=xm[xi][:, NOWN:NB], func=AF.Identity,
                                                                      bias=modv(3, fc, 1), scale=A2[:, fc, 1:2]),
                             reads=[("xm", xi, g) for g in range(NG)] + ["mod", "A2"], writes=[("hst", xi, 1)])
                    P.op("sp", lambda e, xi=xi, fc=fc: e.dma_start(out=H2d[fc], in_=hst[xi][:]), reads=[("hst", xi, 0), ("hst", xi, 1)],
                         writes=[("H2d", fc)], dsem=f"d_hst{xi}")
            P.barrier()
            NSL = 8; HCS = DFF // NSL // 128
            parts = L["parts"]
            TPM = max(n for _, n in parts)
            with ExitStack() as es2:
                h2T = sb("h2T", [128, KC, TPM], BF16, es2)
                hid = sb("hid", [128, HCS, TPM], BF16, es2)
                w1b = [sb(f"w1b{i}", [128, KC, 256], BF16, es2) for i in range(2)]
                w2b = [sb(f"w2b{i}", [128, HCS, 512], BF16, es2) for i in range(2)]
                rl = [sb(f"rl{i}", [128, 512], F32, es2) for i in range(2)]
                xa = [sb(f"xa{i}", [128, TPM], F32, es2) for i in range(3)]
                w1src = w1.rearrange("(k p) n -> p k n", p=128)
                w2src = w2.rearrange("(k p) n -> p k n", p=128)
                c1 = 0; c2 = 0; rc = 0; xc_ = 0
                rot1 = Rot(range(4)); rot2 = Rot(range(4, 8))
                for pi_, (t0, tn) in enumerate(parts):
                    TG = groups(tn)
                    for k in range(KC):
                        P.op("sp", lambda e, k=k: e.dma_start(out=h2T[:, k, 0:tn], in_=H2d[k, :, t0:t0 + tn]), reads=[("H2d", k)],
                             writes=[("h2T", k)], dsem=f"d_h2T{k % 4}")
                    h2_keys = [("h2T", k) for k in range(KC)]
                    for q8 in range(NSL):
                        for b4 in range(HCS // 2):
                            s = c1 % 2; c1 += 1
                            col0 = q8 * HCS * 128 + b4 * 256
                            P.op("pool", lambda e, s=s, col0=col0: e.dma_start(out=w1b[s][:], in_=w1src[:, :, col0:col0 + 256]),
                                 writes=[("w1b", s)], dsem=f"d_w1b{s}")
                            for c4 in range(2):
                                hc = b4 * 2 + c4
                                for gi, (g0, gn) in enumerate(TG):
                                    pb = rot1.next()
                                    for k in range(KC):
                                        P.op("pe", lambda e, k=k, s=s, c4=c4, pb=pb, g0=g0, gn=gn: e.matmul(
                                            ps[pb][:, 0:gn], lhsT=w1b[s][:, k, c4 * 128:(c4 + 1) * 128], rhs=h2T[:, k, g0:g0 + gn],
                                            start=(k == 0), stop=(k == KC - 1)),
                                            reads=[("w1b", s)] + (h2_keys if k in (0, KC - 1) else []), writes=[("ps", pb)], track=(k == KC - 1))
                                    ri = rc % 2; rc += 1
                                    P.op("act", lambda e, pb=pb, gn=gn, ri=ri: e.activation(out=rl[ri][:, 0:gn], in_=ps[pb][:, 0:gn], func=AF.Relu),
                                         reads=[("ps", pb)], writes=[("rl", ri)])
                                    P.op("dve", lambda e, g0=g0, gn=gn, ri=ri, hc=hc: e.tensor_tensor(out=hid[:, hc, g0:g0 + gn], in0=rl[ri][:, 0:gn],
                                                                                               in1=rl[ri][:, 0:gn], op=ALU.mult),
                                         reads=[("rl", ri)], writes=[("hid", hc, gi)])
                        hid_keys = [("hid", hc, gi) for hc in range(HCS) for gi in range(len(TG))]
                        def load_xa(oc, t0=t0, tn=tn):
                            xi = oc % 3
                            P.op("sp", lambda e: e.dma_start(out=xa[xi][:, 0:tn], in_=Xw[oc, :, t0:t0 + tn]), reads=[("Xw", oc)],
                                 writes=[("xa", xi)], dsem=f"d_xa_{xi}")
                        load_xa(0); load_xa(1)
                        for b8 in range(8):
                            s = c2 % 2; c2 += 1
                            P.op("pool", lambda e, s=s, q8=q8, b8=b8: e.dma_start(out=w2b[s][:], in_=w2src[:, q8 * HCS:(q8 + 1) * HCS, b8 * 512:(b8 + 1) * 512]),
                                 writes=[("w2b", s)], dsem=f"d_w2b{s}")
                            for c4 in range(4):
                                oc = b8 * 4 + c4
                                xi = oc % 3
                                if oc + 2 < KC:
                                    load_xa(oc + 2)
                                for gi, (g0, gn) in enumerate(TG):
                                    pb = rot2.next()
                                    for k in range(HCS):
                                        P.op("pe", lambda e, k=k, s=s, c4=c4, pb=pb, g0=g0, gn=gn: e.matmul(
                                            ps[pb][:, 0:gn], lhsT=w2b[s][:, k, c4 * 128:(c4 + 1) * 128], rhs=hid[:, k, g0:g0 + gn],
                                            start=(k == 0), stop=(k == HCS - 1)),
                                            reads=[("w2b", s)] + (hid_keys if k in (0, HCS - 1) else []), writes=[("ps", pb)], track=(k == HCS - 1))
                                    which = 0 if (t0 + g0) < NOWN else 1
                                    P.op("dve", lambda e, pb=pb, g0=g0, gn=gn, xi=xi, oc=oc, which=which: e.scalar_tensor_tensor(
                                        out=xa[xi][:, g0:g0 + gn], in0=ps[pb][:, 0:gn], scalar=modv(5, oc, which), in1=xa[xi][:, g0:g0 + gn],
                                        op0=ALU.mult, op1=ALU.add),
                                        reads=[("ps", pb), ("xa", xi), "mod"], writes=[("xa", xi)])
                                t = P.op("sp", lambda e, xi=xi, oc=oc: e.dma_start(out=Xw[oc, :, t0:t0 + tn], in_=xa[xi][:, 0:tn]),
                                         reads=[("xa", xi)], writes=[("Xw", oc)], dsem=f"d_xs_{xi}")
                                if q8 == NSL - 1 and L.get("final"):
                                    outs.append(t)
        P.barrier()


def build_fused():
    nc = bass.Bass("TRN2", target_bir_lowering=False)
    P = Prog(nc)
    outs = []
    xT = nc.dram_tensor("xT", [D, NE0], F32, kind="ExternalInput").ap()
    xcT = nc.dram_tensor("xcT", [D, NCX], F32, kind="ExternalInput").ap()
    cv = nc.dram_tensor("cv", [128, KC, 2], F32, kind="ExternalInput").ap()
    auxc = nc.dram_tensor("auxc", [4, NCX], F32, kind="ExternalInput").ap()
    ident_d = nc.dram_tensor("ident", [128, 128], F32, kind="ExternalInput").ap()
    msel_d = nc.dram_tensor("msel", [128, 4], F32, kind="ExternalInput").ap()
    NB0 = NOWN0 + NCX
    Xw0 = nc.dram_tensor("Xw0", [KC, 128, NB0], F32).ap()
    XwF = nc.dram_tensor("Xw", [KC, 128, NOWN], F32, kind="ExternalOutput").ap()
    with ExitStack() as top:
        ps = [top.enter_context(nc.psum_tensor(f"ps{i}", [128, 512], F32)) for i in range(8)]
        ident = top.enter_context(nc.sbuf_tensor("sb_ident", [128, 128], F32))
        ones = top.enter_context(nc.sbuf_tensor("sb_ones", [128, 128], BF16))
        epsb = top.enter_context(nc.sbuf_tensor("sb_epsb", [128, 1], F32))
        P.op("sp", lambda e: e.dma_start(out=ident[:], in_=ident_d), writes=["ident"], dsem="d_misc2")
        P.op("dve", lambda e: e.memset(ones[:], 1.0), writes=["ones"])
        P.op("dve", lambda e: e.memset(epsb[:], EPS), writes=["epsb"])
        cmn = {"cv": cv, "auxc": auxc, "ident": ident, "ones": ones, "epsb": epsb}
        L0 = {"tag": "_0", "NE": NE0, "NOWN": NOWN0, "ctx_out": True, "WB": 256,
              "xsrc": xT.rearrange("(k p) n -> k p n", p=128), "xcsrc": xcT.rearrange("(k p) n -> k p n", p=128),
              "Xw": Xw0, "parts": [(0, 1024), (1024, 768)]}
        emit_layer(nc, P, ps, cmn, L0, outs)
        L1 = {"tag": "_1", "NE": NE, "NOWN": NOWN, "ctx_out": False, "WB": 512,
              "xsrc": Xw0[:, :, 0:NOWN0], "xcsrc": Xw0[:, :, NOWN0:NB0], "sel": {"msel": msel_d},
              "Xw": XwF, "parts": [(0, 1024)], "final": True}
        emit_layer(nc, P, ps, cmn, L1, outs)
        P.final_wait("sp", outs)
        with nc.Block() as block:
            P.emit(block)
    return nc


def _chunked(v):
    return np.ascontiguousarray(v.reshape(-1, 128).T)


def _rowmap(r, nrows, rel0):
    R0 = 16 * r
    rows = []
    for b in range(nrows):
        rel = rel0 + b
        g = R0 + rel
        if 0 <= g <= 63 and rel <= 22:
            rows.append((g, True, True))
        elif r == 0 and -4 <= rel <= -1:
            rows.append((rel + 8, False, True))
        elif r == 3 and 16 <= rel <= 18:
            rows.append((56 + rel - 16, False, True))
        else:
            rows.append((0, False, False))
    return rows


def _bias_tables(rpb_l, rows, nq):
    jq = np.arange(64)[None, :]; jk = np.arange(64)[:, None]
    cs = np.clip(jq - 8, 0, 48)
    cvalid = (jk >= cs) & (jk < cs + 16)
    dc = np.clip(jk - jq + 15, 0, 30)
    out = np.full((16, 128, nq * 320), NEG, np.float32)
    for j in range(nq):
        qi, qnat, _ = rows[j + 4]
        nt = 4 if j % 2 == 0 else 5
        t0 = j // 2
        want = set(range(int(np.clip(qi - 4, 0, 56)), int(np.clip(qi - 4, 0, 56)) + 8)) if qnat else None
        got = set()
        for m in range(nt):
            for half in range(2):
                b = 2 * (t0 + m) + half
                if b < j or b > j + 7 or b >= len(rows):
                    continue
                kr, knat, kused = rows[b]
                if qnat:
                    if (not kused) or kr not in want or kr in got:
                        continue
                    got.add(kr)
                    dr = kr - qi + 7
                else:
                    dr = b - j + 3
                blk = np.where(cvalid[None], rpb_l[:, dr][:, dc], NEG)
                out[:, half * 64:(half + 1) * 64, j * 320 + m * 64:j * 320 + (m + 1) * 64] = blk
        if qnat:
            assert got == want, (j, qi, got, want)
    return out


def _aux(rows, r, rel0):
    R0 = 16 * r
    L = 4096
    n = len(rows)
    aux = np.ones((5, n * 64), np.float32)
    for b in range(n):
        nat = rows[b][1]
        aux[0, b * 64:(b + 1) * 64] = 1.0 if nat else 0.0
        if nat:
            g = (R0 + rel0 + b) * 64 + np.arange(64)
            for wi, w in enumerate((2, 4, 8, 16)):
                lo = np.clip(g - w // 2, 0, L - 1); hi = np.clip(g + (w - 1 - w // 2), 0, L - 1)
                aux[1 + wi, b * 64:(b + 1) * 64] = 1.0 / (hi - lo + 1)
    return aux


def _auxc():
    L = NCX
    g = np.arange(L)
    a = np.zeros((4, L), np.float32)
    for wi, w in enumerate((2, 4, 8, 16)):
        lo = np.clip(g - w // 2, 0, L - 1); hi = np.clip(g + (w - 1 - w // 2), 0, L - 1)
        a[wi] = 1.0 / (hi - lo + 1)
    return a


def _ext_xT(xb, rows):
    out = np.zeros((len(rows) * 64, D), np.float32)
    for b, (g, nat, used) in enumerate(rows):
        if used:
            out[b * 64:(b + 1) * 64] = xb[g * 64:(g + 1) * 64]
    return np.ascontiguousarray(out.T)


def _layer_shared(l, inp, tag):
    vec = np.zeros((128, NVEC), np.float32)
    vec[:, V_N1:V_N1 + 32] = _chunked(inp["norm1_g"][l]); vec[:, V_N2:V_N2 + 32] = _chunked(inp["norm2_g"][l])
    vec[:, V_PS:V_PS + 8] = _chunked(inp["pool_scale"][l]); vec[:, V_CB:V_CB + 8] = _chunked(inp["conv_dw_b"][l])
    vec[:, V_CG:V_CG + 8] = _chunked(inp["conv_norm_g"][l])
    vec[:, V_QG] = inp["q_norm_g"][l]; vec[:, V_KG] = inp["k_norm_g"][l]
    cw = inp["conv_dw_w"][l]
    vec[:, V_CW:V_CW + 248] = cw.T.reshape(8, 128, 31).transpose(1, 0, 2).reshape(128, 248)
    return {
        "vecs" + tag: vec, "bada" + tag: np.ascontiguousarray(np.broadcast_to(inp["b_ada"][l][None], (2, 6 * D))),
        "w_ada" + tag: inp["w_ada"][l], "w_in" + tag: inp["w_in"][l], "w_out" + tag: inp["w_out"][l],
        "w1" + tag: inp["w_mlp1"][l], "w2" + tag: inp["w_mlp2"][l],
        "pool_w" + tag: np.ascontiguousarray(inp["pool_w"][l].reshape(1024, 256)), "pw_w" + tag: inp["conv_pw_w"][l],
    }


def _fused_inputs(inp):
    x = inp["x"]; xc = inp["ctx"]
    shared = {"ident": np.eye(128, dtype=np.float32), "auxc": _auxc()}
    shared.update(_layer_shared(0, inp, "_0"))
    shared.update(_layer_shared(1, inp, "_1"))
    per_r = []
    for r in range(4):
        rows0 = _rowmap(r, 32, -8)
        rows1 = _rowmap(r, 24, -4)
        ms = np.zeros((128, 4), np.float32)
        ms[:, 0] = 0.0 if r == 0 else 1.0; ms[:, 1] = 1.0 if r == 0 else 0.0
        ms[:, 2] = 0.0 if r == 3 else 1.0; ms[:, 3] = 1.0 if r == 3 else 0.0
        per_r.append({
            "rows0": rows0,
            "aux_0": _aux(rows0, r, -8), "aux_1": _aux(rows1, r, -4),
            "bias_0": _bias_tables(inp["rpb"][0], rows0, NOWN0 // 64), "bias_1": _bias_tables(inp["rpb"][1], rows1, NOWN // 64),
            "msel": ms,
        })
    maps = []
    for core in range(8):
        b, r = core // 4, core % 4
        cvv = np.stack([inp["c"][b], inp["c_ctx"]], axis=-1)
        m = dict(shared)
        pr = per_r[r]
        m["xT"] = _ext_xT(x[b], pr["rows0"])
        m["xcT"] = np.ascontiguousarray(xc[b].T)
        m["cv"] = np.ascontiguousarray(cvv.reshape(KC, 128, 2).transpose(1, 0, 2))
        for k in ("aux_0", "aux_1", "bias_0", "bias_1", "msel"):
            m[k] = pr[k]
        maps.append(m)
    return maps


_NC_CACHE = {}


def kernel(**inp):
    inp = {k: np.asarray(v) for k, v in inp.items()}
    inp["x"] = inp["x"].astype(np.float32, copy=False)
    inp["ctx"] = inp["ctx"].astype(np.float32, copy=False)
    if "nc" not in _NC_CACHE:
        _NC_CACHE["nc"] = build_fused()
    nc = _NC_CACHE["nc"]
    maps = _fused_inputs(inp)
    res = run_bass_kernel_spmd(nc, maps, core_ids=list(range(8)))
    out = np.empty_like(inp["x"])
    for core in range(8):
        b, r = core // 4, core % 4
        out[b, r * 1024:(r + 1) * 1024] = res.results[core]["Xw"].reshape(D, NOWN).T
    return out
```

```python
import numpy as np
from contextlib import ExitStack
import concourse.bass as bass
import concourse.mybir as mybir
from concourse.bass_utils import run_bass_kernel_spmd

F32 = mybir.dt.float32
BF16 = mybir.dt.bfloat16
AF = mybir.ActivationFunctionType
ALU = mybir.AluOpType

D = 4096
KC = 32
NE = 1536
NCX = 256
NA = NE + NCX
OWN0 = 256
NOWN = 1024
NE0 = 2048
NOWN0 = 1536
INW = 9216
DFF = 16384
NEG = -30000.0
EPS = 1e-6
PAD = 16
EO = PAD
CO = PAD + NE + 2 * PAD
NPADDED = CO + NCX + PAD
NVEC = 32 + 32 + 8 + 8 + 8 + 1 + 1 + 8 * 31
V_N1, V_N2, V_PS, V_CB, V_CG, V_QG, V_KG, V_CW = 0, 32, 64, 72, 80, 88, 89, 90
AGRP = [(0, 512), (512, 512), (1024, 512), (1536, 256)]


class _Rec:
    def __init__(self):
        self.call = None

    def __getattr__(self, name):
        def f(*a, **k):
            self.call = (name, a, k)
        return f


class Prog:
    def __init__(self, nc):
        self.nc = nc
        self.engs = ["pe", "act", "dve", "pool", "sp"]
        self.rec = {e: [] for e in self.engs}
        self.cnt = {e: 0 for e in self.engs}
        self.seen = {e: {} for e in self.engs}
        self.lastw = {}
        self.readers = {}
        self.dsems = {}
        self.semobj = {}
        for e in self.engs:
            self.semobj[f"S_{e}"] = nc.alloc_semaphore(name=f"S_{e}")

    def dma_sem(self, name):
        if name not in self.dsems:
            h = self.nc.alloc_semaphore(name=name)
            self.dsems[name] = [h, 0]
            self.semobj[name] = h
        return name

    def _need(self, eng, tok, waits):
        if tok is None:
            return
        sname, val = tok
        if self.seen[eng].get(sname, 0) >= val:
            return
        waits[sname] = max(waits.get(sname, 0), val)

    def op(self, eng, fn, reads=(), writes=(), dsem=None, track=True, dinc=16):
        waits = {}
        for k in reads:
            self._need(eng, self.lastw.get(k), waits)
        for k in writes:
            self._need(eng, self.lastw.get(k), waits)
            for t in self.readers.get(k, ()):
                self._need(eng, t, waits)
        if eng == "pe":
            waits.pop("S_pe", None)
        for s, v in waits.items():
            self.seen[eng][s] = max(self.seen[eng].get(s, 0), v)
        if dsem is not None:
            self.dma_sem(dsem)
            d = self.dsems[dsem]
            d[1] += dinc
            tok = (dsem, d[1])
            inc = (dsem, dinc)
        elif track:
            self.cnt[eng] += 1
            tok = (f"S_{eng}", self.cnt[eng])
            inc = (f"S_{eng}", 1)
        else:
            tok = None
            inc = None
        r = _Rec()
        fn(r)
        self.rec[eng].append((list(waits.items()), r.call, inc))
        if tok is not None:
            for k in writes:
                self.lastw[k] = tok
                self.readers[k] = []
            for k in reads:
                self.readers.setdefault(k, []).append(tok)
        return tok

    def barrier(self):
        latest = {}
        for e in self.engs:
            if self.cnt[e] > 0:
                latest[f"S_{e}"] = self.cnt[e]
        for name, (h, v) in self.dsems.items():
            if v > 0:
                latest[name] = v
        for e in self.engs:
            waits = {}
            for sname, v in latest.items():
                self._need(e, (sname, v), waits)
            for sname, v in waits.items():
                self.seen[e][sname] = v
            self.rec[e].append((list(waits.items()), None, None))

    def final_wait(self, eng, toks):
        waits = {}
        for t in toks:
            self._need(eng, t, waits)
        self.rec[eng].append((list(waits.items()), None, None))

    def emit(self, block):
        m = {"pe": block.tensor, "act": block.scalar, "dve": block.vector,
             "pool": block.gpsimd, "sp": block.sync}
        semobj = self.semobj
        for e in self.engs:
            rec = self.rec[e]

            def body(engine, rec=rec):
                for waits, fn, inc in rec:
                    for s, v in waits:
                        engine.wait_ge(semobj[s], v)
                    if fn is not None:
                        ins = getattr(engine, fn[0])(*fn[1], **fn[2])
                        if inc is not None:
                            ins.then_inc(semobj[inc[0]], inc[1])
            m[e](body)


def emit_layer(nc, P, ps, cmn, L, outs):
    tag = L["tag"]; with_ctx_out = L["ctx_out"]
    NE = L["NE"]; NOWN = L["NOWN"]; OWN0 = 256
    NA = NE + NCX
    NB = NOWN + (NCX if with_ctx_out else 0)
    NQ = NOWN // 64
    NTT = NA // 128
    E0 = OWN0 - 16; EN = NOWN + 32
    EO = PAD; CO = EO + EN + 2 * PAD; NPADDED = CO + NCX + PAD
    WB = L["WB"]; WC = WB // 128

    def groups(n):
        g = []
        o = 0
        while o < n:
            g.append((o, min(512, n - o)))
            o += 512
        return g
    AGRP = groups(NA)
    BGRP = groups(NB)
    NAG = len(AGRP)

    def din(name, shape, dt=F32):
        return nc.dram_tensor(name + tag, shape, dt, kind="ExternalInput").ap()

    def dscr(name, shape, dt):
        return nc.dram_tensor(name + tag, shape, dt).ap()

    cv = cmn["cv"]
    aux = din("aux", [5, NE]); auxc = cmn["auxc"]
    biasd = din("bias", [16, 128, NQ * 320])
    vecs_d = din("vecs", [128, NVEC]); bada = din("bada", [2, 6 * D])
    w_ada = din("w_ada", [D, 6 * D]); w_in = din("w_in", [D, INW]); w_out = din("w_out", [D, D])
    w1 = din("w1", [D, DFF]); w2 = din("w2", [DFF, D])
    pool_w = din("pool_w", [1024, 256]); pw_w = din("pw_w", [1024, 1024])
    Xw = L["Xw"]
    Upc = dscr("Upc", [24, 128, NA], F32)
    Qd = dscr("Qd", [16, 128, NA], BF16); Kd = dscr("Kd", [16, 128, NA], BF16)
    Vd = dscr("Vd", [NTT, 128, 2048], BF16)
    Yd = dscr("Yd", [KC, 128, NB], BF16)
    H2d = dscr("H2d", [KC, 128, NB], BF16)
    ident = cmn["ident"]; ones = cmn["ones"]; epsb = cmn["epsb"]
    dbg = False

    class Rot:
        def __init__(self, banks):
            self.b = list(banks); self.i = 0

        def next(self):
            v = self.b[self.i % len(self.b)]; self.i += 1
            return v

    with ExitStack() as top:
        sb = lambda name, shape, dt, es=top: es.enter_context(nc.sbuf_tensor("sb_" + tag + name, shape, dt))
        vecs = sb("vecs", [128, NVEC], F32)
        mod = sb("mod", [128, 192, 2], F32)
        A1 = sb("A1", [128, KC, 2], F32); A2 = sb("A2", [128, KC, 2], F32)
        P.op("sp", lambda e: e.dma_start(out=vecs[:], in_=vecs_d), writes=["vecs"], dsem="d_misc")

        rr = {"evac": 0}

        def evac_eng():
            rr["evac"] += 1
            return "dve" if rr["evac"] % 2 == 0 else "act"

        def copy_op(eng, out, in_, reads, writes):
            if eng == "act":
                return P.op("act", lambda e: e.activation(out=out, in_=in_, func=AF.Copy), reads=reads, writes=writes)
            return P.op(eng, lambda e: e.tensor_copy(out=out, in_=in_), reads=reads, writes=writes)

        with ExitStack() as es:
            NWA = 5
            wb = [sb(f"wa{i}", [128, KC, 512], BF16, es) for i in range(NWA)]
            cv32 = sb("cv32", [128, KC, 2], F32, es); sil = sb("sil", [128, KC, 2], BF16, es)
            mrow = [sb(f"mrow{i}", [2, 512], F32, es) for i in range(2)]
            brow8 = [sb(f"brow{i}", [2, 4096], F32, es) for i in range(2)]
            P.op("sp", lambda e: e.dma_start(out=cv32[:], in_=cv), writes=["cv32"], dsem="d_cv")
            P.op("act", lambda e: e.activation(out=sil[:], in_=cv32[:], func=AF.Silu), reads=["cv32"], writes=["sil"])
            wsrc = w_ada.rearrange("(k p) n -> p k n", p=128)
            for b in range(48):
                s = b % NWA
                P.op("pool", lambda e, b=b, s=s: e.dma_start(out=wb[s][:], in_=wsrc[:, :, b * 512:(b + 1) * 512]),
                     writes=[("wa", s)], dsem=f"d_wa{s}")
                pi = b % 2
                bi = (b // 8) % 2
                if b % 8 == 0:
                    P.op("sp", lambda e, b=b, bi=bi: e.dma_start(out=brow8[bi][:], in_=bada[:, b * 512:b * 512 + 4096]),
                         writes=[("brow", bi)], dsem=f"d_brow{bi}")
                for k in range(KC):
                    P.op("pe", lambda e, k=k, s=s, pi=pi: e.matmul(ps[pi][0:2, :], lhsT=sil[:, k, :], rhs=wb[s][:, k, :],
                                                                   start=(k == 0), stop=(k == KC - 1)),
                         reads=[("wa", s), "sil"], writes=[("ps", pi)], track=(k == KC - 1))
                P.op("dve", lambda e, b=b, pi=pi, bi=bi: e.tensor_tensor(out=mrow[pi][:], in0=ps[pi][0:2, :],
                                                                      in1=brow8[bi][:, (b % 8) * 512:(b % 8 + 1) * 512], op=ALU.add),
                     reads=[("ps", pi), ("brow", bi)], writes=[("mrow", pi)])
                tb = 2 + (b // 24)
                for jj in range(4):
                    j = b * 4 + jj
                    P.op("pe", lambda e, j=j, jj=jj, pi=pi, tb=tb: e.transpose(out=ps[tb][:, (j % 96) * 2:(j % 96) * 2 + 2],
                                                                         in_=mrow[pi][0:2, jj * 128:(jj + 1) * 128], identity=ident[0:2, 0:2]),
                         reads=[("mrow", pi), "ident"], writes=[("ps", tb)], track=(jj == 3))
            for hh in range(2):
                P.op("dve", lambda e, hh=hh: e.tensor_copy(out=mod[:, hh * 96:(hh + 1) * 96, :].rearrange("p a b -> p (a b)"),
                                                          in_=ps[2 + hh][:, 0:192]),
                     reads=[("ps", 2 + hh)], writes=["mod"])
            for which in range(2):
                P.op("dve", lambda e, which=which: e.scalar_tensor_tensor(out=A1[:, :, which], in0=mod[:, 32:64, which], scalar=1.0,
                                                                         in1=vecs[:, V_N1:V_N1 + 32], op0=ALU.add, op1=ALU.mult),
                     reads=["mod", "vecs"], writes=["A1"])
                P.op("dve", lambda e, which=which: e.scalar_tensor_tensor(out=A2[:, :, which], in0=mod[:, 128:160, which], scalar=1.0,
                                                                         in1=vecs[:, V_N2:V_N2 + 32], op0=ALU.add, op1=ALU.mult),
                     reads=["mod", "vecs"], writes=["A2"])

        P.barrier()
        def modv(m, fc, which):
            return mod[:, m * 32 + fc, which:which + 1]

        with ExitStack() as es:
            hT = sb("hT", [128, KC, NA], BF16, es)
            with ExitStack() as es2:
                xc = [sb(f"xc{i}", [128, NA], F32, es2) for i in range(2)]
                sq = [sb(f"sq{i}", [128, NA], BF16, es2) for i in range(2)]
                tmp = [sb(f"tmp{i}", [128, NA], F32, es2) for i in range(1)] * 2
                rstd = sb("rstd", [128, NA], F32, es2)
                xsrc = L["xsrc"]; xcsrc = L["xcsrc"]; sel = L.get("sel")
                if sel is not None:
                    alt = [sb(f"alt{i}", [128, 448], F32, es2) for i in range(2)]
                    msel = sb("msel", [128, 4], F32, es2)
                    P.op("sp", lambda e: e.dma_start(out=msel[:], in_=sel["msel"]), writes=["msel"], dsem="d_msel")

                def load_x(fc):
                    s = fc % 2
                    P.op("sp", lambda e: e.dma_start(out=xc[s][:, 0:NE], in_=xsrc[fc]), writes=[("xc", s, 0)], dsem=f"d_xa{s}")
                    P.op("sp", lambda e: e.dma_start(out=xc[s][:, NE:NA], in_=xcsrc[fc]), writes=[("xc", s, 1)], dsem=f"d_xb{s}")
                    if sel is not None:
                        P.op("sp", lambda e: e.dma_start(out=alt[s][:, 0:256], in_=xsrc[fc, :, 512:768]), writes=[("alt", s, 0)], dsem=f"d_alta{s}")
                        P.op("sp", lambda e: e.dma_start(out=alt[s][:, 256:448], in_=xsrc[fc, :, 768:960]), writes=[("alt", s, 1)], dsem=f"d_altb{s}")
                        for (x0, xn, a0, m0) in ((0, 256, 0, 0), (1280, 192, 256, 2)):
                            P.op("dve", lambda e, x0=x0, xn=xn, m0=m0: e.tensor_scalar(out=xc[s][:, x0:x0 + xn], in0=xc[s][:, x0:x0 + xn],
                                                                                 scalar1=msel[:, m0:m0 + 1], scalar2=None, op0=ALU.mult),
                                 reads=[("xc", s, 0), "msel"], writes=[("xc", s, 0)])
                            P.op("dve", lambda e, x0=x0, xn=xn, a0=a0, m0=m0: e.scalar_tensor_tensor(
                                out=xc[s][:, x0:x0 + xn], in0=alt[s][:, a0:a0 + xn], scalar=msel[:, m0 + 1:m0 + 2], in1=xc[s][:, x0:x0 + xn],
                                op0=ALU.mult, op1=ALU.add),
                                reads=[("xc", s, 0), ("alt", s, 0), ("alt", s, 1), "msel"], writes=[("xc", s, 0)])
                    return s
                for fc in range(KC):
                    s = load_x(fc)
                    P.op("act", lambda e, s=s: e.activation(out=sq[s][:], in_=xc[s][:], func=AF.Square),
                         reads=[("xc", s, 0), ("xc", s, 1)], writes=[("sq", s)])
                    for gi, (g0, gn) in enumerate(AGRP):
                        P.op("pe", lambda e, s=s, gi=gi, g0=g0, gn=gn, fc=fc: e.matmul(ps[gi][:, 0:gn], lhsT=ones[:], rhs=sq[s][:, g0:g0 + gn],
                                                                                  start=(fc == 0), stop=(fc == KC - 1)),
                             reads=[("sq", s), "ones"], writes=[("ps", gi)], track=(gi == NAG - 1))
                for gi, (g0, gn) in enumerate(AGRP):
                    P.op("act", lambda e, gi=gi, g0=g0, gn=gn: e.activation(out=rstd[:, g0:g0 + gn], in_=ps[gi][:, 0:gn], func=AF.Sqrt,
                                                                         bias=epsb[:], scale=1.0 / D),
                         reads=[("ps", gi), "epsb"], writes=[("rstd", gi)])
                    P.op("dve", lambda e, g0=g0, gn=gn: e.reciprocal(out=rstd[:, g0:g0 + gn], in_=rstd[:, g0:g0 + gn]),
                         reads=[("rstd", gi)], writes=[("rstd", gi)])
                for fc in range(KC):
                    s = load_x(fc)
                    P.op("dve", lambda e, s=s: e.tensor_tensor(out=tmp[s][:], in0=xc[s][:], in1=rstd[:], op=ALU.mult),
                         reads=[("xc", s, 0), ("xc", s, 1)] + [("rstd", g) for g in range(NAG)], writes=[("tmp", 0)])
                    P.op("act", lambda e, s=s, fc=fc: e.activation(out=hT[:, fc, 0:NE], in_=tmp[s][:, 0:NE], func=AF.Identity,
                                                                 bias=modv(0, fc, 0), scale=A1[:, fc, 0:1]),
                         reads=[("tmp", 0), "mod", "A1"], writes=[("hT", fc, 0)])
                    P.op("act", lambda e, s=s, fc=fc: e.activation(out=hT[:, fc, NE:NA], in_=tmp[s][:, NE:NA], func=AF.Identity,
                                                                 bias=modv(0, fc, 1), scale=A1[:, fc, 1:2]),
                         reads=[("tmp", 0), "mod", "A1"], writes=[("hT", fc, 1)])
            P.barrier()
            hT_keys = [("hT", fc, w) for fc in range(KC) for w in range(2)]
            with ExitStack() as es2:
                wb = [sb(f"wi{i}", [128, KC, WB], BF16, es2) for i in range(2)]
                stg = [sb(f"stg{i}", [128, 512], F32, es2) for i in range(3)]
                stb = [sb(f"stb{i}", [128, 512], BF16, es2) for i in range(3)]
                sqb = [sb(f"sqb{i}", [128, 512], BF16, es2) for i in range(2)]
                rs = [sb(f"rs{i}", [128, 512], F32, es2) for i in range(2)]
                vst = [sb(f"vst{i}", [128, WB], BF16, es2) for i in range(2)]
                wsrc = w_in.rearrange("(k p) n -> p k n", p=128)
                cnt = {"stg": 0, "stb": 0, "g": 0, "v": 0}
                rot = Rot(range(6)); rots = Rot([6, 7])
                for cb in range(INW // WB):
                    s = cb % 2
                    P.op("pool", lambda e, cb=cb, s=s: e.dma_start(out=wb[s][:], in_=wsrc[:, :, cb * WB:(cb + 1) * WB]),
                         writes=[("wi", s)], dsem=f"d_wi{s}")
                    if cb * WB < 7168:
                        for c4 in range(WC):
                            chunk = cb * WC + c4
                            for gi, (g0, gn) in enumerate(AGRP):
                                if chunk < 24:
                                    si = cnt["stg"] % 3; cnt["stg"] += 1
                                else:
                                    si = cnt["stb"] % 3; cnt["stb"] += 1
                                pb = rot.next()
                                for k in range(KC):
                                    P.op("pe", lambda e, k=k, s=s, c4=c4, pb=pb, g0=g0, gn=gn: e.matmul(
                                        ps[pb][:, 0:gn], lhsT=wb[s][:, k, c4 * 128:(c4 + 1) * 128], rhs=hT[:, k, g0:g0 + gn],
                                        start=(k == 0), stop=(k == KC - 1)),
                                        reads=[("wi", s)] + (hT_keys if k in (0, KC - 1) else []), writes=[("ps", pb)], track=(k == KC - 1))
                                if chunk < 24:
                                    copy_op(evac_eng(), stg[si][:, 0:gn], ps[pb][:, 0:gn], [("ps", pb)], [("stg", si)])
                                    P.op("sp", lambda e, si=si, chunk=chunk, g0=g0, gn=gn: e.dma_start(out=Upc[chunk, :, g0:g0 + gn], in_=stg[si][:, 0:gn]),
                                         reads=[("stg", si)], writes=[("Upc", chunk, gi)], dsem=f"d_stg{si}")
                                else:
                                    gcol = V_QG if chunk < 40 else V_KG
                                    qi = cnt["g"] % 2; cnt["g"] += 1
                                    P.op("act", lambda e, pb=pb, gn=gn, qi=qi: e.activation(out=sqb[qi][:, 0:gn], in_=ps[pb][:, 0:gn], func=AF.Square),
                                         reads=[("ps", pb)], writes=[("sqb", qi)])
                                    pj = rots.next()
                                    P.op("pe", lambda e, gn=gn, qi=qi, pj=pj: e.matmul(ps[pj][:, 0:gn], lhsT=ones[:], rhs=sqb[qi][:, 0:gn], start=True, stop=True),
                                         reads=[("sqb", qi), "ones"], writes=[("ps", pj)])
                                    P.op("act", lambda e, gn=gn, qi=qi, pj=pj: e.activation(out=rs[qi][:, 0:gn], in_=ps[pj][:, 0:gn], func=AF.Sqrt,
                                                                                          bias=epsb[:], scale=1.0 / 128),
                                         reads=[("ps", pj), "epsb"], writes=[("rs", qi)])
                                    P.op("dve", lambda e, gn=gn, qi=qi: e.reciprocal(out=rs[qi][:, 0:gn], in_=rs[qi][:, 0:gn]),
                                         reads=[("rs", qi)], writes=[("rs", qi)])
                                    P.op("dve", lambda e, pb=pb, g0=g0, gn=gn, qi=qi, si=si, gcol=gcol: e.scalar_tensor_tensor(
                                        out=stb[si][:, 0:gn], in0=ps[pb][:, 0:gn], scalar=vecs[:, gcol:gcol + 1], in1=rs[qi][:, 0:gn],
                                        op0=ALU.mult, op1=ALU.mult),
                                        reads=[("ps", pb), ("rs", qi), "vecs"], writes=[("stb", si)])
                                    dst = Qd[chunk - 24, :, g0:g0 + gn] if chunk < 40 else Kd[chunk - 40, :, g0:g0 + gn]
                                    key = ("Qd", chunk - 24, gi) if chunk < 40 else ("Kd", chunk - 40, gi)
                                    P.op("sp", lambda e, si=si, dst=dst, gn=gn: e.dma_start(out=dst, in_=stb[si][:, 0:gn]),
                                         reads=[("stb", si)], writes=[key], dsem=f"d_stb{si}")
                    else:
                        vb = cb - 7168 // WB
                        for tt in range(NTT):
                            pi = rot.next()
                            for k in range(KC):
                                P.op("pe", lambda e, k=k, s=s, tt=tt, pi=pi: e.matmul(ps[pi][:, 0:WB], lhsT=hT[:, k, tt * 128:(tt + 1) * 128], rhs=wb[s][:, k, :],
                                                                                    start=(k == 0), stop=(k == KC - 1)),
                                     reads=[("wi", s)] + (hT_keys if k in (0, KC - 1) else []), writes=[("ps", pi)], track=(k == KC - 1))
                            vi = cnt["v"] % 2; cnt["v"] += 1
                            copy_op(evac_eng(), vst[vi][:], ps[pi][:, 0:WB], [("ps", pi)], [("vst", vi)])
                            P.op("sp", lambda e, vi=vi, tt=tt, vb=vb: e.dma_start(out=Vd[tt, :, vb * WB:(vb + 1) * WB], in_=vst[vi][:]),
                                 reads=[("vst", vi)], writes=[("Vd", tt, vb * WB // 512)], dsem=f"d_vst{vi}")

        P.barrier()
        with ExitStack() as es:
            WINS = (2, 4, 8, 16)
            with ExitStack() as es2:
                up = [sb(f"up{i}", [128, NPADDED], F32, es2) for i in range(2)]
                sa = sb("sa", [128, NPADDED], F32, es2); sbb = sb("sbb", [128, NPADDED], F32, es2)
                valid = sb("valid", [128, EN], F32, es2)
                icnt = sb("icnt", [128, 4, NB], F32, es2)
                pooled = sb("pooled", [128, 2, NB], BF16, es2)
                pw = sb("pw", [128, 8, 256], BF16, es2)
                yst = [sb(f"yst{i}", [128, NB], BF16, es2) for i in range(2)]
                P.op("sp", lambda e: e.dma_start(out=valid[:], in_=aux[0, E0:E0 + EN].partition_broadcast(128)), writes=["valid"], dsem="d_valid")
                for wi in range(4):
                    P.op("sp", lambda e, wi=wi: e.dma_start(out=icnt[:, wi, 0:NOWN], in_=aux[1 + wi, OWN0:OWN0 + NOWN].partition_broadcast(128)),
                         writes=[("icnt", wi, 0)], dsem=f"d_icnt{wi}")
                    if with_ctx_out:
                        P.op("sp", lambda e, wi=wi: e.dma_start(out=icnt[:, wi, NOWN:NB], in_=auxc[wi].partition_broadcast(128)),
                             writes=[("icnt", wi, 1)], dsem=f"d_icnt{wi}")
                P.op("pool", lambda e: e.dma_start(out=pw[:], in_=pool_w.rearrange("(k p) n -> p k n", p=128)), writes=["pw"], dsem="d_pw")
                for i in range(2):
                    P.op("dve", lambda e, i=i: e.memset(up[i][:], 0.0), writes=[("up", i)])
                P.op("dve", lambda e: e.memset(sa[:], 0.0), writes=["sa"])
                P.op("dve", lambda e: e.memset(sbb[:], 0.0), writes=["sbb"])
                yc = 0
                for g in range(4):
                    for c2 in range(2):
                        chunk = 2 * g + c2
                        ui = chunk % 2
                        P.op("sp", lambda e, ui=ui, chunk=chunk: e.dma_start(out=up[ui][:, EO:EO + EN], in_=Upc[chunk, :, E0:E0 + EN]),
                             reads=[("Upc", chunk, g_) for g_ in range(NAG)], writes=[("up", ui)], dsem=f"d_up{ui}")
                        P.op("sp", lambda e, ui=ui, chunk=chunk: e.dma_start(out=up[ui][:, CO:CO + NCX], in_=Upc[chunk, :, NE:NA]),
                             reads=[("Upc", chunk, g_) for g_ in range(NAG)], writes=[("upc", ui)], dsem=f"d_upc{ui}")
                        P.op("dve", lambda e, ui=ui: e.tensor_tensor(out=up[ui][:, EO:EO + EN], in0=up[ui][:, EO:EO + EN], in1=valid[:], op=ALU.mult),
                             reads=[("up", ui), "valid"], writes=[("up", ui)])
                        lo, hi = EO - PAD + 8, NPADDED - 8
                        src = up[ui]
                        bufs = [sa, sbb]
                        P.op("dve", lambda e, src=src: e.tensor_tensor(out=sa[:, 8:NPADDED - 8], in0=src[:, 7:NPADDED - 9], in1=src[:, 8:NPADDED - 8], op=ALU.add),
                             reads=[("up", ui), ("upc", ui)], writes=["sa"])
                        cur, other = sa, sbb
                        curk, otherk = "sa", "sbb"
                        for lvl in range(g):
                            sh = 1 << lvl
                            P.op("dve", lambda e, cur=cur, other=other, sh=sh: e.tensor_tensor(
                                out=other[:, 8:NPADDED - 8], in0=cur[:, 8 - sh:NPADDED - 8 - sh], in1=cur[:, 8 + sh:NPADDED - 8 + sh], op=ALU.add),
                                reads=[curk], writes=[otherk])
                            cur, other = other, cur
                            curk, otherk = otherk, curk
                        segs = [(EO + 16, 0, NOWN)] + ([(CO, NOWN, NCX)] if with_ctx_out else [])
                        for (so, bo, n) in segs:
                            P.op("dve", lambda e, cur=cur, so=so, bo=bo, n=n, g=g: e.tensor_tensor(
                                out=cur[:, so:so + n], in0=cur[:, so:so + n], in1=icnt[:, g, bo:bo + n], op=ALU.mult),
                                reads=[curk] + [("icnt", g, 0), ("icnt", g, 1)], writes=[curk])
                            P.op("dve", lambda e, cur=cur, so=so, bo=bo, n=n, c2=c2, src=src: e.tensor_tensor(
                                out=pooled[:, c2, bo:bo + n], in0=cur[:, so:so + n], in1=src[:, so:so + n], op=ALU.subtract),
                                reads=[curk, ("up", ui), ("upc", ui)], writes=[("pooled", c2)])
                    for oc in range(2):
                        yi = yc % 2; yc += 1
                        for gi, (g0, gn) in enumerate(BGRP):
                            for k in range(2):
                                P.op("pe", lambda e, k=k, g=g, oc=oc, gi=gi, g0=g0, gn=gn: e.matmul(
                                    ps[gi][:, 0:gn], lhsT=pw[:, 2 * g + k, oc * 128:(oc + 1) * 128], rhs=pooled[:, k, g0:g0 + gn],
                                    start=(k == 0), stop=(k == 1)),
                                    reads=["pw", ("pooled", 0), ("pooled", 1)], writes=[("ps", gi)], track=(k == 1))
                            P.op("act", lambda e, gi=gi, g0=g0, gn=gn, yi=yi, g=g, oc=oc: e.activation(
                                out=yst[yi][:, g0:g0 + gn], in_=ps[gi][:, 0:gn], func=AF.Copy, scale=vecs[:, V_PS + 2 * g + oc:V_PS + 2 * g + oc + 1]),
                                reads=[("ps", gi), "vecs"], writes=[("yst", yi)])
                        P.op("sp", lambda e, yi=yi, g=g, oc=oc: e.dma_start(out=Yd[2 * g + oc], in_=yst[yi][:]),
                             reads=[("yst", yi)], writes=[("Yd", 2 * g + oc)], dsem=f"d_yst{yi}")

            P.barrier()
            with ExitStack() as es2:
                ua = [sb(f"ua{i}", [128, EN + NCX], F32, es2) for i in range(2)]
                ug = [sb(f"ug{i}", [128, EN + NCX], F32, es2) for i in range(2)]
                hp = [sb(f"hp{i}", [128, NPADDED], F32, es2) for i in range(2)]
                valid = sb("valid2", [128, EN], F32, es2)
                cvo = sb("cvo", [128, 8, NB], F32, es2)
                sqc = [sb(f"sqc{i}", [128, NB], BF16, es2) for i in range(2)]
                rsc = sb("rsc", [128, NB], F32, es2)
                zt = sb("zt", [128, 8, NB], BF16, es2)
                ztmp = [sb(f"ztmp{i}", [128, NB], F32, es2) for i in range(2)]
                pwc = sb("pwc", [128, 8, 1024], BF16, es2)
                yst = [sb(f"ystc{i}", [128, NB], BF16, es2) for i in range(2)]
                P.op("sp", lambda e: e.dma_start(out=valid[:], in_=aux[0, E0:E0 + EN].partition_broadcast(128)), writes=["valid2"], dsem="d_valid")
                P.op("pool", lambda e: e.dma_start(out=pwc[:], in_=pw_w.rearrange("(k p) n -> p k n", p=128)), writes=["pwc"], dsem="d_pwc")
                for i in range(2):
                    P.op("dve", lambda e, i=i: e.memset(hp[i][:], 0.0), writes=[("hp", i)])
                segs = [(EO + 16, 0, NOWN)] + ([(CO, NOWN, NCX)] if with_ctx_out else [])
                for i in range(8):
                    s = i % 2
                    P.op("sp", lambda e, s=s, i=i: e.dma_start(out=ua[s][:, 0:EN], in_=Upc[8 + i, :, E0:E0 + EN]), writes=[("ua", s)], dsem=f"d_ua{s}")
                    P.op("sp", lambda e, s=s, i=i: e.dma_start(out=ua[s][:, EN:EN + NCX], in_=Upc[8 + i, :, NE:NA]), writes=[("uac", s)], dsem=f"d_uac{s}")
                    P.op("sp", lambda e, s=s, i=i: e.dma_start(out=ug[s][:, 0:EN], in_=Upc[16 + i, :, E0:E0 + EN]), writes=[("ug", s)], dsem=f"d_ug{s}")
                    P.op("sp", lambda e, s=s, i=i: e.dma_start(out=ug[s][:, EN:EN + NCX], in_=Upc[16 + i, :, NE:NA]), writes=[("ugc", s)], dsem=f"d_ugc{s}")
                    P.op("act", lambda e, s=s: e.activation(out=ug[s][:], in_=ug[s][:], func=AF.Sigmoid), reads=[("ug", s), ("ugc", s)], writes=[("ug", s), ("ugc", s)])
                    P.op("dve", lambda e, s=s: e.tensor_tensor(out=hp[s][:, EO:EO + EN], in0=ua[s][:, 0:EN], in1=ug[s][:, 0:EN], op=ALU.mult),
                         reads=[("ua", s), ("ug", s)], writes=[("hp", s)])
                    P.op("dve", lambda e, s=s: e.tensor_tensor(out=hp[s][:, CO:CO + NCX], in0=ua[s][:, EN:EN + NCX], in1=ug[s][:, EN:EN + NCX], op=ALU.mult),
                         reads=[("uac", s), ("ugc", s), ("ug", s)], writes=[("hp", s)])
                    P.op("dve", lambda e, s=s: e.tensor_tensor(out=hp[s][:, EO:EO + EN], in0=hp[s][:, EO:EO + EN], in1=valid[:], op=ALU.mult),
                         reads=[("hp", s), "valid2"], writes=[("hp", s)])
                    for (so, bo, n) in segs:
                        for k in range(31):
                            wcol = V_CW + i * 31 + k
                            off = so + k - 15
                            if k == 0:
                                P.op("dve", lambda e, s=s, i=i, off=off, bo=bo, n=n, wcol=wcol: e.tensor_scalar(
                                    out=cvo[:, i, bo:bo + n], in0=hp[s][:, off:off + n], scalar1=vecs[:, wcol:wcol + 1],
                                    scalar2=vecs[:, V_CB + i:V_CB + i + 1], op0=ALU.mult, op1=ALU.add),
                                    reads=[("hp", s), "vecs"], writes=[("cvo", i, bo)])
                            else:
                                P.op("dve", lambda e, s=s, i=i, off=off, bo=bo, n=n, wcol=wcol: e.scalar_tensor_tensor(
                                    out=cvo[:, i, bo:bo + n], in0=hp[s][:, off:off + n], scalar=vecs[:, wcol:wcol + 1],
                                    in1=cvo[:, i, bo:bo + n], op0=ALU.mult, op1=ALU.add),
                                    reads=[("hp", s), "vecs"], writes=[("cvo", i, bo)])
                    qi = i % 2
                    P.op("act", lambda e, i=i, qi=qi: e.activation(out=sqc[qi][:], in_=cvo[:, i, :], func=AF.Square),
                         reads=[("cvo", i, b_) for (_, b_, _) in segs], writes=[("sqc", qi)])
                    for gi, (g0, gn) in enumerate(BGRP):
                        P.op("pe", lambda e, qi=qi, gi=gi, g0=g0, gn=gn, i=i: e.matmul(ps[4 + gi][:, 0:gn], lhsT=ones[:], rhs=sqc[qi][:, g0:g0 + gn],
                                                                                 start=(i == 0), stop=(i == 7)),
                             reads=[("sqc", qi), "ones"], writes=[("ps", 4 + gi)], track=(gi == len(BGRP) - 1))
                for gi, (g0, gn) in enumerate(BGRP):
                    P.op("act", lambda e, gi=gi, g0=g0, gn=gn: e.activation(out=rsc[:, g0:g0 + gn], in_=ps[4 + gi][:, 0:gn], func=AF.Sqrt,
                                                                         bias=epsb[:], scale=1.0 / 1024),
                         reads=[("ps", 4 + gi), "epsb"], writes=[("rsc", gi)])
                    P.op("dve", lambda e, g0=g0, gn=gn: e.reciprocal(out=rsc[:, g0:g0 + gn], in_=rsc[:, g0:g0 + gn]),
                         reads=[("rsc", gi)], writes=[("rsc", gi)])
                for i in range(8):
                    qi = i % 2
                    P.op("dve", lambda e, i=i, qi=qi: e.scalar_tensor_tensor(out=ztmp[qi][:], in0=cvo[:, i, :], scalar=vecs[:, V_CG + i:V_CG + i + 1],
                                                                         in1=rsc[:], op0=ALU.mult, op1=ALU.mult),
                         reads=[("cvo", i, b_) for (_, b_, _) in segs] + [("rsc", g) for g in range(len(BGRP))] + ["vecs"], writes=[("ztmp", qi)])
                    P.op("act", lambda e, i=i, qi=qi: e.activation(out=zt[:, i, :], in_=ztmp[qi][:], func=AF.Silu),
                         reads=[("ztmp", qi)], writes=[("zt", i)])
                yc = 0
                for oc in range(8):
                    yi = yc % 2; yc += 1
                    for gi, (g0, gn) in enumerate(BGRP):
                        for k in range(8):
                            P.op("pe", lambda e, k=k, oc=oc, gi=gi, g0=g0, gn=gn: e.matmul(
                                ps[gi][:, 0:gn], lhsT=pwc[:, k, oc * 128:(oc + 1) * 128], rhs=zt[:, k, g0:g0 + gn], start=(k == 0), stop=(k == 7)),
                                reads=["pwc"] + [("zt", kk) for kk in range(8)], writes=[("ps", gi)], track=(k == 7))
                        copy_op(evac_eng(), yst[yi][:, g0:g0 + gn], ps[gi][:, 0:gn], [("ps", gi)], [("ystc", yi)])
                    P.op("sp", lambda e, yi=yi, oc=oc: e.dma_start(out=Yd[8 + oc], in_=yst[yi][:]),
                         reads=[("ystc", yi)], writes=[("Yd", 8 + oc)], dsem=f"d_ystc{yi}")

            P.barrier()
            with ExitStack() as es2:
                qT = [sb(f"qT{i}", [128, NA], BF16, es2) for i in range(2)]
                kT = [sb(f"kT{i}", [128, NA], BF16, es2) for i in range(2)]
                Vh = [sb(f"Vh{i}", [128, NTT, 128], BF16, es2) for i in range(2)]
                bt = [sb(f"bt{i}", [128, NQ * 320], F32, es2) for i in range(2)]
                NSB = 4
                sbuf_s = [sb(f"ss{i}", [128, 320], F32, es2) for i in range(NSB)]
                pT = [sb(f"pT{i}", [128, 320], BF16, es2) for i in range(NSB)]
                pc = [sb(f"pc{i}", [128, 2, 512], BF16, es2) for i in range(2)]
                rden = [sb(f"rden{i}", [128, 512], F32, es2) for i in range(2)]
                yst = [sb(f"ysta{i}", [128, NB], BF16, es2) for i in range(2)]
                hook = L.get("attn_hook")
                if hook is not None:
                    hook_state = L["attn_hook_alloc"](sb, es2)
                SCALE = 128.0 ** -0.5
                rotS = Rot([0, 1, 2, 3])
                st = {"cq": 0, "u": 0}
                SK = 3
                for h in range(16):
                    s = h % 2
                    P.op("sp", lambda e, s=s, h=h: e.dma_start(out=qT[s][:], in_=Qd[h]), reads=[("Qd", h, g_) for g_ in range(NAG)], writes=[("qT", s)], dsem=f"d_qT{s}")
                    P.op("sp", lambda e, s=s, h=h: e.dma_start(out=kT[s][:], in_=Kd[h]), reads=[("Kd", h, g_) for g_ in range(NAG)], writes=[("kT", s)], dsem=f"d_kT{s}")
                    P.op("sp", lambda e, s=s, h=h: e.dma_start(out=Vh[s][:], in_=Vd[:, :, h * 128:(h + 1) * 128].rearrange("t p d -> p t d")),
                         reads=[("Vd", tt, h // 4) for tt in range(NTT)], writes=[("Vh", s)], dsem=f"d_Vh{s}")
                    P.op("sp", lambda e, s=s, h=h: e.dma_start(out=bt[s][:], in_=biasd[h]), writes=[("bt", s)], dsem=f"d_bt{s}")
                    grp_ci = {}

                    def stageA(j, s=s):
                        u = st["u"] % NSB; st["u"] += 1
                        nt = 4 if j % 2 == 0 else 5
                        t0 = j // 2
                        sbk = rotS.next()
                        psS = ps[sbk]
                        qs = OWN0 + j * 64
                        for m in range(nt):
                            P.op("pe", lambda e, m=m: e.matmul(
                                psS[:, m * 64:(m + 1) * 64], lhsT=kT[s][:, (t0 + m) * 128:(t0 + m + 1) * 128], rhs=qT[s][:, qs:qs + 64],
                                start=True, stop=True),
                                reads=[("kT", s), ("qT", s)], writes=[("ps", sbk)], track=(m == nt - 1))
                        P.op("dve", lambda e: e.scalar_tensor_tensor(
                            out=sbuf_s[u][:, 0:nt * 64], in0=psS[:, 0:nt * 64], scalar=SCALE, in1=bt[s][:, j * 320:j * 320 + nt * 64],
                            op0=ALU.mult, op1=ALU.add),
                            reads=[("ps", sbk), ("bt", s)], writes=[("ss", u)])
                        P.op("act", lambda e: e.activation(out=pT[u][:, 0:nt * 64], in_=sbuf_s[u][:, 0:nt * 64], func=AF.Exp),
                             reads=[("ss", u)], writes=[("pT", u)])
                        return u

                    def stageC(grp, s=s):
                        ci = st["cq"] % 2; st["cq"] += 1
                        grp_ci[grp] = ci
                        pO, pD = 4 + 2 * ci, 5 + 2 * ci
                        q0 = OWN0 + grp * 512
                        for ct in range(2):
                            sbk = rotS.next()
                            P.op("pe", lambda e, ct=ct, sbk=sbk: e.matmul(ps[sbk][:, :], lhsT=kT[s][:, NE + ct * 128:NE + (ct + 1) * 128],
                                                                        rhs=qT[s][:, q0:q0 + 512], start=True, stop=True),
                                 reads=[("kT", s), ("qT", s)], writes=[("ps", sbk)])
                            P.op("act", lambda e, ct=ct, sbk=sbk: e.activation(out=pc[ci][:, ct, :], in_=ps[sbk][:, :], func=AF.Exp, scale=SCALE),
                                 reads=[("ps", sbk)], writes=[("pc", ci, ct)])
                        for ct in range(2):
                            P.op("pe", lambda e, ct=ct: e.matmul(ps[pO][:, :], lhsT=Vh[s][:, NE // 128 + ct, :], rhs=pc[ci][:, ct, :],
                                                               start=(ct == 0), stop=False, skip_group_check=True),
                                 reads=[("Vh", s), ("pc", ci, ct)], writes=[("ps", pO)])
                            P.op("pe", lambda e, ct=ct: e.matmul(ps[pD][:, :], lhsT=ones[:], rhs=pc[ci][:, ct, :],
                                                               start=(ct == 0), stop=False, skip_group_check=True),
                                 reads=["ones", ("pc", ci, ct)], writes=[("ps", pD)])

                    def stageB(j, u, s=s):
                        grp, jr = j // 8, j % 8
                        if jr == 0:
                            stageC(grp)
                        ci = grp_ci[grp]
                        pO, pD = 4 + 2 * ci, 5 + 2 * ci
                        nt = 4 if j % 2 == 0 else 5
                        t0 = j // 2
                        for m in range(nt):
                            last = (jr == 7 and m == nt - 1)
                            P.op("pe", lambda e, m=m, last=last: e.matmul(
                                ps[pO][:, jr * 64:(jr + 1) * 64], lhsT=Vh[s][:, t0 + m, :], rhs=pT[u][:, m * 64:(m + 1) * 64],
                                start=False, stop=last, skip_group_check=True),
                                reads=[("Vh", s), ("pT", u)], writes=[("ps", pO)], track=(m == nt - 1))
                            P.op("pe", lambda e, m=m, last=last: e.matmul(
                                ps[pD][:, jr * 64:(jr + 1) * 64], lhsT=ones[:], rhs=pT[u][:, m * 64:(m + 1) * 64],
                                start=False, stop=last, skip_group_check=True),
                                reads=["ones", ("pT", u)], writes=[("ps", pD)], track=(m == nt - 1))
                        if jr == 7:
                            P.op("dve", lambda e: e.reciprocal(out=rden[ci][:], in_=ps[pD][:, :]),
                                 reads=[("ps", pD)], writes=[("rden", ci)])
                            P.op("dve", lambda e: e.tensor_tensor(out=yst[s][:, grp * 512:(grp + 1) * 512], in0=ps[pO][:, :],
                                                                  in1=rden[ci][:], op=ALU.mult),
                                 reads=[("ps", pO), ("rden", ci)], writes=[("ysta", s, grp)])

                    pend = []
                    for j in range(NQ):
                        pend.append((j, stageA(j)))
                        if len(pend) > SK:
                            stageB(*pend.pop(0))
                        if hook is not None and j % 8 == 7:
                            hook(hook_state)
                    while pend:
                        stageB(*pend.pop(0))
                    if with_ctx_out:
                        ci = st["cq"] % 2; st["cq"] += 1
                        pO, pD = 4 + 2 * ci, 5 + 2 * ci
                        for ct in range(2):
                            sbk = rotS.next()
                            P.op("pe", lambda e, s=s, ct=ct, sbk=sbk: e.matmul(ps[sbk][:, 0:256], lhsT=kT[s][:, NE + ct * 128:NE + (ct + 1) * 128],
                                                                             rhs=qT[s][:, NE:NA], start=True, stop=True),
                                 reads=[("kT", s), ("qT", s)], writes=[("ps", sbk)])
                            P.op("act", lambda e, ct=ct, ci=ci, sbk=sbk: e.activation(out=pc[ci][:, ct, 0:256], in_=ps[sbk][:, 0:256], func=AF.Exp, scale=SCALE),
                                 reads=[("ps", sbk)], writes=[("pc", ci, ct)])
                        for ct in range(2):
                            P.op("pe", lambda e, s=s, ct=ct, ci=ci, pO=pO: e.matmul(ps[pO][:, 0:256], lhsT=Vh[s][:, NE // 128 + ct, :], rhs=pc[ci][:, ct, 0:256],
                                                                               start=(ct == 0), stop=(ct == 1)),
                                 reads=[("Vh", s), ("pc", ci, ct)], writes=[("ps", pO)])
                            P.op("pe", lambda e, ct=ct, ci=ci, pD=pD: e.matmul(ps[pD][:, 0:256], lhsT=ones[:], rhs=pc[ci][:, ct, 0:256],
                                                                          start=(ct == 0), stop=(ct == 1)),
                                 reads=["ones", ("pc", ci, ct)], writes=[("ps", pD)])
                        P.op("dve", lambda e, ci=ci, pD=pD: e.reciprocal(out=rden[ci][:, 0:256], in_=ps[pD][:, 0:256]),
                             reads=[("ps", pD)], writes=[("rden", ci)])
                        P.op("dve", lambda e, ci=ci, pO=pO, s=s: e.tensor_tensor(out=yst[s][:, NOWN:NB], in0=ps[pO][:, 0:256],
                                                                           in1=rden[ci][:, 0:256], op=ALU.mult),
                             reads=[("ps", pO), ("rden", ci)], writes=[("ysta", s, "c")])
                    P.op("sp", lambda e, s=s, h=h: e.dma_start(out=Yd[16 + h], in_=yst[s][:]),
                         reads=[("ysta", s, g) for g in range(NOWN // 512)] + [("ysta", s, "c")], writes=[("Yd", 16 + h)], dsem=f"d_ysta{s}")
                if hook is not None:
                    L["attn_hook_flush"](hook_state)

        P.barrier()
        with ExitStack() as es:
            rstd2 = sb("rstd2", [128, NB], F32, es)
            xsrc = L["xsrc"]; xcsrc = L["xcsrc"]
            NG = len(BGRP)
            with ExitStack() as es2:
                yT = sb("yT", [128, KC, NB], BF16, es2)
                wb = [sb(f"wo{i}", [128, KC, WB], BF16, es2) for i in range(2)]
                xo = [sb(f"xo{i}", [128, NB], F32, es2) for i in range(3)]
                xm = [sb(f"xm{i}", [128, NB], F32, es2) for i in range(2)]
                sq2 = [sb(f"sq2{i}", [128, NB], BF16, es2) for i in range(2)]
                for k in range(KC):
                    P.op("sp", lambda e, k=k: e.dma_start(out=yT[:, k, :], in_=Yd[k]), reads=[("Yd", k)], writes=[("yT", k)], dsem=f"d_yT{k % 4}")
                yT_keys = [("yT", k) for k in range(KC)]
                wsrc = w_out.rearrange("(k p) n -> p k n", p=128)
                rot = Rot(range(4))
                def load_xo(oc):
                    xi = oc % 3
                    P.op("sp", lambda e: e.dma_start(out=xo[xi][:, 0:NOWN], in_=xsrc[oc, :, OWN0:OWN0 + NOWN]),
                         writes=[("xo", xi, 0)], dsem=f"d_xo{xi}")
                    if with_ctx_out:
                        P.op("sp", lambda e: e.dma_start(out=xo[xi][:, NOWN:NB], in_=xcsrc[oc]),
                             writes=[("xo", xi, 1)], dsem=f"d_xoc{xi}")
                load_xo(0); load_xo(1)
                for cb in range(D // WB):
                    s = cb % 2
                    P.op("pool", lambda e, cb=cb, s=s: e.dma_start(out=wb[s][:], in_=wsrc[:, :, cb * WB:(cb + 1) * WB]),
                         writes=[("wo", s)], dsem=f"d_wo{s}")
                    for c4 in range(WC):
                        oc = cb * WC + c4
                        xi = oc % 3
                        if oc + 2 < KC:
                            load_xo(oc + 2)
                        xmi = oc % 2
                        for gi, (g0, gn) in enumerate(BGRP):
                            pb = rot.next()
                            for k in range(KC):
                                P.op("pe", lambda e, k=k, s=s, c4=c4, pb=pb, g0=g0, gn=gn: e.matmul(
                                    ps[pb][:, 0:gn], lhsT=wb[s][:, k, c4 * 128:(c4 + 1) * 128], rhs=yT[:, k, g0:g0 + gn],
                                    start=(k == 0), stop=(k == KC - 1)),
                                    reads=[("wo", s)] + (yT_keys if k in (0, KC - 1) else []), writes=[("ps", pb)], track=(k == KC - 1))
                            which = 0 if g0 < NOWN else 1
                            P.op("dve", lambda e, pb=pb, g0=g0, gn=gn, xi=xi, xmi=xmi, oc=oc, which=which: e.scalar_tensor_tensor(
                                out=xm[xmi][:, g0:g0 + gn], in0=ps[pb][:, 0:gn], scalar=modv(2, oc, which), in1=xo[xi][:, g0:g0 + gn],
                                op0=ALU.mult, op1=ALU.add),
                                reads=[("ps", pb), ("xo", xi, 0), ("xo", xi, 1), "mod"], writes=[("xm", xmi, gi)])
                        P.op("sp", lambda e, xmi=xmi, oc=oc: e.dma_start(out=Xw[oc], in_=xm[xmi][:]),
                             reads=[("xm", xmi, g) for g in range(NG)], writes=[("Xw", oc)], dsem=f"d_xm{xmi}")
                        P.op("act", lambda e, xmi=xmi: e.activation(out=sq2[xmi][:], in_=xm[xmi][:], func=AF.Square),
                             reads=[("xm", xmi, g) for g in range(NG)], writes=[("sq2", xmi)])
                        for gi, (g0, gn) in enumerate(BGRP):
                            P.op("pe", lambda e, xmi=xmi, gi=gi, g0=g0, gn=gn, oc=oc: e.matmul(ps[4 + gi][:, 0:gn], lhsT=ones[:], rhs=sq2[xmi][:, g0:g0 + gn],
                                                                                      start=(oc == 0), stop=(oc == KC - 1), skip_group_check=True),
                                 reads=[("sq2", xmi), "ones"], writes=[("ps", 4 + gi)], track=(gi == NG - 1))
                for gi, (g0, gn) in enumerate(BGRP):
                    P.op("act", lambda e, gi=gi, g0=g0, gn=gn: e.activation(out=rstd2[:, g0:g0 + gn], in_=ps[4 + gi][:, 0:gn], func=AF.Sqrt,
                                                                         bias=epsb[:], scale=1.0 / D),
                         reads=[("ps", 4 + gi), "epsb"], writes=[("rstd2", gi)])
                    P.op("dve", lambda e, g0=g0, gn=gn: e.reciprocal(out=rstd2[:, g0:g0 + gn], in_=rstd2[:, g0:g0 + gn]),
                         reads=[("rstd2", gi)], writes=[("rstd2", gi)])
            P.barrier()
            with ExitStack() as es2:
                xo = [sb(f"xo_{i}", [128, NB], F32, es2) for i in range(2)]
                xm = [sb(f"xm_{i}", [128, NB], F32, es2) for i in range(2)]
                hst = [sb(f"hst{i}", [128, NB], BF16, es2) for i in range(2)]
                for fc in range(KC):
                    xi = fc % 2
                    P.op("sp", lambda e, xi=xi, fc=fc: e.dma_start(out=xo[xi][:], in_=Xw[fc]), reads=[("Xw", fc)],
                         writes=[("xo", xi, 0), ("xo", xi, 1)], dsem=f"d_xo{xi}")
                    P.op("dve", lambda e, xi=xi: e.tensor_tensor(out=xm[xi][:], in0=xo[xi][:], in1=rstd2[:], op=ALU.mult),
                         reads=[("xo", xi, 0), ("xo", xi, 1)] + [("rstd2", g) for g in range(NG)], writes=[("xm", xi, g) for g in range(NG)])
                    P.op("act", lambda e, xi=xi, fc=fc: e.activation(out=hst[xi][:, 0:NOWN], in_=xm[xi][:, 0:NOWN], func=AF.Identity,
                                                                  bias=modv(3, fc, 0), scale=A2[:, fc, 0:1]),
                         reads=[("xm", xi, g) for g in range(NG)] + ["mod", "A2"], writes=[("hst", xi, 0)])
                    if with_ctx_out:
                        P.op("act", lambda e, xi=xi, fc=fc: e.activation(out=hst[xi][:, NOWN:NB], in_=xm[xi][:, NOWN:NB], func=AF.Identity,
                                                                      bias=modv(3, fc, 1), scale=A2[:, fc, 1:2]),
                             reads=[("xm", xi, g) for g in range(NG)] + ["mod", "A2"], writes=[("hst", xi, 1)])
                    P.op("sp", lambda e, xi=xi, fc=fc: e.dma_start(out=H2d[fc], in_=hst[xi][:]), reads=[("hst", xi, 0), ("hst", xi, 1)],
                         writes=[("H2d", fc)], dsem=f"d_hst{xi}")
            P.barrier()
            NSL = 8; HCS = DFF // NSL // 128
            parts = L["parts"]
            TPM = max(n for _, n in parts)
            with ExitStack() as es2:
                h2T = sb("h2T", [128, KC, TPM], BF16, es2)
                hid = sb("hid", [128, HCS, TPM], BF16, es2)
                w1b = [sb(f"w1b{i}", [128, KC, 256], BF16, es2) for i in range(2)]
                w2b = [sb(f"w2b{i}", [128, HCS, 512], BF16, es2) for i in range(2)]
                rl = [sb(f"rl{i}", [128, 512], F32, es2) for i in range(2)]
                xa = [sb(f"xa{i}", [128, TPM], F32, es2) for i in range(3)]
                w1src = w1.rearrange("(k p) n -> p k n", p=128)
                w2src = w2.rearrange("(k p) n -> p k n", p=128)
                c1 = 0; c2 = 0; rc = 0; xc_ = 0
                rot1 = Rot(range(4)); rot2 = Rot(range(4, 8))
                for pi_, (t0, tn) in enumerate(parts):
                    TG = groups(tn)
                    for k in range(KC):
                        P.op("sp", lambda e, k=k: e.dma_start(out=h2T[:, k, 0:tn], in_=H2d[k, :, t0:t0 + tn]), reads=[("H2d", k)],
                             writes=[("h2T", k)], dsem=f"d_h2T{k % 4}")
                    h2_keys = [("h2T", k) for k in range(KC)]
                    for q8 in range(NSL):
                        for b4 in range(HCS // 2):
                            s = c1 % 2; c1 += 1
                            col0 = q8 * HCS * 128 + b4 * 256
                            P.op("pool", lambda e, s=s, col0=col0: e.dma_start(out=w1b[s][:], in_=w1src[:, :, col0:col0 + 256]),
                                 writes=[("w1b", s)], dsem=f"d_w1b{s}")
                            for c4 in range(2):
                                hc = b4 * 2 + c4
                                for gi, (g0, gn) in enumerate(TG):
                                    pb = rot1.next()
                                    for k in range(KC):
                                        P.op("pe", lambda e, k=k, s=s, c4=c4, pb=pb, g0=g0, gn=gn: e.matmul(
                                            ps[pb][:, 0:gn], lhsT=w1b[s][:, k, c4 * 128:(c4 + 1) * 128], rhs=h2T[:, k, g0:g0 + gn],
                                            start=(k == 0), stop=(k == KC - 1)),
                                            reads=[("w1b", s)] + (h2_keys if k in (0, KC - 1) else []), writes=[("ps", pb)], track=(k == KC - 1))
                                    ri = rc % 2; rc += 1
                                    P.op("act", lambda e, pb=pb, gn=gn, ri=ri: e.activation(out=rl[ri][:, 0:gn], in_=ps[pb][:, 0:gn], func=AF.Relu),
                                         reads=[("ps", pb)], writes=[("rl", ri)])
                                    P.op("dve", lambda e, g0=g0, gn=gn, ri=ri, hc=hc: e.tensor_tensor(out=hid[:, hc, g0:g0 + gn], in0=rl[ri][:, 0:gn],
                                                                                               in1=rl[ri][:, 0:gn], op=ALU.mult),
                                         reads=[("rl", ri)], writes=[("hid", hc, gi)])
                        hid_keys = [("hid", hc, gi) for hc in range(HCS) for gi in range(len(TG))]
                        def load_xa(oc, t0=t0, tn=tn):
                            xi = oc % 3
                            P.op("sp", lambda e: e.dma_start(out=xa[xi][:, 0:tn], in_=Xw[oc, :, t0:t0 + tn]), reads=[("Xw", oc)],
                                 writes=[("xa", xi)], dsem=f"d_xa_{xi}")
                        load_xa(0); load_xa(1)
                        for b8 in range(8):
                            s = c2 % 2; c2 += 1
                            P.op("pool", lambda e, s=s, q8=q8, b8=b8: e.dma_start(out=w2b[s][:], in_=w2src[:, q8 * HCS:(q8 + 1) * HCS, b8 * 512:(b8 + 1) * 512]),
                                 writes=[("w2b", s)], dsem=f"d_w2b{s}")
                            for c4 in range(4):
                                oc = b8 * 4 + c4
                                xi = oc % 3
                                if oc + 2 < KC:
                                    load_xa(oc + 2)
                                for gi, (g0, gn) in enumerate(TG):
                                    pb = rot2.next()
                                    for k in range(HCS):
                                        P.op("pe", lambda e, k=k, s=s, c4=c4, pb=pb, g0=g0, gn=gn: e.matmul(
                                            ps[pb][:, 0:gn], lhsT=w2b[s][:, k, c4 * 128:(c4 + 1) * 128], rhs=hid[:, k, g0:g0 + gn],
                                            start=(k == 0), stop=(k == HCS - 1)),
                                            reads=[("w2b", s)] + (hid_keys if k in (0, HCS - 1) else []), writes=[("ps", pb)], track=(k == HCS - 1))
                                    which = 0 if (t0 + g0) < NOWN else 1
                                    P.op("dve", lambda e, pb=pb, g0=g0, gn=gn, xi=xi, oc=oc, which=which: e.scalar_tensor_tensor(
                                        out=xa[xi][:, g0:g0 + gn], in0=ps[pb][:, 0:gn], scalar=modv(5, oc, which), in1=xa[xi][:, g0:g0 + gn],
                                        op0=ALU.mult, op1=ALU.add),
                                        reads=[("ps", pb), ("xa", xi), "mod"], writes=[("xa", xi)])
                                t = P.op("sp", lambda e, xi=xi, oc=oc: e.dma_start(out=Xw[oc, :, t0:t0 + tn], in_=xa[xi][:, 0:tn]),
                                         reads=[("xa", xi)], writes=[("Xw", oc)], dsem=f"d_xs_{xi}")
                                if q8 == NSL - 1 and L.get("final"):
                                    outs.append(t)
        P.barrier()


def build_fused():
    nc = bass.Bass("TRN2", target_bir_lowering=False)
    P = Prog(nc)
    outs = []
    xT = nc.dram_tensor("xT", [D, NE0], F32, kind="ExternalInput").ap()
    xcT = nc.dram_tensor("xcT", [D, NCX], F32, kind="ExternalInput").ap()
    cv = nc.dram_tensor("cv", [128, KC, 2], F32, kind="ExternalInput").ap()
    auxc = nc.dram_tensor("auxc", [4, NCX], F32, kind="ExternalInput").ap()
    ident_d = nc.dram_tensor("ident", [128, 128], F32, kind="ExternalInput").ap()
    msel_d = nc.dram_tensor("msel", [128, 4], F32, kind="ExternalInput").ap()
    NB0 = NOWN0 + NCX
    Xw0 = nc.dram_tensor("Xw0", [KC, 128, NB0], F32).ap()
    XwF = nc.dram_tensor("Xw", [KC, 128, NOWN], F32, kind="ExternalOutput").ap()
    with ExitStack() as top:
        ps = [top.enter_context(nc.psum_tensor(f"ps{i}", [128, 512], F32)) for i in range(8)]
        ident = top.enter_context(nc.sbuf_tensor("sb_ident", [128, 128], F32))
        ones = top.enter_context(nc.sbuf_tensor("sb_ones", [128, 128], BF16))
        epsb = top.enter_context(nc.sbuf_tensor("sb_epsb", [128, 1], F32))
        P.op("sp", lambda e: e.dma_start(out=ident[:], in_=ident_d), writes=["ident"], dsem="d_misc2")
        P.op("dve", lambda e: e.memset(ones[:], 1.0), writes=["ones"])
        P.op("dve", lambda e: e.memset(epsb[:], EPS), writes=["epsb"])
        cmn = {"cv": cv, "auxc": auxc, "ident": ident, "ones": ones, "epsb": epsb}
        L0 = {"tag": "_0", "NE": NE0, "NOWN": NOWN0, "ctx_out": True, "WB": 256,
              "xsrc": xT.rearrange("(k p) n -> k p n", p=128), "xcsrc": xcT.rearrange("(k p) n -> k p n", p=128),
              "Xw": Xw0, "parts": [(0, 1024), (1024, 768)]}
        emit_layer(nc, P, ps, cmn, L0, outs)
        L1 = {"tag": "_1", "NE": NE, "NOWN": NOWN, "ctx_out": False, "WB": 512,
              "xsrc": Xw0[:, :, 0:NOWN0], "xcsrc": Xw0[:, :, NOWN0:NB0], "sel": {"msel": msel_d},
              "Xw": XwF, "parts": [(0, 1024)], "final": True}
        emit_layer(nc, P, ps, cmn, L1, outs)
        P.final_wait("sp", outs)
        with nc.Block() as block:
            P.emit(block)
    return nc


def _chunked(v):
    return np.ascontiguousarray(v.reshape(-1, 128).T)


def _rowmap(r, nrows, rel0):
    R0 = 16 * r
    rows = []
    for b in range(nrows):
        rel = rel0 + b
        g = R0 + rel
        if 0 <= g <= 63 and rel <= 22:
            rows.append((g, True, True))
        elif r == 0 and -4 <= rel <= -1:
            rows.append((rel + 8, False, True))
        elif r == 3 and 16 <= rel <= 18:
            rows.append((56 + rel - 16, False, True))
        else:
            rows.append((0, False, False))
    return rows


def _bias_tables(rpb_l, rows, nq):
    jq = np.arange(64)[None, :]; jk = np.arange(64)[:, None]
    cs = np.clip(jq - 8, 0, 48)
    cvalid = (jk >= cs) & (jk < cs + 16)
    dc = np.clip(jk - jq + 15, 0, 30)
    out = np.full((16, 128, nq * 320), NEG, np.float32)
    for j in range(nq):
        qi, qnat, _ = rows[j + 4]
        nt = 4 if j % 2 == 0 else 5
        t0 = j // 2
        want = set(range(int(np.clip(qi - 4, 0, 56)), int(np.clip(qi - 4, 0, 56)) + 8)) if qnat else None
        got = set()
        for m in range(nt):
            for half in range(2):
                b = 2 * (t0 + m) + half
                if b < j or b > j + 7 or b >= len(rows):
                    continue
                kr, knat, kused = rows[b]
                if qnat:
                    if (not kused) or kr not in want or kr in got:
                        continue
                    got.add(kr)
                    dr = kr - qi + 7
                else:
                    dr = b - j + 3
                blk = np.where(cvalid[None], rpb_l[:, dr][:, dc], NEG)
                out[:, half * 64:(half + 1) * 64, j * 320 + m * 64:j * 320 + (m + 1) * 64] = blk
        if qnat:
            assert got == want, (j, qi, got, want)
    return out


def _aux(rows, r, rel0):
    R0 = 16 * r
    L = 4096
    n = len(rows)
    aux = np.ones((5, n * 64), np.float32)
    for b in range(n):
        nat = rows[b][1]
        aux[0, b * 64:(b + 1) * 64] = 1.0 if nat else 0.0
        if nat:
            g = (R0 + rel0 + b) * 64 + np.arange(64)
            for wi, w in enumerate((2, 4, 8, 16)):
                lo = np.clip(g - w // 2, 0, L - 1); hi = np.clip(g + (w - 1 - w // 2), 0, L - 1)
                aux[1 + wi, b * 64:(b + 1) * 64] = 1.0 / (hi - lo + 1)
    return aux


def _auxc():
    L = NCX
    g = np.arange(L)
    a = np.zeros((4, L), np.float32)
    for wi, w in enumerate((2, 4, 8, 16)):
        lo = np.clip(g - w // 2, 0, L - 1); hi = np.clip(g + (w - 1 - w // 2), 0, L - 1)
        a[wi] = 1.0 / (hi - lo + 1)
    return a


def _ext_xT(xb, rows):
    out = np.zeros((len(rows) * 64, D), np.float32)
    for b, (g, nat, used) in enumerate(rows):
        if used:
            out[b * 64:(b + 1) * 64] = xb[g * 64:(g + 1) * 64]
    return np.ascontiguousarray(out.T)


def _layer_shared(l, inp, tag):
    vec = np.zeros((128, NVEC), np.float32)
    vec[:, V_N1:V_N1 + 32] = _chunked(inp["norm1_g"][l]); vec[:, V_N2:V_N2 + 32] = _chunked(inp["norm2_g"][l])
    vec[:, V_PS:V_PS + 8] = _chunked(inp["pool_scale"][l]); vec[:, V_CB:V_CB + 8] = _chunked(inp["conv_dw_b"][l])
    vec[:, V_CG:V_CG + 8] = _chunked(inp["conv_norm_g"][l])
    vec[:, V_QG] = inp["q_norm_g"][l]; vec[:, V_KG] = inp["k_norm_g"][l]
    cw = inp["conv_dw_w"][l]
    vec[:, V_CW:V_CW + 248] = cw.T.reshape(8, 128, 31).transpose(1, 0, 2).reshape(128, 248)
    return {
        "vecs" + tag: vec, "bada" + tag: np.ascontiguousarray(np.broadcast_to(inp["b_ada"][l][None], (2, 6 * D))),
        "w_ada" + tag: inp["w_ada"][l], "w_in" + tag: inp["w_in"][l], "w_out" + tag: inp["w_out"][l],
        "w1" + tag: inp["w_mlp1"][l], "w2" + tag: inp["w_mlp2"][l],
        "pool_w" + tag: np.ascontiguousarray(inp["pool_w"][l].reshape(1024, 256)), "pw_w" + tag: inp["conv_pw_w"][l],
    }


def _fused_inputs(inp):
    x = inp["x"]; xc = inp["ctx"]
    shared = {"ident": np.eye(128, dtype=np.float32), "auxc": _auxc()}
    shared.update(_layer_shared(0, inp, "_0"))
    shared.update(_layer_shared(1, inp, "_1"))
    per_r = []
    for r in range(4):
        rows0 = _rowmap(r, 32, -8)
        rows1 = _rowmap(r, 24, -4)
        ms = np.zeros((128, 4), np.float32)
        ms[:, 0] = 0.0 if r == 0 else 1.0; ms[:, 1] = 1.0 if r == 0 else 0.0
        ms[:, 2] = 0.0 if r == 3 else 1.0; ms[:, 3] = 1.0 if r == 3 else 0.0
        per_r.append({
            "rows0": rows0,
            "aux_0": _aux(rows0, r, -8), "aux_1": _aux(rows1, r, -4),
            "bias_0": _bias_tables(inp["rpb"][0], rows0, NOWN0 // 64), "bias_1": _bias_tables(inp["rpb"][1], rows1, NOWN // 64),
            "msel": ms,
        })
    maps = []
    for core in range(8):
        b, r = core // 4, core % 4
        cvv = np.stack([inp["c"][b], inp["c_ctx"]], axis=-1)
        m = dict(shared)
        pr = per_r[r]
        m["xT"] = _ext_xT(x[b], pr["rows0"])
        m["xcT"] = np.ascontiguousarray(xc[b].T)
        m["cv"] = np.ascontiguousarray(cvv.reshape(KC, 128, 2).transpose(1, 0, 2))
        for k in ("aux_0", "aux_1", "bias_0", "bias_1", "msel"):
            m[k] = pr[k]
        maps.append(m)
    return maps


_NC_CACHE = {}


def kernel(**inp):
    inp = {k: np.asarray(v) for k, v in inp.items()}
    inp["x"] = inp["x"].astype(np.float32, copy=False)
    inp["ctx"] = inp["ctx"].astype(np.float32, copy=False)
    if "nc" not in _NC_CACHE:
        _NC_CACHE["nc"] = build_fused()
    nc = _NC_CACHE["nc"]
    maps = _fused_inputs(inp)
    res = run_bass_kernel_spmd(nc, maps, core_ids=list(range(8)))
    out = np.empty_like(inp["x"])
    for core in range(8):
        b, r = core // 4, core % 4
        out[b, r * 1024:(r + 1) * 1024] = res.results[core]["Xw"].reshape(D, NOWN).T
    return out
```
